# Optimizing a Trainium2 kernel written in Bass

```python
import math, functools
import jax, jax.numpy as jnp
from jax import lax
import numpy as np

D_MODEL = 2048
BATCH = 4
SEQ = 2048
DEPTH = 2

GRID_W = 64
CTX_LEN = 256
N_MOD = 9
D_FF = 5632
CONV_W = 4
EPS = 1e-6
D_MIX = D_MODEL

SSD_WIDTH = D_MIX // 4
SSD_HEADDIM = 64
SSD_HEADS = SSD_WIDTH // SSD_HEADDIM
SSD_GROUPS = 2
SSD_STATE = 64
SSD_CHUNK = 128
SSD_XBC = SSD_WIDTH + 2 * SSD_GROUPS * SSD_STATE
LRU_WIDTH = D_MIX // 4
LRU_BLOCKS = 8
LRU_BLOCK_DIM = LRU_WIDTH // LRU_BLOCKS
LRU_C = 8.0
HGRN_WIDTH = D_MIX // 4
HGRN_HEADDIM = 128
HGRN_HEADS = HGRN_WIDTH // HGRN_HEADDIM
HGRN_CHUNK = 16
RET_WIDTH = D_MIX // 4
RET_V_DIM = 128
RET_HEADS = RET_WIDTH // RET_V_DIM
RET_QK_DIM = RET_V_DIM // 2
RET_CHUNK = 128
ROPE_BASE = 10000.0

SSD_COLS = SSD_WIDTH + SSD_XBC + 2 * SSD_HEADS
LRU_COLS = 2 * LRU_WIDTH
HGRN_COLS = 5 * HGRN_WIDTH
RET_COLS = 2 * RET_HEADS * RET_QK_DIM + 2 * RET_WIDTH
IN_COLS = SSD_COLS + LRU_COLS + HGRN_COLS + RET_COLS

F32 = jnp.float32

kernel_name = 'hybrid_parallel_ssd_rglru_hgrn2_retention_dit'


def rmsnorm(x, g=None):
    xf = x.astype(F32)
    y = xf * lax.rsqrt(jnp.mean(xf * xf, axis=-1, keepdims=True) + EPS)
    if g is not None:
        y = y * g.astype(F32)
    return y.astype(x.dtype)


def dwconv(x, w, b):
    y = lax.conv_general_dilated(
        x, w.astype(x.dtype)[:, None, :], window_strides=(1,),
        padding=[((CONV_W - 1) // 2, CONV_W // 2)],
        dimension_numbers=('NWC', 'WIO', 'NWC'), feature_group_count=x.shape[-1])
    return y + b.astype(x.dtype)


def rope_1d(x, pos):
    half = x.shape[-1] // 2
    inv = ROPE_BASE ** (-jnp.arange(half, dtype=F32) / half)
    ang = pos.astype(F32)[:, None] * inv[None, :]
    cos, sin = jnp.cos(ang)[:, None, :], jnp.sin(ang)[:, None, :]
    x1, x2 = x[..., :half], x[..., half:]
    return jnp.concatenate([x1 * cos - x2 * sin, x1 * sin + x2 * cos], axis=-1)


def rope_2d(x, rows, cols):
    h = x.shape[-1] // 2
    return jnp.concatenate([rope_1d(x[..., :h], rows), rope_1d(x[..., h:], cols)], axis=-1)


def linear_scan(a, b, h0):
    def comb(l, r):
        return (l[0] * r[0], r[0] * l[1] + r[1])
    a_cum, b_cum = lax.associative_scan(comb, (a, b), axis=1)
    h = a_cum * h0[:, None, :] + b_cum
    return h, h[:, -1]


def scalar_decay_chunks(q, k, v, log_a, s0, chunk):
    Bn, T, H, N = q.shape
    P = v.shape[-1]
    nc = T // chunk
    qc = q.reshape(Bn, nc, chunk, H, N)
    kc = k.reshape(Bn, nc, chunk, H, N)
    vc = v.reshape(Bn, nc, chunk, H, P)
    cum = jnp.cumsum(log_a.astype(F32).reshape(Bn, nc, chunk, H), axis=2)
    tri = jnp.tril(jnp.ones((chunk, chunk), bool))[:, :, None]
    diff = cum[:, :, :, None, :] - cum[:, :, None, :, :]
    decay = jnp.where(tri, jnp.exp(jnp.where(tri, diff, 0.0)), 0.0)
    scores = jnp.einsum('bcthn,bcshn->bctsh', qc, kc) * decay
    y = jnp.einsum('bctsh,bcshp->bcthp', scores, vc)
    last = cum[:, :, -1]
    u = jnp.einsum('bcshn,bcshp->bchnp', kc * jnp.exp(last[:, :, None] - cum)[..., None], vc)

    def step(s, inp):
        g, uc = inp
        return g[..., None, None] * s + uc, s

    s_fin, s_in = lax.scan(step, s0, (jnp.moveaxis(jnp.exp(last), 1, 0), jnp.moveaxis(u, 1, 0)))
    y = y + jnp.einsum('bcthn,bchnp->bcthp', qc * jnp.exp(cum)[..., None], jnp.moveaxis(s_in, 0, 1))
    return y.reshape(Bn, T, H, P), s_fin


def vector_decay_chunks(q, k, v, log_f, s0, chunk):
    Bn, T, H, K = q.shape
    V = v.shape[-1]
    nc = T // chunk
    qc = q.reshape(Bn, nc, chunk, H, K)
    kc = k.reshape(Bn, nc, chunk, H, K)
    vc = v.reshape(Bn, nc, chunk, H, V)
    cum = jnp.cumsum(log_f.astype(F32).reshape(Bn, nc, chunk, H, K), axis=2)
    tri = jnp.tril(jnp.ones((chunk, chunk), bool))[:, :, None, None]
    diff = cum[:, :, :, None] - cum[:, :, None, :]
    decay = jnp.where(tri, jnp.exp(jnp.where(tri, diff, 0.0)), 0.0)
    scores = jnp.einsum('bcthk,bcshk,bctshk->bctsh', qc, kc, decay)
    y = jnp.einsum('bctsh,bcshv->bcthv', scores, vc)
    last = cum[:, :, -1]
    u = jnp.einsum('bcshk,bcshv->bchkv', kc * jnp.exp(last[:, :, None] - cum), vc)

    def step(s, inp):
        g, uc = inp
        return g[..., None] * s + uc, s

    s_fin, s_in = lax.scan(step, s0, (jnp.moveaxis(jnp.exp(last), 1, 0), jnp.moveaxis(u, 1, 0)))
    y = y + jnp.einsum('bcthk,bchkv->bcthv', qc * jnp.exp(cum), jnp.moveaxis(s_in, 0, 1))
    return y.reshape(Bn, T, H, V), s_fin


def bidir_scan(scan, ctx_args, lat_args, s0, reverse):
    if reverse:
        ctx_args = tuple(jnp.flip(t, axis=1) for t in ctx_args)
        lat_args = tuple(jnp.flip(t, axis=1) for t in lat_args)
    y_c, s_c = scan(*ctx_args, s0)
    y_l, _ = scan(*lat_args, s_c)
    if reverse:
        y_c, y_l = jnp.flip(y_c, axis=1), jnp.flip(y_l, axis=1)
    return y_c, y_l


def ssd_mixer(u_c, u_l, conv_w, conv_b, dt_bias, a_log, d_skip, norm_w):
    def prep(u):
        Bn, T = u.shape[:2]
        z, xbc, dt = jnp.split(u, [SSD_WIDTH, SSD_WIDTH + SSD_XBC], axis=-1)
        xbc = jax.nn.silu(dwconv(xbc, conv_w, conv_b))
        xs, bm, cm = jnp.split(xbc, [SSD_WIDTH, SSD_WIDTH + SSD_GROUPS * SSD_STATE], axis=-1)
        rep = SSD_HEADS // SSD_GROUPS
        bm = jnp.repeat(bm.reshape(Bn, T, SSD_GROUPS, SSD_STATE), rep, axis=2)
        cm = jnp.repeat(cm.reshape(Bn, T, SSD_GROUPS, SSD_STATE), rep, axis=2)
        return z, xs.reshape(Bn, T, SSD_HEADS, SSD_HEADDIM), bm, cm, dt

    def dir_args(stream, d):
        _, xs, bm, cm, dt = stream
        delta = jax.nn.softplus(dt[..., d * SSD_HEADS:(d + 1) * SSD_HEADS] + dt_bias[d])
        return (cm, bm * delta[..., None], xs, -jnp.exp(a_log[d]) * delta)

    sc, sl = prep(u_c), prep(u_l)
    scan = functools.partial(scalar_decay_chunks, chunk=SSD_CHUNK)
    s0 = jnp.zeros((u_c.shape[0], SSD_HEADS, SSD_STATE, SSD_HEADDIM), F32)
    y_c = sc[1] * d_skip[:, None]
    y_l = sl[1] * d_skip[:, None]
    for d in range(2):
        yc_d, yl_d = bidir_scan(scan, dir_args(sc, d), dir_args(sl, d), s0, d == 1)
        y_c = y_c + yc_d
        y_l = y_l + yl_d

    def out(y, z):
        return rmsnorm(y.reshape(z.shape) * jax.nn.silu(z), norm_w)

    return out(y_c, sc[0]), out(y_l, sl[0])


def rglru_mixer(u_c, u_l, conv_w, conv_b, wa, ba, wx, bx, lam):
    def prep(u):
        xb, gb = jnp.split(u, 2, axis=-1)
        return dwconv(xb, conv_w, conv_b), gb

    def dir_args(xb, d):
        Bn, T = xb.shape[:2]
        xr = xb.reshape(Bn, T, LRU_BLOCKS, LRU_BLOCK_DIM)
        r = jax.nn.sigmoid(jnp.einsum('bthi,hij->bthj', xr, wa[d]) + ba[d])
        i = jax.nn.sigmoid(jnp.einsum('bthi,hij->bthj', xr, wx[d]) + bx[d])
        log_a = -LRU_C * r * jax.nn.softplus(-lam[d]).reshape(LRU_BLOCKS, LRU_BLOCK_DIM)
        b = jnp.sqrt(jnp.maximum(-jnp.expm1(2.0 * log_a), 1e-12)) * (i * xr)
        return (jnp.exp(log_a).reshape(Bn, T, LRU_WIDTH), b.reshape(Bn, T, LRU_WIDTH))

    (xc, gc), (xl, gl) = prep(u_c), prep(u_l)
    h0 = jnp.zeros((u_c.shape[0], LRU_WIDTH), F32)
    h_c = jnp.zeros_like(xc)
    h_l = jnp.zeros_like(xl)
    for d in range(2):
        hc_d, hl_d = bidir_scan(linear_scan, dir_args(xc, d), dir_args(xl, d), h0, d == 1)
        h_c = h_c + hc_d
        h_l = h_l + hl_d
    return h_c * jax.nn.gelu(gc), h_l * jax.nn.gelu(gl)


def hgrn2_mixer(u_c, u_l, lb, norm_w):
    def prep(u):
        Bn, T = u.shape[:2]
        hs = lambda t: t.reshape(Bn, T, HGRN_HEADS, HGRN_HEADDIM)
        q, ff, fb, i, g = jnp.split(u, 5, axis=-1)
        return hs(jax.nn.silu(q) * HGRN_HEADDIM ** -0.5), (hs(ff), hs(fb)), hs(i), g

    def dir_args(stream, d):
        q, fr, i, _ = stream
        lbd = lb[d].reshape(HGRN_HEADS, HGRN_HEADDIM)
        f = lbd + (1.0 - lbd) * jax.nn.sigmoid(fr[d])
        return (q, 1.0 - f, i, jnp.log(f))

    sc, sl = prep(u_c), prep(u_l)
    scan = functools.partial(vector_decay_chunks, chunk=HGRN_CHUNK)
    s0 = jnp.zeros((u_c.shape[0], HGRN_HEADS, HGRN_HEADDIM, HGRN_HEADDIM), F32)
    o_c = jnp.zeros_like(sc[2])
    o_l = jnp.zeros_like(sl[2])
    for d in range(2):
        oc_d, ol_d = bidir_scan(scan, dir_args(sc, d), dir_args(sl, d), s0, d == 1)
        o_c = o_c + oc_d
        o_l = o_l + ol_d

    def out(o, g):
        return rmsnorm(o, norm_w).reshape(g.shape) * jax.nn.silu(g)

    return out(o_c, sc[3]), out(o_l, sl[3])


def retention_mixer(u_c, u_l, rows, cols, log_decay):
    qk = RET_HEADS * RET_QK_DIM

    def prep(u, rotate):
        Bn, T = u.shape[:2]
        q, k, v, g = jnp.split(u, [qk, 2 * qk, 2 * qk + RET_WIDTH], axis=-1)
        q = q.reshape(Bn, T, RET_HEADS, RET_QK_DIM)
        k = k.reshape(Bn, T, RET_HEADS, RET_QK_DIM)
        if rotate:
            q, k = rope_2d(q, rows, cols), rope_2d(k, rows, cols)
        la = jnp.broadcast_to(log_decay, (Bn, T, RET_HEADS))
        return (q, k * RET_QK_DIM ** -0.5, v.reshape(Bn, T, RET_HEADS, RET_V_DIM), la), g

    (ac, gc), (al, gl) = prep(u_c, False), prep(u_l, True)
    scan = functools.partial(scalar_decay_chunks, chunk=RET_CHUNK)
    s0 = jnp.zeros((u_c.shape[0], RET_HEADS, RET_QK_DIM, RET_V_DIM), F32)
    o_c = jnp.zeros_like(ac[2])
    o_l = jnp.zeros_like(al[2])
    for d in range(2):
        oc_d, ol_d = bidir_scan(scan, ac, al, s0, d == 1)
        o_c = o_c + oc_d
        o_l = o_l + ol_d

    def out(o, g):
        return jax.nn.silu(g) * rmsnorm(o).reshape(g.shape)

    return out(o_c, gc), out(o_l, gl)


def swiglu_half_step(x, m, j, g_pre, g_post, w1, w3, w2):
    h = rmsnorm(x, g_pre) * (1.0 + m[:, j + 1]) + m[:, j]
    y = (jax.nn.silu(h @ w1) * (h @ w3)) @ w2
    return x + 0.5 * m[:, j + 2] * rmsnorm(y, g_post)


def setup_inputs(seed: int = 0) -> dict:
    key = jax.random.key(seed)
    ks = iter(jax.random.split(key, 40))

    def nrm(shape, s):
        return jax.random.normal(next(ks), shape, F32) * s

    dt = jnp.exp(jax.random.uniform(next(ks), (DEPTH, 2, SSD_HEADS), F32,
                                    minval=math.log(1e-3), maxval=math.log(1e-1)))
    a_c = jax.random.uniform(next(ks), (DEPTH, 2, LRU_WIDTH), F32, minval=0.9, maxval=0.999)
    a_lru = a_c ** (1.0 / LRU_C)
    return {
        'x': nrm((BATCH, SEQ, D_MODEL), 1.0),
        'c': nrm((BATCH, D_MODEL), 1.0),
        'ctx': nrm((BATCH, CTX_LEN, D_MODEL), 1.0),
        'c_ctx': nrm((D_MODEL,), 1.0),
        'w_mod': nrm((DEPTH, D_MODEL, N_MOD * D_MODEL), 0.5 * D_MODEL ** -0.5),
        'b_mod': nrm((DEPTH, N_MOD * D_MODEL), 0.02),
        'norm_pre': 1.0 + nrm((DEPTH, 3, D_MODEL), 0.02),
        'norm_post': 1.0 + nrm((DEPTH, 3, D_MODEL), 0.02),
        'ffn_w1': nrm((DEPTH, 2, D_MODEL, D_FF), D_MODEL ** -0.5),
        'ffn_w3': nrm((DEPTH, 2, D_MODEL, D_FF), D_MODEL ** -0.5),
        'ffn_w2': nrm((DEPTH, 2, D_FF, D_MODEL), D_FF ** -0.5),
        'w_in': nrm((DEPTH, D_MODEL, IN_COLS), D_MODEL ** -0.5),
        'w_out': nrm((DEPTH, D_MIX, D_MODEL), D_MIX ** -0.5),
        'ssd_conv_w': nrm((DEPTH, CONV_W, SSD_XBC), CONV_W ** -0.5),
        'ssd_conv_b': nrm((DEPTH, SSD_XBC), 0.02),
        'ssd_dt_bias': dt + jnp.log(-jnp.expm1(-dt)),
        'ssd_a_log': jnp.log(jax.random.uniform(next(ks), (DEPTH, 2, SSD_HEADS), F32, minval=1.0, maxval=16.0)),
        'ssd_d': 1.0 + nrm((DEPTH, SSD_HEADS), 0.1),
        'ssd_norm_w': 1.0 + nrm((DEPTH, SSD_WIDTH), 0.02),
        'lru_conv_w': nrm((DEPTH, CONV_W, LRU_WIDTH), CONV_W ** -0.5),
        'lru_conv_b': nrm((DEPTH, LRU_WIDTH), 0.02),
        'lru_wa': nrm((DEPTH, 2, LRU_BLOCKS, LRU_BLOCK_DIM, LRU_BLOCK_DIM), LRU_BLOCK_DIM ** -0.5),
        'lru_ba': nrm((DEPTH, 2, LRU_BLOCKS, LRU_BLOCK_DIM), 0.02),
        'lru_wx': nrm((DEPTH, 2, LRU_BLOCKS, LRU_BLOCK_DIM, LRU_BLOCK_DIM), LRU_BLOCK_DIM ** -0.5),
        'lru_bx': nrm((DEPTH, 2, LRU_BLOCKS, LRU_BLOCK_DIM), 0.02),
        'lru_lambda': jnp.log(a_lru) - jnp.log1p(-a_lru),
        'hgrn_lb_logits': nrm((2, DEPTH, HGRN_WIDTH), 1.0),
        'hgrn_norm_w': 1.0 + nrm((DEPTH, HGRN_HEADDIM), 0.02),
    }


def reference(x, c, ctx, c_ctx, w_mod, b_mod, norm_pre, norm_post, ffn_w1, ffn_w3, ffn_w2,
              w_in, w_out, ssd_conv_w, ssd_conv_b, ssd_dt_bias, ssd_a_log, ssd_d, ssd_norm_w,
              lru_conv_w, lru_conv_b, lru_wa, lru_ba, lru_wx, lru_bx, lru_lambda,
              hgrn_lb_logits, hgrn_norm_w):
    Bn, L, D = x.shape
    ROWS = L // GRID_W
    rr, cc = jnp.meshgrid(jnp.arange(ROWS), jnp.arange(GRID_W), indexing='ij')
    rows, cols = rr.reshape(-1), cc.reshape(-1)
    sm = jax.nn.softmax(hgrn_lb_logits.astype(F32), axis=1)
    lb_all = jnp.cumsum(sm, axis=1) - sm[:, :1]
    ret_log_decay = jnp.log1p(-jnp.exp2(-5.0 - jnp.arange(RET_HEADS, dtype=F32)))
    splits = [SSD_COLS, SSD_COLS + LRU_COLS, SSD_COLS + LRU_COLS + HGRN_COLS]

    xl, xc = x, ctx
    for l in range(DEPTH):
        last = l == DEPTH - 1
        m_l = (jax.nn.silu(c) @ w_mod[l] + b_mod[l]).reshape(Bn, N_MOD, 1, D)
        m_c = (jax.nn.silu(c_ctx) @ w_mod[l] + b_mod[l]).reshape(1, N_MOD, 1, D)

        xl = swiglu_half_step(xl, m_l, 0, norm_pre[l, 0], norm_post[l, 0], ffn_w1[l, 0], ffn_w3[l, 0], ffn_w2[l, 0])
        xc = swiglu_half_step(xc, m_c, 0, norm_pre[l, 0], norm_post[l, 0], ffn_w1[l, 0], ffn_w3[l, 0], ffn_w2[l, 0])

        h_l = rmsnorm(xl, norm_pre[l, 1]) * (1.0 + m_l[:, 4]) + m_l[:, 3]
        h_c = rmsnorm(xc, norm_pre[l, 1]) * (1.0 + m_c[:, 4]) + m_c[:, 3]
        ua_l, ub_l, uc_l, ud_l = jnp.split((h_l @ w_in[l]).astype(F32), splits, axis=-1)
        ua_c, ub_c, uc_c, ud_c = jnp.split((h_c @ w_in[l]).astype(F32), splits, axis=-1)
        ya = ssd_mixer(ua_c, ua_l, ssd_conv_w[l], ssd_conv_b[l], ssd_dt_bias[l], ssd_a_log[l], ssd_d[l], ssd_norm_w[l])
        yb = rglru_mixer(ub_c, ub_l, lru_conv_w[l], lru_conv_b[l], lru_wa[l], lru_ba[l], lru_wx[l], lru_bx[l], lru_lambda[l])
        yc = hgrn2_mixer(uc_c, uc_l, lb_all[:, l], hgrn_norm_w[l])
        yd = retention_mixer(ud_c, ud_l, rows, cols, ret_log_decay)
        y_l = jnp.concatenate([ya[1], yb[1], yc[1], yd[1]], axis=-1).astype(xl.dtype) @ w_out[l]
        xl = xl + m_l[:, 5] * rmsnorm(y_l, norm_post[l, 1])
        if not last:
            y_c = jnp.concatenate([ya[0], yb[0], yc[0], yd[0]], axis=-1).astype(xc.dtype) @ w_out[l]
            xc = xc + m_c[:, 5] * rmsnorm(y_c, norm_post[l, 1])

        xl = swiglu_half_step(xl, m_l, 6, norm_pre[l, 2], norm_post[l, 2], ffn_w1[l, 1], ffn_w3[l, 1], ffn_w2[l, 1])
        if not last:
            xc = swiglu_half_step(xc, m_c, 6, norm_pre[l, 2], norm_post[l, 2], ffn_w1[l, 1], ffn_w3[l, 1], ffn_w2[l, 1])
    return xl
```

```python
import numpy as np
import ml_dtypes
import concourse.bass as bass
import concourse.mybir as mybir
from concourse.bass_utils import run_bass_kernel_spmd
from contextlib import ExitStack

F32 = mybir.dt.float32
BF16 = mybir.dt.bfloat16
AF = mybir.ActivationFunctionType
ALU = mybir.AluOpType

T = 2304
NCTX = 256
NLAT = 2048
D = 2048
KC = 16
DFF = 5632
MFF = 44
NB = 18
EPS = 1e-6
IN_COLS = 6416
DEPTH = 2

PV = {}
_off = 0
for _name, _n in [("b_mod", 144), ("npre", 48), ("npost", 48), ("ssd_cw", 32), ("ssd_cb", 8), ("ssd_d", 4),
                  ("ssd_nw", 4), ("ssd_dtb", 1), ("ssd_alog", 1), ("lru_cw", 16), ("lru_cb", 4), ("lru_ba", 8),
                  ("lru_bx", 8), ("lru_lam", 8), ("hg_lb", 16), ("hg_nw", 1)]:
    PV[_name] = (_off, _n)
    _off += _n
NPV = _off


class Sched:
    NDS = 12

    def __init__(self, nc, es):
        self.nc = nc
        self.eng = {"pe": nc.tensor, "dve": nc.vector, "act": nc.scalar, "pool": nc.gpsimd, "sp": nc.sync}
        self.sems = {}
        self.pcnt = {}
        for e in ["pe", "dve", "act", "pool"]:
            self.sems[("p", e)] = es.enter_context(nc.semaphore("prog_" + e))
            self.pcnt[e] = 0
        self.dslots = {}
        self.dqi = {}
        for q in ["sp", "pool", "act"]:
            self.dslots[q] = []
            for i in range(self.NDS):
                key = ("d", q, i)
                self.sems[key] = es.enter_context(nc.semaphore(f"dma_{q}_{i}"))
                self.dslots[q].append([key, 0])
            self.dqi[q] = 0
        self.seen = {e: {} for e in self.eng}
        self.res = {}
        self.n_ops = 0
        self.n_waits = 0

    def _deps(self, r, w):
        deps = {}

        def add(tok):
            if tok is None:
                return
            k, v = tok
            if deps.get(k, 0) < v:
                deps[k] = v
        for k in r:
            st = self.res.get(k)
            if st:
                add(st["w"])
        for k in w:
            st = self.res.get(k)
            if st:
                add(st["w"])
                for kk, vv in st["r"].items():
                    add((kk, vv))
        return deps

    def _wait(self, e, deps):
        eng = self.eng[e]
        for k, v in deps.items():
            if e == "pe" and k == ("p", "pe"):
                continue
            if self.seen[e].get(k, 0) >= v:
                continue
            eng.wait_ge(self.sems[k], v)
            self.seen[e][k] = v
            self.n_waits += 1

    def _commit(self, tok, r, w):
        k, v = tok
        for x in r:
            st = self.res.setdefault(x, {"w": None, "r": {}})
            if st["r"].get(k, 0) < v:
                st["r"][k] = v
        for x in w:
            self.res[x] = {"w": tok, "r": {}}

    def op(self, e, fn, r=(), w=()):
        return self.group(e, [fn], r, w)

    def group(self, e, fns, r=(), w=()):
        psr = [k for k in r if isinstance(k, tuple) and k[0] == "ps"]
        if psr:
            r = [k for k in r if k not in psr]
            w = list(w) + psr
        self._wait(e, self._deps(r, w))
        eng = self.eng[e]
        ins = None
        for fn in fns:
            ins = fn(eng)
        self.pcnt[e] += 1
        ins.then_inc(self.sems[("p", e)], 1)
        tok = (("p", e), self.pcnt[e])
        self._commit(tok, r, w)
        self.n_ops += len(fns)
        return tok

    def dma(self, q, out, in_, r=(), w=(), **kw):
        deps = self._deps(r, w)
        slot = self.dslots[q][self.dqi[q] % self.NDS]
        self.dqi[q] += 1
        if slot[1] > 0 and deps.get(slot[0], 0) < slot[1]:
            deps[slot[0]] = slot[1]
        self._wait(q, deps)
        ins = self.eng[q].dma_start(out=out, in_=in_, **kw)
        ins.then_inc(self.sems[slot[0]], 16)
        slot[1] += 16
        tok = (slot[0], slot[1])
        self._commit(tok, r, w)
        self.n_ops += 1
        return tok

    def barrier(self):
        allt = {}
        for e in ["pe", "dve", "act", "pool"]:
            if self.pcnt[e] > 0:
                allt[("p", e)] = self.pcnt[e]
        for q in self.dslots:
            for key, tot in self.dslots[q]:
                if tot > 0:
                    allt[key] = tot
        for e in self.eng:
            d = {k: v for k, v in allt.items() if k != ("p", e)}
            if e != "pe":
                if ("p", e) in allt:
                    d[("p", e)] = allt[("p", e)]
            self._wait(e, d)
        self.res = {}


def tiles_of(start, stop, step=768):
    out = []
    t = start
    while t < stop:
        n = min(step, stop - t)
        out.append((t, n))
        t += n
    return out


def halves(n, maxn=512):
    k = (n + maxn - 1) // maxn
    base = (n // 128) // k
    rem = (n // 128) % k
    out = []
    o = 0
    for i in range(k):
        sz = (base + (1 if i < rem else 0)) * 128
        out.append((o, sz))
        o += sz
    return out


def seg_ranges(t0, n):
    out = []
    if t0 < NCTX:
        c = min(NCTX, t0 + n) - t0
        out.append((0, c, 1))
        if c < n:
            out.append((c, n - c, 0))
    else:
        out.append((0, n, 0))
    return out


class Builder:
    def __init__(self, debug=None):
        self.debug = debug or {}
        self.nc = bass.Bass("TRN2", target_bir_lowering=False)
        self.es = ExitStack()
        nc = self.nc
        di = lambda name, shape, dt=F32: nc.dram_tensor(name, list(shape), dt, kind="ExternalInput").ap()
        self.xin = di("xin", [T, D])
        self.cT = di("cT", [128, KC, 2])
        self.w_mod = di("w_mod", [DEPTH, D, 9 * D])
        self.w1 = di("ffn_w1", [DEPTH, 2, D, DFF])
        self.w3 = di("ffn_w3", [DEPTH, 2, D, DFF])
        self.w2 = di("ffn_w2", [DEPTH, 2, DFF, D])
        self.w_in = di("w_in", [DEPTH, D, IN_COLS])
        self.w_out = di("w_out", [DEPTH, D, D])
        self.pv = di("pv", [DEPTH, 128, NPV])
        self.lru_w = di("lru_w", [DEPTH, 2, 2, 8, 64, 64])
        self.ctab = di("ctab", [128, CT_N])
        self.rope = di("rope", [2, 64, NLAT])
        self.ropeR = di("ropeR", [64, 64])
        self.rtab = di("rtab", [4, 128, 5, 128])
        self.out = nc.dram_tensor("out", [NLAT, D], F32, kind="ExternalOutput").ap()
        self.Xs = nc.dram_tensor("Xs", [KC, 128, T], F32, kind="Internal").ap()
        self.Ys = nc.dram_tensor("Ys", [KC, 128, T], BF16, kind="Internal").ap()
        self.Us = nc.dram_tensor("Us", [Builder.NG, 128, T], F32, kind="Internal").ap()
        self.dbg_out = None
        if self.debug.get("dump"):
            self.dbg_out = nc.dram_tensor("dbg", [KC, 128, T], F32, kind="ExternalOutput").ap()

    def build(self):
        nc, es = self.nc, self.es
        with es:
            self.S = S = Sched(nc, es)
            sb = lambda name, shape, dt=F32: es.enter_context(nc.sbuf_tensor(name, list(shape), dt))
            self.ps = [es.enter_context(nc.psum_tensor(f"ps{i}", [128, 512], F32)) for i in range(8)]
            self.ones_bf = sb("ones_bf", [128, 128], BF16)
            self.identf = sb("identf", [128, 128], F32)
            self.ident_bf = sb("ident_bf", [128, 128], BF16)
            self.pvt_l = [sb(f"pvt{l}", [128, NPV], F32) for l in range(DEPTH)]
            self.modT_l = [sb(f"modT{l}", [128, 144, 2], F32) for l in range(DEPTH)]
            self.Apre_l = [sb(f"Apre{l}", [128, 3, KC, 2], F32) for l in range(DEPTH)]
            self.Gpost_l = [sb(f"Gpost{l}", [128, 3, KC, 2], F32) for l in range(DEPTH)]
            self.sc_l = sb("mod_sc", [128, KC, 2], BF16)
            self.mod_q = []
            self.pump_wm = None
            self.pump_cnt = 0
            self.ctab_sb = sb("ctab_sb", [128, CT_N], F32)
            self.eps_col = sb("eps_col", [128, 1], F32)
            S.op("pool", lambda e: e.memset(self.eps_col[:], EPS), w=["eps_col"])
            self.one_col = sb("one_col", [128, 1], F32)
            S.op("pool", lambda e: e.memset(self.one_col[:], 1.0), w=["one_col"])
            S.op("pool", lambda e: e.memset(self.ones_bf[:], 1.0), w=["ones_bf"])
            S.op("pool", lambda e: e.memset(self.identf[:], 1.0), w=["identf"])
            S.op("pool", lambda e: e.affine_select(out=self.identf[:], in_=self.identf[:], pattern=[[-1, 128]],
                                                   compare_op=ALU.is_equal, fill=0.0, base=0, channel_multiplier=1),
                 r=["identf"], w=["identf"])
            S.op("dve", lambda e: e.tensor_copy(out=self.ident_bf[:], in_=self.identf[:]), r=["identf"], w=["ident_bf"])
            S.dma("sp", self.ctab_sb[:], self.ctab[:, :], w=["ctab_sb"])

            self.load_x()
            stop = self.debug.get("stop")
            self.mods_setup()
            for l in range(DEPTH):
                last = l == DEPTH - 1
                self.pvt, self.modT, self.Apre, self.Gpost = self.pvt_l[l], self.modT_l[l], self.Apre_l[l], self.Gpost_l[l]
                self.cur_l = l
                if l == 0:
                    self.mods_initial()
                if not self.debug.get("noffn"):
                    self.ffn(l, 0, 0, tiles_of(0, T))
                if stop == (l, "ffn1"):
                    break
                if not self.debug.get("nomix"):
                    self.mixer(l)
                if stop == (l, "mix"):
                    break
                t_lo = NCTX if last else 0
                self.outproj(l, tiles_of(t_lo, T))
                if stop == (l, "outproj"):
                    break
                self.ffn(l, 1, 2, tiles_of(t_lo, T))
                if stop == (l, "ffn2"):
                    break
            if self.dbg_out is not None:
                src = self.Ys if self.debug.get("dump") == "Ys" else self.Xs
                self.dump(src)
            self.store_out()
            S.barrier()
        return nc

    def un(self, name):
        self._uid = getattr(self, "_uid", 0) + 1
        return f"{name}_u{self._uid}"

    def load_x(self):
        nc, S, es = self.nc, self.S, self.es
        with ExitStack() as ls:
            xt = [ls.enter_context(nc.sbuf_tensor(self.un(f"lx_in{i}"), [128, D], F32)) for i in range(2)]
            xo = [ls.enter_context(nc.sbuf_tensor(self.un(f"lx_out{i}"), [128, KC, 128], F32)) for i in range(2)]
            for b in range(NB):
                bi = b % 2
                S.dma("sp", xt[bi][:], self.xin[b * 128:(b + 1) * 128, :], w=[("lx_in", bi)])
                for g in range(4):
                    pst = self.ps[(b * 4 + g) % 8]
                    S.group("pe", [
                        (lambda e, kc=kc, pst=pst, bi=bi: e.transpose(pst[:, (kc % 4) * 128:(kc % 4 + 1) * 128],
                                                                     xt[bi][:, kc * 128:(kc + 1) * 128], self.identf[:]))
                        for kc in range(g * 4, g * 4 + 4)], r=[("lx_in", bi), "identf"], w=[("ps", (b * 4 + g) % 8)])
                    eng = "dve" if g % 2 == 0 else "act"
                    if eng == "dve":
                        S.op("dve", lambda e, g=g, pst=pst, bi=bi: e.tensor_copy(
                            out=xo[bi][:, g * 4:(g + 1) * 4, :], in_=pst[:, :].rearrange("p (a b) -> p a b", a=4)),
                            r=[("ps", (b * 4 + g) % 8)], w=[("lx_out", bi, g)])
                    else:
                        S.op("act", lambda e, g=g, pst=pst, bi=bi: e.activation(
                            out=xo[bi][:, g * 4:(g + 1) * 4, :], in_=pst[:, :].rearrange("p (a b) -> p a b", a=4), func=AF.Copy),
                            r=[("ps", (b * 4 + g) % 8)], w=[("lx_out", bi, g)])
                S.dma("sp", self.Xs[:, :, b * 128:(b + 1) * 128].rearrange("k p t -> p k t"), xo[bi][:],
                      r=[("lx_out", bi, g) for g in range(4)], w=[("Xs", b)])
            S.barrier()

    def store_out(self):
        nc, S = self.nc, self.S
        with ExitStack() as ls:
            xi = [ls.enter_context(nc.sbuf_tensor(self.un(f"so_in{i}"), [128, KC, 128], F32)) for i in range(2)]
            xo = [ls.enter_context(nc.sbuf_tensor(self.un(f"so_out{i}"), [128, D], F32)) for i in range(2)]
            for b in range(2, NB):
                bi = b % 2
                S.dma("sp", xi[bi][:], self.Xs[:, :, b * 128:(b + 1) * 128].rearrange("k p t -> p k t"),
                      r=[("Xs", b), "Xs"], w=[("so_in", bi)])
                for g in range(4):
                    pi = (b * 4 + g) % 8
                    pst = self.ps[pi]
                    S.group("pe", [
                        (lambda e, kc=kc, pst=pst, bi=bi: e.transpose(pst[:, (kc % 4) * 128:(kc % 4 + 1) * 128],
                                                                     xi[bi][:, kc, :], self.identf[:]))
                        for kc in range(g * 4, g * 4 + 4)], r=[("so_in", bi), "identf"], w=[("ps", pi)])
                    if g % 2 == 0:
                        S.op("dve", lambda e, g=g, pst=pst, bi=bi: e.tensor_copy(out=xo[bi][:, g * 512:(g + 1) * 512], in_=pst[:, :]),
                             r=[("ps", pi)], w=[("so_out", bi, g)])
                    else:
                        S.op("act", lambda e, g=g, pst=pst, bi=bi: e.activation(out=xo[bi][:, g * 512:(g + 1) * 512], in_=pst[:, :], func=AF.Copy),
                             r=[("ps", pi)], w=[("so_out", bi, g)])
                S.dma("sp", self.out[(b - 2) * 128:(b - 1) * 128, :], xo[bi][:],
                      r=[("so_out", bi, g) for g in range(4)], w=[("out", b)])

    def dump(self, src):
        nc, S = self.nc, self.S
        S.barrier()
        with ExitStack() as ls:
            if src is self.Ys:
                tb = ls.enter_context(nc.sbuf_tensor(self.un("dump_b"), [128, T], BF16))
            tf = ls.enter_context(nc.sbuf_tensor(self.un("dump_f"), [128, T], F32))
            for kc in range(KC):
                if src is self.Ys:
                    S.dma("sp", tb[:], src[kc, :, :], w=["dump_b"])
                    S.op("dve", lambda e: e.tensor_copy(out=tf[:], in_=tb[:]), r=["dump_b"], w=["dump_f"])
                else:
                    S.dma("sp", tf[:], src[kc, :, :], w=["dump_f"])
                S.dma("sp", self.dbg_out[kc, :, :], tf[:], r=["dump_f"], w=[("dbg", kc)])
            S.barrier()

    def pvc(self, name, i=0, rows=128):
        o, n = PV[name]
        return self.pvt[0:rows, o + i:o + i + 1]

    def mods_setup(self):
        nc, S = self.nc, self.S
        for l in range(DEPTH):
            S.dma("sp", self.pvt_l[l][:], self.pv[l, :, :], w=["pvt"])
        with ExitStack() as ls:
            ct = ls.enter_context(nc.sbuf_tensor(self.un("lp_ct"), [128, KC, 2], F32))
            S.dma("sp", ct[:], self.cT[:, :, :], w=["lp_ct"])
            S.op("act", lambda e: e.activation(out=self.sc_l[:], in_=ct[:], func=AF.Silu), r=["lp_ct"], w=["mod_sc"])
            S.barrier()
        self.mod_q = [(l, s) for l in range(DEPTH) for s in range(72)]

    def mod_slab(self, l, s, wm, key):
        nc, S = self.nc, self.S
        wv = self.w_mod[l].rearrange("(kc p) n -> p kc n", p=128)
        S.dma("pool", wm[:], wv[:, :, s * 256:(s + 1) * 256], w=[key])
        pm = self.ps[7]
        S.group("pe", [
            (lambda e, kc=kc, m=m: e.matmul(pm[:, 2 * m:2 * m + 2], lhsT=wm[:, kc, m * 128:(m + 1) * 128], rhs=self.sc_l[:, kc, :],
                                            start=(kc == 0), stop=(kc == KC - 1)))
            for m in range(2) for kc in range(KC)], r=[key, "mod_sc"], w=[("ps", 7)])
        o, _ = PV["b_mod"]
        modT, pvt = self.modT_l[l], self.pvt_l[l]
        S.op("dve", lambda e: e.tensor_tensor(out=modT[:, 2 * s:2 * s + 2, :], in0=pm[:, 0:4].rearrange("p (a b) -> p a b", b=2),
                                              in1=pvt[:, o + 2 * s:o + 2 * s + 2].unsqueeze(2).to_broadcast([128, 2, 2]), op=ALU.add),
             r=[("ps", 7), "pvt"], w=["modT"])
        if s % 8 == 7:
            j = s // 8
            i = j // 3
            opre, _ = PV["npre"]
            opost, _ = PV["npost"]
            if j % 3 == 1:
                S.op("dve", lambda e: e.scalar_tensor_tensor(
                    out=self.Apre_l[l][:, i, :, :], in0=modT[:, j * 16:(j + 1) * 16, :], scalar=1.0,
                    in1=pvt[:, opre + 16 * i:opre + 16 * i + 16].unsqueeze(2).to_broadcast([128, 16, 2]),
                    op0=ALU.add, op1=ALU.mult), r=["modT", "pvt"], w=[("Apre", i)])
            elif j % 3 == 2:
                S.op("dve", lambda e: e.scalar_tensor_tensor(
                    out=self.Gpost_l[l][:, i, :, :], in0=modT[:, j * 16:(j + 1) * 16, :], scalar=(1.0 if i == 1 else 0.5),
                    in1=pvt[:, opost + 16 * i:opost + 16 * i + 16].unsqueeze(2).to_broadcast([128, 16, 2]),
                    op0=ALU.mult, op1=ALU.mult), r=["modT", "pvt"], w=[("Gpost", i)])

    def mods_initial(self):
        nc, S = self.nc, self.S
        with ExitStack() as ls:
            wm = [ls.enter_context(nc.sbuf_tensor(self.un(f"lp_wm{i}"), [128, KC, 256], BF16)) for i in range(4)]
            for k in range(40):
                l, s = self.mod_q.pop(0)
                self.mod_slab(l, s, wm[k % 4], ("lp_wm", k % 4))
            S.barrier()

    def pump_mods(self, n):
        for _ in range(n):
            if not self.mod_q or self.pump_wm is None:
                return
            l, s = self.mod_q.pop(0)
            k = self.pump_cnt % len(self.pump_wm)
            self.pump_cnt += 1
            self.mod_slab(l, s, self.pump_wm[k], ("pump_wm", k))

    def prenorm_h(self, i, t0, n, hT, ls_tiles):
        self.prenorm_stats(t0, n, ls_tiles)
        self.prenorm_apply(i, t0, n, hT, ls_tiles)

    def prenorm_stats(self, t0, n, ls_tiles):
        nc, S = self.nc, self.S
        xg, sq, rstd, tmp = ls_tiles
        hv = halves(n)
        pss = [self.ps[0], self.ps[1]]
        for g in range(8):
            bi = g % 2
            S.dma("sp", xg[bi][:, :, 0:n], self.Xs[2 * g:2 * g + 2, :, t0:t0 + n].rearrange("k p t -> p k t"),
                  r=["Xs"], w=[("xg", bi)])
            S.op("act", lambda e, bi=bi: e.activation(out=sq[bi][:, :, 0:n], in_=xg[bi][:, :, 0:n], func=AF.Square),
                 r=[("xg", bi)], w=[("sq", bi, 0), ("sq", bi, 1)])
            for j in range(2):
                kc = 2 * g + j
                S.group("pe", [
                    (lambda e, hi=hi, o=o, sz=sz, j=j, bi=bi, kc=kc: e.matmul(pss[hi][:, 0:sz], lhsT=self.ones_bf[:], rhs=sq[bi][:, j, o:o + sz],
                                                                           start=(kc == 0), stop=(kc == KC - 1)))
                    for hi, (o, sz) in enumerate(hv)], r=[("sq", bi, j), "ones_bf"], w=[("ps", 0), ("ps", 1)])
        for hi, (o, sz) in enumerate(hv):
            S.op("act", lambda e, hi=hi, o=o, sz=sz: e.activation(out=rstd[:, o:o + sz], in_=pss[hi][:, 0:sz], func=AF.Sqrt,
                                                                 scale=1.0 / D, bias=self.eps_col[:, 0:1]),
                 r=[("ps", hi), "eps_col"], w=[("rstd", id(rstd), hi)])
            S.op("dve", lambda e, o=o, sz=sz: e.reciprocal(rstd[:, o:o + sz], rstd[:, o:o + sz]), r=[("rstd", id(rstd), hi)], w=[("rstd", id(rstd), hi)])

    def prenorm_apply(self, i, t0, n, hT, ls_tiles, hkey="hT"):
        nc, S = self.nc, self.S
        xg, sq, rstd, tmp = ls_tiles
        for g in range(8):
            bi = g % 2
            S.dma("sp", xg[bi][:, :, 0:n], self.Xs[2 * g:2 * g + 2, :, t0:t0 + n].rearrange("k p t -> p k t"),
                  r=["Xs"], w=[("xg", bi)])
            for j in range(2):
                kc = 2 * g + j
                S.op("dve", lambda e, bi=bi, j=j: e.tensor_tensor(out=tmp[j][:, 0:n], in0=xg[bi][:, j, 0:n], in1=rstd[:, 0:n], op=ALU.mult),
                     r=[("xg", bi), ("rstd", id(rstd), 0), ("rstd", id(rstd), 1)], w=[("tmp", j)])
                for (o, sz, mi) in seg_ranges(t0, n):
                    S.op("act", lambda e, o=o, sz=sz, mi=mi, kc=kc, j=j: e.activation(
                        out=hT[:, kc, o:o + sz], in_=tmp[j][:, o:o + sz], func=AF.Identity,
                        scale=self.Apre[:, i, kc, mi:mi + 1], bias=self.modT[:, (3 * i) * 16 + kc, mi:mi + 1]),
                        r=[("tmp", j), ("Apre", i), "modT"], w=[(hkey, kc)])

    def postnorm_update(self, i, t0, n, yT, ls_tiles, ss_ps_ready, ykey="hT"):
        nc, S = self.nc, self.S
        xg, sq, rstd, tmp = ls_tiles
        hv = halves(n)
        pss = [self.ps[0], self.ps[1]]
        rk = [("rstd", id(rstd), 0), ("rstd", id(rstd), 1)]
        for hi, (o, sz) in enumerate(hv):
            S.op("act", lambda e, hi=hi, o=o, sz=sz: e.activation(out=rstd[:, o:o + sz], in_=pss[hi][:, 0:sz], func=AF.Sqrt,
                                                                 scale=1.0 / D, bias=self.eps_col[:, 0:1]),
                 r=[("ps", hi), "eps_col"], w=[rk[hi]])
            S.op("dve", lambda e, o=o, sz=sz: e.reciprocal(rstd[:, o:o + sz], rstd[:, o:o + sz]), r=[rk[hi]], w=[rk[hi]])
        for g in range(8):
            bi = g % 2
            S.dma("sp", xg[bi][:, :, 0:n], self.Xs[2 * g:2 * g + 2, :, t0:t0 + n].rearrange("k p t -> p k t"),
                  r=["Xs"], w=[("xg", bi)])
            for j in range(2):
                kc = 2 * g + j
                for (o, sz, mi) in seg_ranges(t0, n):
                    S.op("dve", lambda e, o=o, sz=sz, mi=mi, kc=kc, j=j: e.scalar_tensor_tensor(
                        out=tmp[j][:, o:o + sz], in0=yT[:, kc, o:o + sz], scalar=self.Gpost[:, i, kc, mi:mi + 1],
                        in1=rstd[:, o:o + sz], op0=ALU.mult, op1=ALU.mult),
                        r=[(ykey, kc), ("Gpost", i)] + rk, w=[("tmp", j)])
                S.op("dve", lambda e, bi=bi, j=j: e.tensor_tensor(out=xg[bi][:, j, 0:n], in0=xg[bi][:, j, 0:n], in1=tmp[j][:, 0:n], op=ALU.add),
                     r=[("tmp", j), ("xg", bi)], w=[("xg", bi)])
            S.dma("sp", self.Xs[2 * g:2 * g + 2, :, t0:t0 + n].rearrange("k p t -> p k t"), xg[bi][:, :, 0:n],
                  r=[("xg", bi)], w=["Xs"])

    def postnorm_apply(self, i, t0, n, yT, ls_tiles, ykey="hT", groups=None):
        nc, S = self.nc, self.S
        xg, sq, rstd, tmp = ls_tiles
        rk = [("rstd", id(rstd), 0), ("rstd", id(rstd), 1)]
        for g in (range(8) if groups is None else groups):
            bi = g % 2
            S.dma("sp", xg[bi][:, :, 0:n], self.Xs[2 * g:2 * g + 2, :, t0:t0 + n].rearrange("k p t -> p k t"),
                  r=["Xs"], w=[("xg", bi)])
            for j in range(2):
                kc = 2 * g + j
                for (o, sz, mi) in seg_ranges(t0, n):
                    S.op("dve", lambda e, o=o, sz=sz, mi=mi, kc=kc, j=j: e.scalar_tensor_tensor(
                        out=tmp[j][:, o:o + sz], in0=yT[:, kc, o:o + sz], scalar=self.Gpost[:, i, kc, mi:mi + 1],
                        in1=rstd[:, o:o + sz], op0=ALU.mult, op1=ALU.mult),
                        r=[(ykey, kc), ("Gpost", i)] + rk, w=[("tmp", j)])
                S.op("dve", lambda e, bi=bi, j=j: e.tensor_tensor(out=xg[bi][:, j, 0:n], in0=xg[bi][:, j, 0:n], in1=tmp[j][:, 0:n], op=ALU.add),
                     r=[("tmp", j), ("xg", bi)], w=[("xg", bi)])
            S.dma("sp", self.Xs[2 * g:2 * g + 2, :, t0:t0 + n].rearrange("k p t -> p k t"), xg[bi][:, :, 0:n],
                  r=[("xg", bi)], w=["Xs"])

    def ffn(self, l, fi, i, tiles):
        nc, S = self.nc, self.S
        TT = 768
        with ExitStack() as ls:
            sbl = lambda name, shape, dt=F32: ls.enter_context(nc.sbuf_tensor(self.un(name), list(shape), dt))
            hTs = [sbl(f"f_hT{k}", [128, KC, TT], BF16) for k in range(2)]
            gT = sbl("f_gT", [128, MFF, TT], BF16)
            w1s = [sbl(f"f_w1s{k}", [128, KC, 256], BF16) for k in range(2)]
            w3s = [sbl(f"f_w3s{k}", [128, KC, 256], BF16) for k in range(2)]
            w2s = [sbl(f"f_w2s{k}", [128, 22, 256], BF16) for k in range(2)]
            xg = [sbl(f"f_xg{k}", [128, 2, TT], F32) for k in range(2)]
            sq = [sbl(f"f_sq{k}", [128, 2, TT], BF16) for k in range(2)]
            rstd_a = sbl("f_rstda", [128, TT], F32)
            rstd_b = sbl("f_rstdb", [128, TT], F32)
            tmp = [sbl(f"f_tmp{k}", [128, TT], F32) for k in range(2)]
            sa = [sbl("f_sa0", [128, TT], BF16)]
            lt_pre = (xg, sq, rstd_a, tmp)
            lt_post = (xg, sq, rstd_b, tmp)
            w1v = self.w1[l, fi].rearrange("(kc p) n -> p kc n", p=128)
            w3v = self.w3[l, fi].rearrange("(kc p) n -> p kc n", p=128)
            w2v = self.w2[l, fi].rearrange("(kc p) n -> p kc n", p=128)
            hkeys = ["hTa", "hTb"]
            t0, n = tiles[0]
            self.prenorm_stats(t0, n, lt_pre)
            self.prenorm_apply(i, t0, n, hTs[0], lt_pre, hkey=hkeys[0])
            pending_post = None
            for ti_, (t0, n) in enumerate(tiles):
                hT = hTs[ti_ % 2]
                hk = hkeys[ti_ % 2]
                hv = halves(n)
                nh = len(hv)
                for s_ in range(22):
                    bi = s_ % 2
                    S.dma("pool", w1s[bi][:], w1v[:, :, s_ * 256:(s_ + 1) * 256], w=[("w1s", bi)])
                    S.dma("pool", w3s[bi][:], w3v[:, :, s_ * 256:(s_ + 1) * 256], w=[("w3s", bi)])
                    for m in range(2):
                        mc = 2 * s_ + m
                        pa = [(mc % 2) * 4 + 0, (mc % 2) * 4 + 1]
                        pb = [(mc % 2) * 4 + 2, (mc % 2) * 4 + 3]
                        S.group("pe", [
                            (lambda e, kc=kc, hi=hi, o=o, sz=sz, m=m, bi=bi, pa=pa: e.matmul(
                                self.ps[pa[hi]][:, 0:sz], lhsT=w1s[bi][:, kc, m * 128:(m + 1) * 128], rhs=hT[:, kc, o:o + sz],
                                start=(kc == 0), stop=(kc == KC - 1)))
                            for kc in range(KC) for hi, (o, sz) in enumerate(hv)],
                            r=[("w1s", bi)] + [(hk, kc) for kc in range(KC)], w=[("ps", p) for p in pa[:nh]])
                        S.group("pe", [
                            (lambda e, kc=kc, hi=hi, o=o, sz=sz, m=m, bi=bi, pb=pb: e.matmul(
                                self.ps[pb[hi]][:, 0:sz], lhsT=w3s[bi][:, kc, m * 128:(m + 1) * 128], rhs=hT[:, kc, o:o + sz],
                                start=(kc == 0), stop=(kc == KC - 1)))
                            for kc in range(KC) for hi, (o, sz) in enumerate(hv)],
                            r=[("w3s", bi)] + [(hk, kc) for kc in range(KC)], w=[("ps", p) for p in pb[:nh]])
                        sai = 0
                        for hi, (o, sz) in enumerate(hv):
                            S.op("act", lambda e, hi=hi, o=o, sz=sz, pa=pa, sai=sai: e.activation(
                                out=sa[sai][:, o:o + sz], in_=self.ps[pa[hi]][:, 0:sz], func=AF.Silu),
                                r=[("ps", pa[hi])], w=[("sa", sai, hi)])
                            S.op("dve", lambda e, hi=hi, o=o, sz=sz, pb=pb, sai=sai, mc=mc: e.tensor_tensor(
                                out=gT[:, mc, o:o + sz], in0=self.ps[pb[hi]][:, 0:sz], in1=sa[sai][:, o:o + sz], op=ALU.mult),
                                r=[("ps", pb[hi]), ("sa", sai, hi)], w=[("gT", mc)])
                    if pending_post is not None and 2 <= s_ < 10:
                        pending_post(s_ - 2)
                        if s_ == 9:
                            pending_post = None
                yT = hT
                has_next = ti_ + 1 < len(tiles)
                if has_next:
                    t0n, nn = tiles[ti_ + 1]
                    hvn = halves(nn)
                    hTn = hTs[(ti_ + 1) % 2]
                    hkn = hkeys[(ti_ + 1) % 2]
                PB = [(2, 3), (4, 5), (6, 7)]
                pss = [self.ps[0], self.ps[1]]
                rka = [("rstd", id(rstd_a), 0), ("rstd", id(rstd_a), 1)]

                def pre_stats_group(g):
                    bi = g % 2
                    S.dma("sp", xg[bi][:, :, 0:nn], self.Xs[2 * g:2 * g + 2, :, t0n:t0n + nn].rearrange("k p t -> p k t"),
                          r=["Xs"], w=[("xg", bi)])
                    S.op("act", lambda e, bi=bi: e.activation(out=sq[bi][:, :, 0:nn], in_=xg[bi][:, :, 0:nn], func=AF.Square),
                         r=[("xg", bi)], w=[("sq", bi, 0), ("sq", bi, 1)])
                    for j in range(2):
                        kc = 2 * g + j
                        S.group("pe", [
                            (lambda e, hi=hi, o=o, sz=sz, j=j, bi=bi, kc=kc: e.matmul(pss[hi][:, 0:sz], lhsT=self.ones_bf[:], rhs=sq[bi][:, j, o:o + sz],
                                                                                   start=(kc == 0), stop=(kc == KC - 1)))
                            for hi, (o, sz) in enumerate(hvn)], r=[("sq", bi, j), "ones_bf"], w=[("ps", 0), ("ps", 1)])

                def pre_rstd():
                    for hi, (o, sz) in enumerate(hvn):
                        S.op("act", lambda e, hi=hi, o=o, sz=sz: e.activation(out=rstd_a[:, o:o + sz], in_=pss[hi][:, 0:sz], func=AF.Sqrt,
                                                                             scale=1.0 / D, bias=self.eps_col[:, 0:1]),
                             r=[("ps", hi), "eps_col"], w=[rka[hi]])
                        S.op("dve", lambda e, o=o, sz=sz: e.reciprocal(rstd_a[:, o:o + sz], rstd_a[:, o:o + sz]), r=[rka[hi]], w=[rka[hi]])

                def apply_group(g):
                    bi = g % 2
                    S.dma("sp", xg[bi][:, :, 0:nn], self.Xs[2 * g:2 * g + 2, :, t0n:t0n + nn].rearrange("k p t -> p k t"),
                          r=["Xs"], w=[("xg", bi)])
                    for j in range(2):
                        kc = 2 * g + j
                        S.op("dve", lambda e, bi=bi, j=j: e.tensor_tensor(out=tmp[j][:, 0:nn], in0=xg[bi][:, j, 0:nn], in1=rstd_a[:, 0:nn], op=ALU.mult),
                             r=[("xg", bi)] + rka, w=[("tmp", j)])
                        for (o, sz, mi) in seg_ranges(t0n, nn):
                            S.op("act", lambda e, o=o, sz=sz, mi=mi, kc=kc, j=j: e.activation(
                                out=hTn[:, kc, o:o + sz], in_=tmp[j][:, o:o + sz], func=AF.Identity,
                                scale=self.Apre[:, i, kc, mi:mi + 1], bias=self.modT[:, (3 * i) * 16 + kc, mi:mi + 1]),
                                r=[("tmp", j), ("Apre", i), "modT"], w=[(hkn, kc)])

                def post_stats(mo):
                    j = mo % 2
                    S.op("act", lambda e, mo=mo, j=j: e.activation(out=sq[0][:, j, 0:n], in_=yT[:, mo, 0:n], func=AF.Square),
                         r=[(hk, mo)], w=[("sq", 0, j)])
                    S.group("pe", [
                        (lambda e, hi=hi, o=o, sz=sz, j=j, mo=mo: e.matmul(pss[hi][:, 0:sz], lhsT=self.ones_bf[:], rhs=sq[0][:, j, o:o + sz],
                                                                       start=(mo == 0), stop=(mo == KC - 1)))
                        for hi, (o, sz) in enumerate(hv)], r=[("sq", 0, j), "ones_bf"], w=[("ps", 0), ("ps", 1)])

                for pr in range(8):
                    pbank = [list(PB[(2 * pr) % 3]), list(PB[(2 * pr + 1) % 3])]
                    for hf in range(2):
                        S.dma("pool", w2s[hf][:], w2v[:, hf * 22:(hf + 1) * 22, pr * 256:(pr + 1) * 256], w=[("w2s", hf)])
                        for m in range(2):
                            S.group("pe", [
                                (lambda e, k=k, hi=hi, o=o, sz=sz, m=m, hf=hf, pbank=pbank: e.matmul(
                                    self.ps[pbank[m][hi]][:, 0:sz], lhsT=w2s[hf][:, k, m * 128:(m + 1) * 128], rhs=gT[:, hf * 22 + k, o:o + sz],
                                    start=(hf == 0 and k == 0), stop=(hf == 1 and k == 21)))
                                for k in range(22) for hi, (o, sz) in enumerate(hv)],
                                r=[("w2s", hf)] + [("gT", hf * 22 + k) for k in range(22)], w=[("ps", p) for p in pbank[m][:nh]])
                    for m in range(2):
                        mo = 2 * pr + m
                        for hi, (o, sz) in enumerate(hv):
                            S.op("act", lambda e, hi=hi, o=o, sz=sz, mo=mo, m=m, pbank=pbank: e.activation(
                                out=yT[:, mo, o:o + sz], in_=self.ps[pbank[m][hi]][:, 0:sz], func=AF.Copy),
                                r=[("ps", pbank[m][hi])], w=[(hk, mo)])
                    if has_next:
                        if pr < 4:
                            pre_stats_group(2 * pr)
                            pre_stats_group(2 * pr + 1)
                            if pr == 3:
                                pre_rstd()
                        else:
                            apply_group(2 * (pr - 4))
                            apply_group(2 * (pr - 4) + 1)
                    if pr >= 4:
                        for mo in range(4 * (pr - 4), 4 * (pr - 4) + 4):
                            post_stats(mo)
                hv_ = hv
                rk = [("rstd", id(rstd_b), 0), ("rstd", id(rstd_b), 1)]

                def post(t0=t0, n=n, yT=yT, hk=hk):
                    self.postnorm_update(i, t0, n, yT, lt_post, None, ykey=hk)
                if ti_ + 1 < len(tiles):
                    for hi, (o, sz) in enumerate(hv):
                        S.op("act", lambda e, hi=hi, o=o, sz=sz: e.activation(out=rstd_b[:, o:o + sz], in_=pss[hi][:, 0:sz], func=AF.Sqrt,
                                                                             scale=1.0 / D, bias=self.eps_col[:, 0:1]),
                             r=[("ps", hi), "eps_col"], w=[rk[hi]])
                        S.op("dve", lambda e, o=o, sz=sz: e.reciprocal(rstd_b[:, o:o + sz], rstd_b[:, o:o + sz]), r=[rk[hi]], w=[rk[hi]])

                    def post(g, t0=t0, n=n, yT=yT, hk=hk):
                        self.postnorm_apply(i, t0, n, yT, lt_post, ykey=hk, groups=[g])
                    pending_post = post
                else:
                    post()
            S.barrier()

    GROUPS = ([("ssd_z", 128 * c, 128) for c in range(4)] + [("ssd_x", 512 + 128 * c, 128) for c in range(4)]
              + [("ssd_B", 1024 + 64 * g, 64) for g in range(2)] + [("ssd_C", 1152 + 64 * g, 64) for g in range(2)] + [("ssd_dt", 1280, 16)]
              + [("lru_x", 1296 + 128 * c, 128) for c in range(4)] + [("lru_g", 1808 + 128 * c, 128) for c in range(4)]
              + [("hg_q", 2320 + 128 * c, 128) for c in range(4)] + [("hg_ff", 2832 + 128 * c, 128) for c in range(4)]
              + [("hg_fb", 3344 + 128 * c, 128) for c in range(4)] + [("hg_i", 3856 + 128 * c, 128) for c in range(4)]
              + [("hg_g", 4368 + 128 * c, 128) for c in range(4)]
              + [("ret_q", 4880 + 64 * h, 64) for h in range(4)] + [("ret_k", 5136 + 64 * h, 64) for h in range(4)]
              + [("ret_v", 5392 + 128 * c, 128) for c in range(4)] + [("ret_g", 5904 + 128 * c, 128) for c in range(4)])
    NG = len(GROUPS)
    GIDX = {}
    for _i, (_n, _c0, _nc) in enumerate(GROUPS):
        GIDX.setdefault(_n, []).append(_i)
    TG = [(0, 512), (512, 512), (1024, 512), (1536, 512), (2048, 256)]

    def mixer(self, l):
        nc, S = self.nc, self.S
        self.mixer_inproj(l)
        which = self.debug.get("mixers", ["ssd", "lru", "hg", "ret"])
        if "lru" in which:
            self.mix_lru(l)
        if "hg" in which:
            self.mix_hgrn(l)
        if "ret" in which:
            self.mix_ret(l)
        if "ssd" in which:
            self.mix_ssd(l)

    def mixer_inproj(self, l):
        nc, S = self.nc, self.S
        with ExitStack() as ls:
            sbl = lambda name, shape, dt=F32: ls.enter_context(nc.sbuf_tensor(self.un(name), list(shape), dt))
            hT = sbl("mi_hT", [128, KC, T], BF16)
            xg = [sbl(f"mi_xg{k}", [128, 2, 768], F32) for k in range(2)]
            sq = [sbl(f"mi_sq{k}", [128, 2, 768], BF16) for k in range(2)]
            rstd = sbl("mi_rstd", [128, 768], F32)
            tmp = [sbl(f"mi_tmp{k}", [128, 768], F32) for k in range(2)]
            ws = [sbl(f"mi_ws{k}", [128, KC, 512], BF16) for k in range(2)]
            stage = [sbl(f"mi_st{k}", [128, T], F32) for k in range(2)]
            self.pump_wm = [sbl(f"mi_pump{k}", [128, KC, 256], BF16) for k in range(4)]
            self.pump_cnt = 0
            n_need = len([x for x in self.mod_q if x[0] in (l, l + 1)])
            per_group = (n_need + len(self.GROUPS) - 1) // len(self.GROUPS)
            for (t0, n) in tiles_of(0, T):
                self.prenorm_h(1, t0, n, hT[:, :, t0:t0 + n], (xg, sq, rstd, tmp))
            wv = self.w_in[l].rearrange("(kc p) n -> p kc n", p=128)
            slabs = []
            cur = []
            for gi, (name, c0, ncol) in enumerate(self.GROUPS):
                if cur and (cur[0][1] != name or sum(x[3] for x in cur) + ncol > 512):
                    slabs.append(cur)
                    cur = []
                cur.append((gi, name, c0, ncol))
            slabs.append(cur)
            cnt = 0
            for si, slab in enumerate(slabs):
                bi = si % 2
                c0 = slab[0][2]
                tot = sum(x[3] for x in slab)
                S.dma("pool", ws[bi][:, :, 0:tot], wv[:, :, c0:c0 + tot], w=[("mi_ws", bi)])
                for (gi, name, gc0, ncol) in slab:
                    lo = gc0 - c0
                    sti = cnt % 2
                    for ti, (to, tn) in enumerate(self.TG):
                        pi = cnt * 5 + ti
                        pb = pi % 4
                        S.group("pe", [
                            (lambda e, kc=kc, pb=pb, lo=lo, ncol=ncol, to=to, tn=tn, bi=bi: e.matmul(
                                self.ps[pb][0:ncol, 0:tn], lhsT=ws[bi][:, kc, lo:lo + ncol], rhs=hT[:, kc, to:to + tn],
                                start=(kc == 0), stop=(kc == KC - 1)))
                            for kc in range(KC)], r=[("mi_ws", bi)] + [("hT", kc) for kc in range(KC)], w=[("ps", pb)])
                        if pi % 2 == 0:
                            S.op("act", lambda e, pb=pb, ncol=ncol, to=to, tn=tn, sti=sti: e.activation(
                                out=stage[sti][0:ncol, to:to + tn], in_=self.ps[pb][0:ncol, 0:tn], func=AF.Copy),
                                r=[("ps", pb)], w=[("mi_st", sti, ti)])
                        else:
                            S.op("dve", lambda e, pb=pb, ncol=ncol, to=to, tn=tn, sti=sti: e.tensor_copy(
                                out=stage[sti][0:ncol, to:to + tn], in_=self.ps[pb][0:ncol, 0:tn]),
                                r=[("ps", pb)], w=[("mi_st", sti, ti)])
                    S.dma("sp", self.Us[gi, 0:ncol, :], stage[sti][0:ncol, :], r=[("mi_st", sti, ti) for ti in range(5)], w=[("Us", gi)])
                    cnt += 1
                    self.pump_mods(per_group)
            self.pump_mods(len([x for x in self.mod_q if x[0] in (l, l + 1)]))
            S.barrier()
            self.pump_wm = None

    def conv_row(self, eng, out, x, wcols, bcol, key_out, key_x, extra_r=()):
        S = self.S
        r = [key_x] + list(extra_r)
        S.op(eng, lambda e: e.tensor_scalar(out=out, in0=x, scalar1=wcols[1], scalar2=bcol, op0=ALU.mult, op1=ALU.add), r=r, w=[key_out])
        for (s0, s1) in [(0, NCTX), (NCTX, T)]:
            S.op(eng, lambda e, s0=s0, s1=s1: e.scalar_tensor_tensor(out=out[:, s0 + 1:s1], in0=x[:, s0:s1 - 1], scalar=wcols[0],
                                                                     in1=out[:, s0 + 1:s1], op0=ALU.mult, op1=ALU.add), r=r + [key_out], w=[key_out])
            S.op(eng, lambda e, s0=s0, s1=s1: e.scalar_tensor_tensor(out=out[:, s0:s1 - 1], in0=x[:, s0 + 1:s1], scalar=wcols[2],
                                                                     in1=out[:, s0:s1 - 1], op0=ALU.mult, op1=ALU.add), r=r + [key_out], w=[key_out])
            S.op(eng, lambda e, s0=s0, s1=s1: e.scalar_tensor_tensor(out=out[:, s0:s1 - 2], in0=x[:, s0 + 2:s1], scalar=wcols[3],
                                                                     in1=out[:, s0:s1 - 2], op0=ALU.mult, op1=ALU.add), r=r + [key_out], w=[key_out])

    def mix_lru(self, l):
        nc, S = self.nc, self.S
        with ExitStack() as ls:
            sbl = lambda name, shape, dt=F32: ls.enter_context(nc.sbuf_tensor(self.un(name), list(shape), dt))
            ux = sbl("lr_ux", [128, T]); ug = sbl("lr_ug", [128, T]); xb = sbl("lr_xb", [128, T])
            xbb = sbl("lr_xbb", [128, T], BF16)
            rr_ = [sbl(f"lr_r{d}", [128, T]) for d in range(2)]; ii_ = [sbl(f"lr_i{d}", [128, T]) for d in range(2)]
            aa_ = [sbl(f"lr_a{d}", [128, T]) for d in range(2)]; bb_ = [sbl(f"lr_b{d}", [128, T]) for d in range(2)]
            hf = sbl("lr_hf", [128, T]); hb = sbl("lr_hb", [128, T]); t1 = sbl("lr_t1", [128, T])
            yb = sbl("lr_yb", [128, T], BF16)
            wbd = [[sbl(f"lr_w{g}{d}", [128, 128], BF16) for d in range(2)] for g in range(2)]
            c8 = sbl("lr_c8", [128, 8])
            olam, _ = PV["lru_lam"]
            S.op("act", lambda e: e.activation(out=c8[:], in_=self.pvt[:, olam:olam + 8], func=AF.Exp, scale=-1.0), r=["pvt"], w=["lr_c8"])
            S.op("act", lambda e: e.activation(out=c8[:], in_=c8[:], func=AF.Ln, bias=self.one_col[:, 0:1]), r=["lr_c8", "one_col"], w=["lr_c8"])
            S.op("dve", lambda e: e.tensor_scalar(out=c8[:], in0=c8[:], scalar1=-8.0, scalar2=None, op0=ALU.mult), r=["lr_c8"], w=["lr_c8"])
            ocw, _ = PV["lru_cw"]
            for c in range(4):
                gx = self.GIDX["lru_x"][c]; gg = self.GIDX["lru_g"][c]
                S.dma("sp", ux[:], self.Us[gx, :, :], r=[("Us", gx)], w=["lr_ux"])
                S.dma("sp", ug[:], self.Us[gg, :, :], r=[("Us", gg)], w=["lr_ug"])
                for g in range(2):
                    for d in range(2):
                        S.op("pool", lambda e, g=g, d=d: e.memset(wbd[g][d][:], 0.0), w=[("lr_w", g, d)])
                        for hh in range(2):
                            S.dma("pool", wbd[g][d][64 * hh:64 * hh + 64, 64 * hh:64 * hh + 64], self.lru_w[l, g, d, 2 * c + hh, :, :],
                                  r=[("lr_w", g, d)], w=[("lr_w", g, d, hh)])
                wc = [self.pvt[:, ocw + 4 * j + c:ocw + 4 * j + c + 1] for j in range(4)]
                self.conv_row("dve", xb[:], ux[:], wc, self.pvc("lru_cb", c), "lr_xb", "lr_ux", ["pvt"])
                S.op("act", lambda e: e.activation(out=xbb[:], in_=xb[:], func=AF.Copy), r=["lr_xb"], w=["lr_xbb"])
                S.op("act", lambda e: e.activation(out=t1[:], in_=ug[:], func=AF.Square), r=["lr_ug"], w=["lr_t1"])
                S.op("pool", lambda e: e.tensor_scalar(out=t1[:], in0=t1[:], scalar1=0.044715, scalar2=1.0, op0=ALU.mult, op1=ALU.add), r=["lr_t1"], w=["lr_t1"])
                S.op("pool", lambda e: e.tensor_tensor(out=t1[:], in0=t1[:], in1=ug[:], op=ALU.mult), r=["lr_t1", "lr_ug"], w=["lr_t1"])
                S.op("act", lambda e: e.activation(out=t1[:], in_=t1[:], func=AF.Sigmoid, scale=1.5957691216057308), r=["lr_t1"], w=["lr_t1"])
                S.op("pool", lambda e: e.tensor_tensor(out=ug[:], in0=t1[:], in1=ug[:], op=ALU.mult), r=["lr_t1", "lr_ug"], w=["lr_ug"])
                for d in range(2):
                    rr, ii, aa, bb = rr_[d], ii_[d], aa_[d], bb_[d]
                    kr, ki, ka, kb_ = ("lr_r", d), ("lr_i", d), ("lr_a", d), ("lr_b", d)
                    for g, (dst, bname, key) in enumerate([(rr, "lru_ba", kr), (ii, "lru_bx", ki)]):
                        for ti, (to, tn) in enumerate(self.TG):
                            pb = (g * 5 + ti) % 4
                            S.op("pe", lambda e, g=g, d=d, pb=pb, to=to, tn=tn: e.matmul(self.ps[pb][:, 0:tn], lhsT=wbd[g][d][:], rhs=xbb[:, to:to + tn],
                                                                                        start=True, stop=True),
                                 r=[("lr_w", g, d), ("lr_w", g, d, 0), ("lr_w", g, d, 1), "lr_xbb"], w=[("ps", pb)])
                            S.op("act", lambda e, dst=dst, bname=bname, d=d, pb=pb, to=to, tn=tn: e.activation(
                                out=dst[:, to:to + tn], in_=self.ps[pb][:, 0:tn], func=AF.Sigmoid, bias=self.pvc(bname, 4 * d + c)),
                                r=[("ps", pb), "pvt"], w=[key])
                    S.op("act", lambda e, d=d, rr=rr, ii=ii, aa=aa, bb=bb: e.activation(out=aa[:], in_=rr[:], func=AF.Exp, scale=c8[:, 4 * d + c:4 * d + c + 1]),
                         r=[kr, "lr_c8"], w=[ka])
                    S.op("dve", lambda e, rr=rr, ii=ii, aa=aa, bb=bb: e.tensor_tensor(out=bb[:], in0=aa[:], in1=aa[:], op=ALU.mult), r=[ka], w=[kb_])
                    S.op("dve", lambda e, rr=rr, ii=ii, aa=aa, bb=bb: e.tensor_scalar(out=bb[:], in0=bb[:], scalar1=-1.0, scalar2=1.0, op0=ALU.mult, op1=ALU.add), r=[kb_], w=[kb_])
                    S.op("dve", lambda e, rr=rr, ii=ii, aa=aa, bb=bb: e.tensor_scalar(out=bb[:], in0=bb[:], scalar1=1e-12, scalar2=None, op0=ALU.max), r=[kb_], w=[kb_])
                    S.op("act", lambda e, rr=rr, ii=ii, aa=aa, bb=bb: e.activation(out=bb[:], in_=bb[:], func=AF.Sqrt), r=[kb_], w=[kb_])
                    S.op("pool", lambda e, rr=rr, ii=ii, aa=aa, bb=bb: e.tensor_tensor(out=ii[:], in0=ii[:], in1=xb[:], op=ALU.mult), r=[ki, "lr_xb"], w=[ki])
                    S.op("dve", lambda e, rr=rr, ii=ii, aa=aa, bb=bb: e.tensor_tensor(out=bb[:], in0=bb[:], in1=ii[:], op=ALU.mult), r=[kb_, ki], w=[kb_])
                    if d == 0:
                        S.op("dve", lambda e, rr=rr, ii=ii, aa=aa, bb=bb: e.tensor_tensor_scan(out=hf[:], data0=aa[:], data1=bb[:], initial=0.0, op0=ALU.mult, op1=ALU.add),
                             r=[ka, kb_], w=["lr_hf"])
                    else:
                        S.op("dve", lambda e, rr=rr, ii=ii, aa=aa, bb=bb: e.tensor_tensor_scan(out=hb[:, 0:NCTX][:, ::-1], data0=aa[:, 0:NCTX][:, ::-1], data1=bb[:, 0:NCTX][:, ::-1],
                                                                   initial=0.0, op0=ALU.mult, op1=ALU.add), r=[ka, kb_], w=["lr_hb"])
                        S.op("dve", lambda e, rr=rr, ii=ii, aa=aa, bb=bb: e.tensor_tensor_scan(out=hb[:, NCTX:T][:, ::-1], data0=aa[:, NCTX:T][:, ::-1], data1=bb[:, NCTX:T][:, ::-1],
                                                                   initial=hb[:, 0:1], op0=ALU.mult, op1=ALU.add), r=[ka, kb_, "lr_hb"], w=["lr_hb"])
                S.op("dve", lambda e: e.tensor_tensor(out=hf[:], in0=hf[:], in1=hb[:], op=ALU.add), r=["lr_hf", "lr_hb"], w=["lr_hf"])
                S.op("dve", lambda e: e.tensor_tensor(out=yb[:], in0=hf[:], in1=ug[:], op=ALU.mult), r=["lr_hf", "lr_ug"], w=["lr_yb"])
                S.dma("sp", self.Ys[4 + c, :, :], yb[:], r=["lr_yb"], w=[("Ys", 4 + c)])
                self.pump_mods(9)
            S.barrier()

    def rms_rows(self, rows_sq, n_chunks, width, rstd, key_sq, key_rstd, pbase=0):
        S = self.S
        for ti, (to, tn) in enumerate(self.TG):
            pb = pbase + ti % 2
            S.group("pe", [
                (lambda e, ci=ci, pb=pb, to=to, tn=tn: e.matmul(self.ps[pb][:, 0:tn], lhsT=self.ones_bf[:], rhs=rows_sq[ci][:, to:to + tn],
                                                             start=(ci == 0), stop=(ci == n_chunks - 1)))
                for ci in range(n_chunks)], r=list(key_sq) + ["ones_bf"], w=[("ps", pb)])
            S.op("act", lambda e, pb=pb, to=to, tn=tn: e.activation(out=rstd[:, to:to + tn], in_=self.ps[pb][:, 0:tn], func=AF.Sqrt,
                                                                  scale=1.0 / width, bias=self.eps_col[:, 0:1]),
                 r=[("ps", pb), "eps_col"], w=[key_rstd])
        S.op("dve", lambda e: e.reciprocal(rstd[:, :], rstd[:, :]), r=[key_rstd], w=[key_rstd])

    def mix_hgrn(self, l):
        nc, S = self.nc, self.S
        SUB = 32
        NSC = T // SUB
        orders = {0: list(range(NB)), 1: [1, 0] + list(range(NB - 1, 1, -1))}
        with ExitStack() as ls:
            sbl = lambda name, shape, dt=F32: ls.enter_context(nc.sbuf_tensor(self.un(name), list(shape), dt))
            uin = sbl("hg_uin", [128, T])
            q = sbl("hg_q", [128, T]); sgate = sbl("hg_sg", [128, T]); vbf = sbl("hg_vbf", [128, T], BF16)
            v_tok = sbl("hg_vtok", [128, NB, 128], BF16)
            maskS = [sbl(f"hg_mask{d}", [128, T], BF16) for d in range(2)]
            bdm = [sbl(f"hg_bdm{d}", [128, 128], F32) for d in range(2)]
            cum = [sbl(f"hg_cum{d}", [128, T]) for d in range(2)]
            kk = [sbl(f"hg_kk{d}", [128, T]) for d in range(2)]
            tmpr = [sbl(f"hg_tmp{d}", [128, T]) for d in range(2)]
            qd = [sbl(f"hg_qd{d}", [128, T], BF16) for d in range(2)]
            kd = [sbl(f"hg_kd{d}", [128, T], BF16) for d in range(2)]
            kl = [sbl(f"hg_kl{d}", [128, T], BF16) for d in range(2)]
            kl_tok = [sbl(f"hg_kltok{d}", [128, NB, 128], BF16) for d in range(2)]
            klm = [sbl(f"hg_klm{d}", [128, 4, 128], BF16) for d in range(2)]
            submask = sbl("hg_submask", [128, 4], BF16)
            S.op("pool", lambda e: e.memset(submask[:], 0.0), w=["hg_submask"])
            for c in range(4):
                S.op("pool", lambda e, c=c: e.memset(submask[32 * c:32 * c + 32, c:c + 1], 1.0), r=["hg_submask"], w=["hg_submask"])
            gdec = [sbl(f"hg_gdec{d}", [128, NSC]) for d in range(2)]
            orow = [sbl(f"hg_o{d}", [128, T]) for d in range(2)]
            Sst = [[sbl(f"hg_S{d}{k}", [128, 128]) for k in range(2)] for d in range(2)]
            Sbf = [[sbl(f"hg_Sbf{d}{k}", [128, 128], BF16) for k in range(2)] for d in range(2)]
            PT = [sbl(f"hg_PT{d}", [128, 128], BF16) for d in range(2)]
            lbc = sbl("hg_lbc", [128, 8]); omlb = sbl("hg_omlb", [128, 8]); nomlb = sbl("hg_nomlb", [128, 8])
            for d in range(2):
                S.op("pool", lambda e, d=d: e.memset(maskS[d][:], 1.0), w=[("hg_mask", d)])
                off = 0 if d == 0 else SUB - 1
                S.op("pool", lambda e, d=d, off=off: e.memset(maskS[d][:, off::SUB], 0.0), r=[("hg_mask", d)], w=[("hg_mask", d)])
                S.op("pool", lambda e, d=d: e.memset(bdm[d][:], 1.0), w=[("hg_bdm", d)])
                sgn = 1 if d == 0 else -1
                S.op("pool", lambda e, d=d, sgn=sgn: e.affine_select(out=bdm[d][:], in_=bdm[d][:], pattern=[[sgn, 128]], compare_op=ALU.is_ge,
                                                                    fill=0.0, base=0, channel_multiplier=-sgn), r=[("hg_bdm", d)], w=[("hg_bdm", d)])
                for c in range(4):
                    if d == 0:
                        S.op("pool", lambda e, d=d, c=c: e.affine_select(out=bdm[d][:, SUB * c:SUB * c + SUB], in_=bdm[d][:, SUB * c:SUB * c + SUB],
                                                                        pattern=[[0, SUB]], compare_op=ALU.is_ge, fill=0.0, base=-SUB * c, channel_multiplier=1),
                             r=[("hg_bdm", d)], w=[("hg_bdm", d)])
                    else:
                        S.op("pool", lambda e, d=d, c=c: e.affine_select(out=bdm[d][:, SUB * c:SUB * c + SUB], in_=bdm[d][:, SUB * c:SUB * c + SUB],
                                                                        pattern=[[0, SUB]], compare_op=ALU.is_ge, fill=0.0, base=SUB * c + SUB - 1, channel_multiplier=-1),
                             r=[("hg_bdm", d)], w=[("hg_bdm", d)])
            olb, _ = PV["hg_lb"]
            if l == 0:
                S.op("dve", lambda e: e.memset(lbc[:], 0.0), w=["hg_lbc"])
            else:
                for d in range(2):
                    S.op("dve", lambda e, d=d: e.tensor_tensor(out=lbc[:, 4 * d:4 * d + 4], in0=self.pvt[:, olb + (2 * d + 1) * 4:olb + (2 * d + 1) * 4 + 4],
                                                               in1=self.pvt[:, olb + (2 * d) * 4:olb + (2 * d) * 4 + 4], op=ALU.subtract), r=["pvt"], w=["hg_lbc"])
                S.op("act", lambda e: e.activation(out=lbc[:], in_=lbc[:], func=AF.Sigmoid), r=["hg_lbc"], w=["hg_lbc"])
            S.op("dve", lambda e: e.tensor_scalar(out=omlb[:], in0=lbc[:], scalar1=-1.0, scalar2=1.0, op0=ALU.mult, op1=ALU.add), r=["hg_lbc"], w=["hg_omlb"])
            S.op("dve", lambda e: e.tensor_scalar(out=nomlb[:], in0=lbc[:], scalar1=-1.0, scalar2=None, op0=ALU.add), r=["hg_lbc"], w=["hg_nomlb"])
            hg_stage = self.debug.get("hg_stage", 99)
            if hg_stage <= 1:
                S.barrier()
                return
            for hd in range(4):
                g_q = self.GIDX["hg_q"][hd]; g_i = self.GIDX["hg_i"][hd]; g_g = self.GIDX["hg_g"][hd]
                S.dma("sp", uin[:], self.Us[g_q, :, :], r=[("Us", g_q)], w=["hg_uin"])
                S.op("act", lambda e: e.activation(out=q[:], in_=uin[:], func=AF.Silu), r=["hg_uin"], w=["hg_q"])
                S.op("pool", lambda e: e.tensor_scalar(out=q[:], in0=q[:], scalar1=128.0 ** -0.5, scalar2=None, op0=ALU.mult), r=["hg_q"], w=["hg_q"])
                S.dma("sp", uin[:], self.Us[g_i, :, :], r=[("Us", g_i)], w=["hg_uin"])
                S.op("act", lambda e: e.activation(out=vbf[:], in_=uin[:], func=AF.Copy), r=["hg_uin"], w=["hg_vbf"])
                S.dma("sp", uin[:], self.Us[g_g, :, :], r=[("Us", g_g)], w=["hg_uin"])
                S.op("act", lambda e: e.activation(out=sgate[:], in_=uin[:], func=AF.Silu), r=["hg_uin"], w=["hg_sg"])
                for b4 in range(0, NB, 4):
                    nb_ = min(4, NB - b4)
                    pb = 6 + (b4 // 4) % 2
                    pv_ = self.ps[pb][:, :].bitcast(BF16)
                    S.group("pe", [
                        (lambda e, b=b, pv_=pv_, b4=b4: e.transpose(pv_[:, (b - b4) * 128:(b - b4 + 1) * 128], vbf[:, b * 128:(b + 1) * 128], self.ident_bf[:]))
                        for b in range(b4, b4 + nb_)], r=["hg_vbf", "ident_bf"], w=[("ps", pb)])
                    S.op("dve", lambda e, pv_=pv_, b4=b4, nb_=nb_: e.tensor_copy(out=v_tok[:, b4:b4 + nb_, :],
                                                                                in_=pv_[:, 0:nb_ * 128].rearrange("p (a b) -> p a b", b=128)),
                         r=[("ps", pb)], w=["hg_vtok"])
                if hg_stage <= 2:
                    S.barrier()
                    return
                for d in range(2):
                    g_f = self.GIDX["hg_ff" if d == 0 else "hg_fb"][hd]
                    ci = 4 * d + hd
                    S.dma("sp", uin[:], self.Us[g_f, :, :], r=[("Us", g_f)], w=["hg_uin"])
                    S.op("act", lambda e, d=d: e.activation(out=tmpr[d][:], in_=uin[:], func=AF.Sigmoid), r=["hg_uin"], w=[("hg_tmp", d)])
                    S.op("dve", lambda e, d=d, ci=ci: e.tensor_scalar(out=kk[d][:], in0=tmpr[d][:], scalar1=nomlb[:, ci:ci + 1], scalar2=omlb[:, ci:ci + 1],
                                                                     op0=ALU.mult, op1=ALU.add), r=[("hg_tmp", d), "hg_omlb", "hg_nomlb"], w=[("hg_kk", d)])
                    S.op("act", lambda e, d=d, ci=ci: e.activation(out=tmpr[d][:], in_=tmpr[d][:], func=AF.Ln, scale=omlb[:, ci:ci + 1], bias=lbc[:, ci:ci + 1]),
                         r=[("hg_tmp", d), "hg_omlb", "hg_lbc"], w=[("hg_tmp", d)])
                    if d == 0:
                        S.op("dve", lambda e, d=d: e.tensor_tensor_scan(out=cum[d][:], data0=maskS[d][:], data1=tmpr[d][:], initial=0.0, op0=ALU.mult, op1=ALU.add),
                             r=[("hg_mask", d), ("hg_tmp", d)], w=[("hg_cum", d)])
                    else:
                        S.op("dve", lambda e, d=d: e.tensor_tensor_scan(out=cum[d][:, ::-1], data0=maskS[d][:, ::-1], data1=tmpr[d][:, ::-1], initial=0.0,
                                                                       op0=ALU.mult, op1=ALU.add), r=[("hg_mask", d), ("hg_tmp", d)], w=[("hg_cum", d)])
                    cl_off = SUB - 1 if d == 0 else 0
                    cumL = cum[d][:, cl_off::SUB]
                    S.op("act", lambda e, d=d, cumL=cumL: e.activation(out=gdec[d][:], in_=cumL, func=AF.Exp), r=[("hg_cum", d)], w=[("hg_gdec", d)])
                    S.op("pool", lambda e, d=d, cumL=cumL: e.tensor_tensor(out=tmpr[d][:].rearrange("p (a b) -> p a b", b=SUB),
                                                                          in0=cumL.unsqueeze(2).to_broadcast([128, NSC, SUB]),
                                                                          in1=cum[d][:].rearrange("p (a b) -> p a b", b=SUB), op=ALU.subtract),
                         r=[("hg_cum", d)], w=[("hg_tmp", d)])
                    S.op("act", lambda e, d=d: e.activation(out=tmpr[d][:], in_=tmpr[d][:], func=AF.Exp), r=[("hg_tmp", d)], w=[("hg_tmp", d)])
                    S.op("dve", lambda e, d=d: e.tensor_tensor(out=kl[d][:], in0=tmpr[d][:], in1=kk[d][:], op=ALU.mult), r=[("hg_tmp", d), ("hg_kk", d)], w=[("hg_kl", d)])
                    S.op("act", lambda e, d=d: e.activation(out=tmpr[d][:], in_=cum[d][:], func=AF.Exp, scale=-1.0), r=[("hg_cum", d)], w=[("hg_tmp", d)])
                    S.op("dve", lambda e, d=d: e.tensor_tensor(out=kd[d][:], in0=tmpr[d][:], in1=kk[d][:], op=ALU.mult), r=[("hg_tmp", d), ("hg_kk", d)], w=[("hg_kd", d)])
                    S.op("act", lambda e, d=d: e.activation(out=tmpr[d][:], in_=cum[d][:], func=AF.Exp), r=[("hg_cum", d)], w=[("hg_tmp", d)])
                    S.op("pool", lambda e, d=d: e.tensor_tensor(out=qd[d][:], in0=tmpr[d][:], in1=q[:], op=ALU.mult), r=[("hg_tmp", d), "hg_q"], w=[("hg_qd", d)])
                    for b4 in range(0, NB, 4):
                        nb_ = min(4, NB - b4)
                        pb = 6 + (b4 // 4) % 2
                        pv_ = self.ps[pb][:, :].bitcast(BF16)
                        S.group("pe", [
                            (lambda e, b=b, pv_=pv_, b4=b4, d=d: e.transpose(pv_[:, (b - b4) * 128:(b - b4 + 1) * 128], kl[d][:, b * 128:(b + 1) * 128], self.ident_bf[:]))
                            for b in range(b4, b4 + nb_)], r=[("hg_kl", d), "ident_bf"], w=[("ps", pb)])
                        S.op("act", lambda e, pv_=pv_, b4=b4, nb_=nb_, d=d: e.activation(out=kl_tok[d][:, b4:b4 + nb_, :],
                                                                                         in_=pv_[:, 0:nb_ * 128].rearrange("p (a b) -> p a b", b=128), func=AF.Copy),
                             r=[("ps", pb)], w=[("hg_kltok", d)])
                    S.op("pool", lambda e, d=d: e.memset(Sst[d][0][:], 0.0), w=[("hg_S", d, 0)])
                    S.op("pool", lambda e, d=d: e.memset(Sbf[d][0][:], 0.0), w=[("hg_Sbf", d, 0)])
                if hg_stage <= 3:
                    S.barrier()
                    return
                sbi = [0, 0]
                for step in range(NB):
                    for d in range(2):
                        b = orders[d][step]
                        p_sc, p_o, p_u = 3 * d, 3 * d + 1, 3 * d + 2
                        bs = slice(b * 128, (b + 1) * 128)
                        S.op("pe", lambda e, d=d, bs=bs, p_sc=p_sc: e.matmul(self.ps[p_sc][:, 0:128], lhsT=kd[d][:, bs], rhs=qd[d][:, bs], start=True, stop=True),
                             r=[("hg_kd", d), ("hg_qd", d)], w=[("ps", p_sc)])
                        S.op("dve", lambda e, d=d, p_sc=p_sc: e.tensor_tensor(out=PT[d][:], in0=self.ps[p_sc][:, 0:128], in1=bdm[d][:], op=ALU.mult),
                             r=[("ps", p_sc), ("hg_bdm", d)], w=[("hg_PT", d)])
                        S.op("pool", lambda e, d=d, b=b: e.tensor_tensor(out=klm[d][:], in0=kl_tok[d][:, b, :].unsqueeze(1).to_broadcast([128, 4, 128]),
                                                                        in1=submask[:, :].unsqueeze(2).to_broadcast([128, 4, 128]), op=ALU.mult),
                             r=[("hg_kltok", d), "hg_submask"], w=[("hg_klm", d)])
                        S.group("pe", [
                            (lambda e, c=c, d=d, b=b, p_u=p_u: e.matmul(self.ps[p_u][:, c * 128:(c + 1) * 128], lhsT=klm[d][:, c, :],
                                                                        rhs=v_tok[:, b, :], start=True, stop=True))
                            for c in range(4)], r=[("hg_klm", d), "hg_vtok"], w=[("ps", p_u)])
                        S.op("pe", lambda e, d=d, b=b, p_o=p_o: e.matmul(self.ps[p_o][:, 0:128], lhsT=v_tok[:, b, :], rhs=PT[d][:], start=True, stop=False),
                             r=["hg_vtok", ("hg_PT", d)], w=[("ps", p_o)])
                        corder = range(4) if d == 0 else range(3, -1, -1)
                        for ci_, c in enumerate(corder):
                            cur = sbi[d]
                            S.op("pe", lambda e, d=d, b=b, c=c, p_o=p_o, cur=cur, ci_=ci_: e.matmul(
                                self.ps[p_o][:, SUB * c:SUB * c + SUB], lhsT=Sbf[d][cur][:], rhs=qd[d][:, b * 128 + SUB * c:b * 128 + SUB * c + SUB],
                                start=False, stop=(ci_ == 3)), r=[("hg_Sbf", d, cur), ("hg_qd", d)], w=[("ps", p_o)])
                            sc_idx = b * 4 + c
                            nxt = 1 - cur
                            S.op("dve", lambda e, d=d, c=c, p_u=p_u, sc_idx=sc_idx, cur=cur, nxt=nxt: e.scalar_tensor_tensor(
                                out=Sst[d][nxt][:], in0=Sst[d][cur][:], scalar=gdec[d][:, sc_idx:sc_idx + 1], in1=self.ps[p_u][:, c * 128:(c + 1) * 128],
                                op0=ALU.mult, op1=ALU.add), r=[("hg_S", d, cur), ("hg_gdec", d), ("ps", p_u)], w=[("hg_S", d, nxt)])
                            S.op("act", lambda e, d=d, nxt=nxt: e.activation(out=Sbf[d][nxt][:], in_=Sst[d][nxt][:], func=AF.Copy),
                                 r=[("hg_S", d, nxt)], w=[("hg_Sbf", d, nxt)])
                            sbi[d] = nxt
                        S.op("act", lambda e, d=d, bs=bs, p_o=p_o: e.activation(out=orow[d][:, bs], in_=self.ps[p_o][:, 0:128], func=AF.Copy),
                             r=[("ps", p_o)], w=[("hg_o", d)])
                if hg_stage <= 4:
                    S.barrier()
                    return
                S.op("dve", lambda e: e.tensor_tensor(out=orow[0][:], in0=orow[0][:], in1=orow[1][:], op=ALU.add), r=[("hg_o", 0), ("hg_o", 1)], w=[("hg_o", 0)])
                S.op("act", lambda e: e.activation(out=qd[0][:], in_=orow[0][:], func=AF.Square), r=[("hg_o", 0)], w=[("hg_qd", 0)])
                self.rms_rows([qd[0]], 1, 128.0, tmpr[0], [("hg_qd", 0)], ("hg_tmp", 0), pbase=6)
                S.op("dve", lambda e: e.scalar_tensor_tensor(out=orow[0][:], in0=orow[0][:], scalar=self.pvc("hg_nw"), in1=tmpr[0][:], op0=ALU.mult, op1=ALU.mult),
                     r=[("hg_o", 0), ("hg_tmp", 0), "pvt"], w=[("hg_o", 0)])
                S.op("dve", lambda e: e.tensor_tensor(out=kd[0][:], in0=orow[0][:], in1=sgate[:], op=ALU.mult), r=[("hg_o", 0), "hg_sg"], w=[("hg_kd", 0)])
                S.dma("sp", self.Ys[8 + hd, :, :], kd[0][:], r=[("hg_kd", 0)], w=[("Ys", 8 + hd)])
                self.pump_mods(9)
            S.barrier()

    def mix_ret(self, l):
        nc, S = self.nc, self.S
        orders = {0: list(range(NB)), 1: [1, 0] + list(range(NB - 1, 1, -1))}
        with ExitStack() as ls:
            sbl = lambda name, shape, dt=F32: ls.enter_context(nc.sbuf_tensor(self.un(name), list(shape), dt))
            uin = sbl("rt_uin", [128, T]); t1 = sbl("rt_t1", [128, T]); t2 = sbl("rt_t2", [128, T])
            xbf = sbl("rt_xbf", [64, T], BF16)
            cosT = sbl("rt_cos", [64, NLAT]); sinT = sbl("rt_sin", [64, NLAT])
            Rbf = sbl("rt_R", [64, 64], BF16)
            tab = sbl("rt_tab", [128, 5, 128])
            qb = sbl("rt_qb", [64, T], BF16); kb = sbl("rt_kb", [64, T], BF16)
            qE = [sbl(f"rt_qE{d}", [64, T], BF16) for d in range(2)]
            kw = [sbl(f"rt_kw{d}", [64, T], BF16) for d in range(2)]
            kw_tok = [sbl(f"rt_kwtok{d}", [128, NB, 64], BF16) for d in range(2)]
            vbf = sbl("rt_vbf", [128, T], BF16); v_tok = sbl("rt_vtok", [128, NB, 128], BF16)
            Sall = [sbl(f"rt_Sall{d}", [64, NB, 128], BF16) for d in range(2)]
            Sst = [[sbl(f"rt_S{d}{k}", [64, 128]) for k in range(2)] for d in range(2)]
            PT = [sbl(f"rt_PT{k}", [128, 128], BF16) for k in range(2)]
            orow = sbl("rt_o", [128, T]); sgate = sbl("rt_sg", [128, T]); ybf = sbl("rt_ybf", [128, T], BF16)
            S.dma("sp", cosT[:], self.rope[0, :, :], w=["rt_cos"])
            S.dma("sp", sinT[:], self.rope[1, :, :], w=["rt_sin"])
            S.dma("pool", Rbf[:], self.ropeR[:, :], w=["rt_R"])
            for h in range(4):
                g128 = float(np.exp(128.0 * np.log1p(-(2.0 ** (-5.0 - h)))))
                S.dma("sp", tab[:], self.rtab[h, :, :, :], w=["rt_tab"])
                for which, dst, scl in (("ret_q", qb, 1.0), ("ret_k", kb, 0.125)):
                    gi = self.GIDX[which][h]
                    S.dma("sp", uin[0:64, :], self.Us[gi, 0:64, :], r=[("Us", gi)], w=["rt_uin"])
                    S.op("act", lambda e: e.activation(out=xbf[:], in_=uin[0:64, :], func=AF.Copy), r=["rt_uin"], w=["rt_xbf"])
                    for ti in range(4):
                        pb = ti % 2
                        S.op("pe", lambda e, ti=ti, pb=pb: e.matmul(self.ps[pb][0:64, 0:512], lhsT=Rbf[:], rhs=xbf[:, NCTX + ti * 512:NCTX + (ti + 1) * 512],
                                                                  start=True, stop=True), r=["rt_R", "rt_xbf"], w=[("ps", pb)])
                        S.op("dve", lambda e, ti=ti, pb=pb: e.tensor_tensor(out=t1[0:64, NCTX + ti * 512:NCTX + (ti + 1) * 512], in0=self.ps[pb][0:64, 0:512],
                                                                           in1=sinT[:, ti * 512:(ti + 1) * 512], op=ALU.mult),
                             r=[("ps", pb), "rt_sin"], w=["rt_t1"])
                    S.op("pool", lambda e: e.tensor_tensor(out=t2[0:64, NCTX:T], in0=uin[0:64, NCTX:T], in1=cosT[:], op=ALU.mult), r=["rt_uin", "rt_cos"], w=["rt_t2"])
                    S.op("dve", lambda e, dst=dst, scl=scl: e.scalar_tensor_tensor(out=dst[:, NCTX:T], in0=t1[0:64, NCTX:T], scalar=scl, in1=t2[0:64, NCTX:T],
                                                                                  op0=ALU.mult, op1=ALU.add) if scl == 1.0 else
                         e.tensor_tensor(out=t1[0:64, NCTX:T], in0=t1[0:64, NCTX:T], in1=t2[0:64, NCTX:T], op=ALU.add),
                         r=["rt_t1", "rt_t2"], w=["rt_t1", ("rt_qk", which)])
                    if scl != 1.0:
                        S.op("dve", lambda e, dst=dst, scl=scl: e.tensor_scalar(out=dst[:, NCTX:T], in0=t1[0:64, NCTX:T], scalar1=scl, scalar2=None, op0=ALU.mult),
                             r=["rt_t1"], w=[("rt_qk", which)])
                    S.op("act", lambda e, dst=dst, scl=scl: e.activation(out=dst[:, 0:NCTX], in_=uin[0:64, 0:NCTX], func=AF.Copy, scale=scl),
                         r=["rt_uin"], w=[("rt_qk", which, "c")])
                rq = [("rt_qk", "ret_q"), ("rt_qk", "ret_q", "c")]
                rk = [("rt_qk", "ret_k"), ("rt_qk", "ret_k", "c")]
                for d in range(2):
                    S.op("dve", lambda e, d=d: e.tensor_tensor(out=qE[d][:].rearrange("p (a b) -> p a b", b=128), in0=qb[:].rearrange("p (a b) -> p a b", b=128),
                                                               in1=tab[0:64, 1 + d, :].unsqueeze(1).to_broadcast([64, NB, 128]), op=ALU.mult),
                         r=rq + ["rt_tab"], w=[("rt_qE", d)])
                    S.op("pool", lambda e, d=d: e.tensor_tensor(out=kw[d][:].rearrange("p (a b) -> p a b", b=128), in0=kb[:].rearrange("p (a b) -> p a b", b=128),
                                                                in1=tab[0:64, 3 + d, :].unsqueeze(1).to_broadcast([64, NB, 128]), op=ALU.mult),
                         r=rk + ["rt_tab"], w=[("rt_kw", d)])
                    for b8 in range(0, NB, 8):
                        nb_ = min(8, NB - b8)
                        pb = 6 + (b8 // 8) % 2
                        pv_ = self.ps[pb][:, :].bitcast(BF16)
                        S.group("pe", [
                            (lambda e, b=b, pv_=pv_, b8=b8, d=d: e.transpose(pv_[:, (b - b8) * 64:(b - b8 + 1) * 64], kw[d][:, b * 128:(b + 1) * 128], self.ident_bf[0:64, 0:64]))
                            for b in range(b8, b8 + nb_)], r=[("rt_kw", d), "ident_bf"], w=[("ps", pb)])
                        S.op("act", lambda e, pv_=pv_, b8=b8, nb_=nb_, d=d: e.activation(out=kw_tok[d][:, b8:b8 + nb_, :],
                                                                                         in_=pv_[:, 0:nb_ * 64].rearrange("p (a b) -> p a b", b=64), func=AF.Copy),
                             r=[("ps", pb)], w=[("rt_kwtok", d)])
                gv = self.GIDX["ret_v"][h]; gg = self.GIDX["ret_g"][h]
                S.dma("sp", uin[:], self.Us[gv, :, :], r=[("Us", gv)], w=["rt_uin"])
                S.op("act", lambda e: e.activation(out=vbf[:], in_=uin[:], func=AF.Copy), r=["rt_uin"], w=["rt_vbf"])
                for b4 in range(0, NB, 4):
                    nb_ = min(4, NB - b4)
                    pb = 6 + (b4 // 4) % 2
                    pv_ = self.ps[pb][:, :].bitcast(BF16)
                    S.group("pe", [
                        (lambda e, b=b, pv_=pv_, b4=b4: e.transpose(pv_[:, (b - b4) * 128:(b - b4 + 1) * 128], vbf[:, b * 128:(b + 1) * 128], self.ident_bf[:]))
                        for b in range(b4, b4 + nb_)], r=["rt_vbf", "ident_bf"], w=[("ps", pb)])
                    S.op("dve", lambda e, pv_=pv_, b4=b4, nb_=nb_: e.tensor_copy(out=v_tok[:, b4:b4 + nb_, :],
                                                                                in_=pv_[:, 0:nb_ * 128].rearrange("p (a b) -> p a b", b=128)),
                         r=[("ps", pb)], w=["rt_vtok"])
                S.dma("sp", uin[:], self.Us[gg, :, :], r=[("Us", gg)], w=["rt_uin"])
                S.op("act", lambda e: e.activation(out=sgate[:], in_=uin[:], func=AF.Silu), r=["rt_uin"], w=["rt_sg"])
                for d in range(2):
                    S.op("pool", lambda e, d=d: e.memset(Sst[d][0][:], 0.0), w=[("rt_S", d, 0)])
                    S.op("pool", lambda e, d=d: e.memset(Sall[d][:, orders[d][0], :], 0.0), w=[("rt_Sall", d)])
                for step in range(NB - 1):
                    for d in range(2):
                        b = orders[d][step]
                        bn = orders[d][step + 1]
                        pu = 4 + d
                        S.op("pe", lambda e, d=d, b=b, pu=pu: e.matmul(self.ps[pu][0:64, 0:128], lhsT=kw_tok[d][:, b, :], rhs=v_tok[:, b, :], start=True, stop=True),
                             r=[("rt_kwtok", d), "rt_vtok"], w=[("ps", pu)])
                        cur = step % 2
                        nxt = 1 - cur
                        S.op("dve", lambda e, d=d, pu=pu, cur=cur, nxt=nxt: e.scalar_tensor_tensor(out=Sst[d][nxt][:], in0=Sst[d][cur][:], scalar=g128, in1=self.ps[pu][0:64, 0:128],
                                                                                 op0=ALU.mult, op1=ALU.add), r=[("rt_S", d, cur), ("ps", pu)], w=[("rt_S", d, nxt)])
                        S.op("act", lambda e, d=d, bn=bn, nxt=nxt: e.activation(out=Sall[d][:, bn, :], in_=Sst[d][nxt][:], func=AF.Copy), r=[("rt_S", d, nxt)], w=[("rt_Sall", d, bn)])
                for b in range(NB):
                    bs = slice(b * 128, (b + 1) * 128)
                    pa = b % 2
                    po = 2 + b % 2
                    S.op("pe", lambda e, bs=bs, pa=pa: e.matmul(self.ps[pa][:, 0:128], lhsT=kb[:, bs], rhs=qb[:, bs], start=True, stop=True),
                         r=rq + rk, w=[("ps", pa)])
                    S.op("dve", lambda e, pa=pa: e.tensor_tensor(out=PT[pa][:], in0=self.ps[pa][:, 0:128], in1=tab[:, 0, :], op=ALU.mult),
                         r=[("ps", pa), "rt_tab"], w=[("rt_PT", pa)])
                    S.group("pe", [
                        lambda e, b=b, po=po, pa=pa: e.matmul(self.ps[po][:, 0:128], lhsT=v_tok[:, b, :], rhs=PT[pa][:], start=True, stop=False),
                        lambda e, b=b, po=po, bs=bs: e.matmul(self.ps[po][:, 0:128], lhsT=Sall[0][:, b, :], rhs=qE[0][:, bs], start=False, stop=False),
                        lambda e, b=b, po=po, bs=bs: e.matmul(self.ps[po][:, 0:128], lhsT=Sall[1][:, b, :], rhs=qE[1][:, bs], start=False, stop=True)],
                        r=["rt_vtok", ("rt_PT", pa), ("rt_Sall", 0), ("rt_Sall", 1), ("rt_Sall", 0, b), ("rt_Sall", 1, b), ("rt_qE", 0), ("rt_qE", 1)], w=[("ps", po)])
                    S.op("act", lambda e, bs=bs, po=po: e.activation(out=orow[:, bs], in_=self.ps[po][:, 0:128], func=AF.Copy), r=[("ps", po)], w=["rt_o"])
                S.op("act", lambda e: e.activation(out=ybf[:], in_=orow[:], func=AF.Square), r=["rt_o"], w=["rt_ybf"])
                self.rms_rows([ybf], 1, 128.0, t1, ["rt_ybf"], "rt_t1", pbase=6)
                S.op("dve", lambda e: e.tensor_tensor(out=orow[:], in0=orow[:], in1=t1[:], op=ALU.mult), r=["rt_o", "rt_t1"], w=["rt_o"])
                S.op("dve", lambda e: e.tensor_tensor(out=ybf[:], in0=orow[:], in1=sgate[:], op=ALU.mult), r=["rt_o", "rt_sg"], w=["rt_ybf"])
                S.dma("sp", self.Ys[12 + h, :, :], ybf[:], r=["rt_ybf"], w=[("Ys", 12 + h)])
                self.pump_mods(9)
            S.barrier()

    def mix_ssd(self, l):
        nc, S = self.nc, self.S
        orders = {0: list(range(NB)), 1: [1, 0] + list(range(NB - 1, 1, -1))}
        with ExitStack() as ls:
            sbl = lambda name, shape, dt=F32: ls.enter_context(nc.sbuf_tensor(self.un(name), list(shape), dt))
            uin = sbl("sd_uin", [128, T]); t1 = sbl("sd_t1", [128, T])
            xs = sbl("sd_xs", [128, 4, T], BF16)
            BT = [sbl(f"sd_BT{g}", [64, T], BF16) for g in range(2)]
            CT = [sbl(f"sd_CT{g}", [64, T], BF16) for g in range(2)]
            x_tok = sbl("sd_xtok", [128, NB, 512], BF16)
            B_tok = sbl("sd_Btok", [128, NB, 2, 64], BF16)
            dl = sbl("sd_dl", [16, T]); la = sbl("sd_la", [16, T])
            dl_tok = sbl("sd_dltok", [128, NB, 16]); la_tok = sbl("sd_latok", [128, NB, 16])
            negA = sbl("sd_negA", [16, 1])
            yacc = sbl("sd_yacc", [128, 4, T])
            utri = [sbl(f"sd_utri{d}", [128, 128]) for d in range(2)]
            cumt = [sbl(f"sd_cumt{d}", [128, 8]) for d in range(2)]
            la_bc = [sbl(f"sd_labc{d}", [128, 8, 128]) for d in range(2)]
            E = [sbl(f"sd_E{d}", [64, 8, 128]) for d in range(2)]
            LT = [sbl(f"sd_LT{d}", [128, 8, 128]) for d in range(2)]
            GT = [sbl(f"sd_GT{d}", [128, 2, 128]) for d in range(2)]
            PT = [sbl(f"sd_PT{d}", [128, 8, 128], BF16) for d in range(2)]
            CE = [sbl(f"sd_CE{d}", [64, 8, 128], BF16) for d in range(2)]
            wcol = [sbl(f"sd_w{d}", [128, 8]) for d in range(2)]
            Bw = [sbl(f"sd_Bw{d}", [128, 8, 64], BF16) for d in range(2)]
            Sst = [[sbl(f"sd_S{d}{k}", [64, 8, 64]) for k in range(2)] for d in range(2)]
            Sbf = [sbl(f"sd_Sbf{d}", [64, 8, 64], BF16) for d in range(2)]
            ocw, _ = PV["ssd_cw"]
            onesf = sbl("sd_onesf", [128, 8, 128])
            S.op("pool", lambda e: e.memset(onesf[:], 1.0), w=["sd_onesf"])
            for d in range(2):
                sgn = 1 if d == 0 else -1
                S.op("pool", lambda e, d=d: e.memset(utri[d][:], 1.0), w=[("sd_utri", d)])
                S.op("pool", lambda e, d=d, sgn=sgn: e.affine_select(out=utri[d][:], in_=utri[d][:], pattern=[[sgn, 128]], compare_op=ALU.is_ge,
                                                                    fill=0.0, base=0, channel_multiplier=-sgn), r=[("sd_utri", d)], w=[("sd_utri", d)])
            for c in range(4):
                gi = self.GIDX["ssd_x"][c]
                S.dma("sp", uin[:], self.Us[gi, :, :], r=[("Us", gi)], w=["sd_uin"])
                wc = [self.pvt[:, ocw + 8 * j + c:ocw + 8 * j + c + 1] for j in range(4)]
                self.conv_row("dve", t1[:], uin[:], wc, self.pvc("ssd_cb", c), "sd_t1", "sd_uin", ["pvt"])
                S.op("act", lambda e, c=c: e.activation(out=xs[:, c, :], in_=t1[:], func=AF.Silu), r=["sd_t1"], w=[("sd_xs", c)])
            for which, dsts, base in (("ssd_B", BT, 4), ("ssd_C", CT, 6)):
                for g in range(2):
                    gi = self.GIDX[which][g]
                    S.dma("sp", uin[0:64, :], self.Us[gi, 0:64, :], r=[("Us", gi)], w=["sd_uin"])
                    col = base + g
                    wc = [self.pvt[0:64, ocw + 8 * j + col:ocw + 8 * j + col + 1] for j in range(4)]
                    self.conv_row("dve", t1[0:64, :], uin[0:64, :], wc, self.pvc("ssd_cb", col, rows=64), "sd_t1", "sd_uin", ["pvt"])
                    S.op("act", lambda e, dsts=dsts, g=g: e.activation(out=dsts[g][:], in_=t1[0:64, :], func=AF.Silu), r=["sd_t1"], w=[(which, g)])
            gi = self.GIDX["ssd_dt"][0]
            S.dma("sp", dl[:], self.Us[gi, 0:16, :], r=[("Us", gi)], w=["sd_dl"])
            S.op("act", lambda e: e.activation(out=dl[:], in_=dl[:], func=AF.Exp, bias=self.pvc("ssd_dtb", 0, rows=16)), r=["sd_dl", "pvt"], w=["sd_dl"])
            S.op("act", lambda e: e.activation(out=dl[:], in_=dl[:], func=AF.Ln, bias=self.one_col[0:16, 0:1]), r=["sd_dl", "one_col"], w=["sd_dl"])
            S.op("act", lambda e: e.activation(out=negA[:], in_=self.pvc("ssd_alog", 0, rows=16), func=AF.Exp), r=["pvt"], w=["sd_negA"])
            S.op("dve", lambda e: e.tensor_scalar(out=negA[:], in0=negA[:], scalar1=-1.0, scalar2=None, op0=ALU.mult), r=["sd_negA"], w=["sd_negA"])
            S.op("dve", lambda e: e.tensor_scalar(out=la[:], in0=dl[:], scalar1=negA[:, 0:1], scalar2=None, op0=ALU.mult), r=["sd_dl", "sd_negA"], w=["sd_la"])
            sd_stage = self.debug.get("sd_stage", 99)
            if sd_stage <= 1:
                S.barrier()
                return
            for src, dst, key in ((dl, dl_tok, "sd_dltok"), (la, la_tok, "sd_latok")):
                S.group("pe", [
                    (lambda e, b=b, src=src: e.transpose(self.ps[6][:, b * 16:(b + 1) * 16], src[:, b * 128:(b + 1) * 128], self.identf[0:16, 0:16]))
                    for b in range(NB)], r=["sd_dl", "sd_la", "identf"], w=[("ps", 6)])
                S.op("dve", lambda e, dst=dst: e.tensor_copy(out=dst[:], in_=self.ps[6][:, 0:NB * 16].rearrange("p (a b) -> p a b", b=16)),
                     r=[("ps", 6)], w=[key])
            if sd_stage <= 2:
                S.barrier()
                return
            cnt = 0
            for c in range(4):
                for b4 in range(0, NB, 4):
                    nb_ = min(4, NB - b4)
                    pb = 6 + cnt % 2
                    cnt += 1
                    pv_ = self.ps[pb][:, :].bitcast(BF16)
                    S.group("pe", [
                        (lambda e, b=b, pv_=pv_, b4=b4, c=c: e.transpose(pv_[:, (b - b4) * 128:(b - b4 + 1) * 128], xs[:, c, b * 128:(b + 1) * 128], self.ident_bf[:]))
                        for b in range(b4, b4 + nb_)], r=[("sd_xs", c), "ident_bf"], w=[("ps", pb)])
                    S.op("act" if cnt % 2 else "dve", (lambda e, pv_=pv_, b4=b4, nb_=nb_, c=c: e.activation(
                        out=x_tok[:, b4:b4 + nb_, c * 128:(c + 1) * 128], in_=pv_[:, 0:nb_ * 128].rearrange("p (a b) -> p a b", b=128), func=AF.Copy)) if cnt % 2 else
                        (lambda e, pv_=pv_, b4=b4, nb_=nb_, c=c: e.tensor_copy(
                            out=x_tok[:, b4:b4 + nb_, c * 128:(c + 1) * 128], in_=pv_[:, 0:nb_ * 128].rearrange("p (a b) -> p a b", b=128))),
                        r=[("ps", pb)], w=[("sd_xtok", c)])
            for g in range(2):
                for b8 in range(0, NB, 8):
                    nb_ = min(8, NB - b8)
                    pb = 6 + cnt % 2
                    cnt += 1
                    pv_ = self.ps[pb][:, :].bitcast(BF16)
                    S.group("pe", [
                        (lambda e, b=b, pv_=pv_, b8=b8, g=g: e.transpose(pv_[:, (b - b8) * 64:(b - b8 + 1) * 64], BT[g][:, b * 128:(b + 1) * 128], self.ident_bf[0:64, 0:64]))
                        for b in range(b8, b8 + nb_)], r=[("ssd_B", g), "ident_bf"], w=[("ps", pb)])
                    S.op("dve", lambda e, pv_=pv_, b8=b8, nb_=nb_, g=g: e.tensor_copy(out=B_tok[:, b8:b8 + nb_, g, :],
                                                                                     in_=pv_[:, 0:nb_ * 64].rearrange("p (a b) -> p a b", b=64)),
                         r=[("ps", pb)], w=[("sd_Btok", g)])
            for d in range(2):
                S.op("pool", lambda e, d=d: e.memset(Sst[d][0][:], 0.0), w=[("sd_S", d, 0)])
                S.op("pool", lambda e, d=d: e.memset(Sbf[d][:], 0.0), w=[("sd_Sbf", d)])
            xtok_keys = [("sd_xtok", c) for c in range(4)]
            if sd_stage <= 3:
                S.barrier()
                return
            ywritten = set()
            for step in range(NB):
                for d in range(2):
                    b = orders[d][step]
                    bs = slice(b * 128, (b + 1) * 128)
                    p0, p1, p2, p3 = 4 * d, 4 * d + 1, 4 * d + 2, 4 * d + 3
                    tl = 127 if d == 0 else 0
                    sgn = 1 if d == 0 else -1
                    la_b = la_tok[:, b, 8 * d:8 * d + 8]
                    dl_b = dl_tok[:, b, 8 * d:8 * d + 8]
                    S.op("pe", lambda e, d=d, p0=p0, la_b=la_b: e.matmul(self.ps[p0][:, 256:264], lhsT=utri[d][:], rhs=la_b, start=True, stop=True),
                         r=[("sd_utri", d), "sd_latok"], w=[("ps", p0)])
                    S.op("act", lambda e, d=d, p0=p0: e.activation(out=cumt[d][:], in_=self.ps[p0][:, 256:264], func=AF.Copy), r=[("ps", p0)], w=[("sd_cumt", d)])
                    S.op("dve", lambda e, d=d, la_b=la_b: e.tensor_tensor(out=la_bc[d][:], in0=onesf[:], in1=la_b.unsqueeze(2).to_broadcast([128, 8, 128]), op=ALU.mult),
                         r=["sd_latok", "sd_onesf"], w=[("sd_labc", d)])
                    if sd_stage == 41:
                        S.barrier()
                        return
                    S.group("pe", [
                        (lambda e, g=g, p0=p0, bs=bs: e.matmul(self.ps[p0][:, g * 128:(g + 1) * 128], lhsT=BT[g][:, bs], rhs=CT[g][:, bs], start=True, stop=True))
                        for g in range(2)], r=[("ssd_B", 0), ("ssd_B", 1), ("ssd_C", 0), ("ssd_C", 1)], w=[("ps", p0)])
                    S.op("act", lambda e, d=d, p0=p0: e.activation(out=GT[d][:], in_=self.ps[p0][:, 0:256].rearrange("p (a b) -> p a b", b=128), func=AF.Copy),
                         r=[("ps", p0)], w=[("sd_GT", d)])
                    if sd_stage == 42:
                        S.barrier()
                        return
                    for hf in range(2):
                        hs = slice(4 * hf, 4 * hf + 4)
                        S.group("pe", [
                            (lambda e, j=j, d=d, p1=p1, hf=hf: e.matmul(self.ps[p1][:, j * 128:(j + 1) * 128], lhsT=la_bc[d][:, 4 * hf + j, :], rhs=utri[d][:],
                                                                        start=True, stop=True))
                            for j in range(4)], r=[("sd_labc", d), ("sd_utri", d)], w=[("ps", p1)])
                        S.op("act", lambda e, d=d, p1=p1, hs=hs: e.activation(out=E[d][:, hs, :], in_=self.ps[p1][0:64, :].rearrange("p (a b) -> p a b", b=128), func=AF.Exp),
                             r=[("ps", p1)], w=[("sd_E", d, hf)])
                        if sd_stage == 43:
                            S.barrier()
                            return
                        S.op("dve", lambda e, d=d, p1=p1, hs=hs: e.tensor_tensor(out=LT[d][:, hs, :], in0=self.ps[p1][:, :].rearrange("p (a b) -> p a b", b=128),
                                                                                in1=cumt[d][:, hs].unsqueeze(2).to_broadcast([128, 4, 128]), op=ALU.subtract),
                             r=[("ps", p1), ("sd_cumt", d)], w=[("sd_LT", d, hf)])
                        if sd_stage == 44:
                            S.barrier()
                            return
                        S.op("act", lambda e, d=d, hs=hs: e.activation(out=LT[d][:, hs, :], in_=LT[d][:, hs, :], func=AF.Exp), r=[("sd_LT", d, hf)], w=[("sd_LT", d, hf)])
                        if sd_stage == 45:
                            S.barrier()
                            return
                        S.op("pool", lambda e, d=d, hs=hs, sgn=sgn: e.affine_select(out=LT[d][:, hs, :], in_=LT[d][:, hs, :], pattern=[[0, 4], [sgn, 128]],
                                                                                  compare_op=ALU.is_ge, fill=0.0, base=0, channel_multiplier=-sgn),
                             r=[("sd_LT", d, hf)], w=[("sd_LT", d, hf)])
                    ltk = [("sd_LT", d, 0), ("sd_LT", d, 1)]
                    if sd_stage <= 4:
                        S.barrier()
                        return
                    S.op("dve", lambda e, d=d, dl_b=dl_b, tl=tl: e.tensor_tensor(out=wcol[d][:], in0=dl_b, in1=LT[d][:, :, tl], op=ALU.mult),
                         r=ltk + ["sd_dltok"], w=[("sd_w", d)])
                    S.op("pool", lambda e, d=d, dl_b=dl_b: e.tensor_tensor(out=LT[d][:], in0=LT[d][:], in1=dl_b.unsqueeze(2).to_broadcast([128, 8, 128]), op=ALU.mult),
                         r=ltk + ["sd_dltok"], w=ltk)
                    for g in range(2):
                        S.op("dve", lambda e, d=d, g=g: e.tensor_tensor(out=PT[d][:, 4 * g:4 * g + 4, :], in0=LT[d][:, 4 * g:4 * g + 4, :],
                                                                       in1=GT[d][:, g, :].unsqueeze(1).to_broadcast([128, 4, 128]), op=ALU.mult),
                             r=ltk + [("sd_GT", d)], w=[("sd_PT", d, g)])
                        S.op("pool", lambda e, d=d, g=g, bs=bs: e.tensor_tensor(out=CE[d][:, 4 * g:4 * g + 4, :], in0=E[d][:, 4 * g:4 * g + 4, :],
                                                                              in1=CT[g][:, bs].unsqueeze(1).to_broadcast([64, 4, 128]), op=ALU.mult),
                             r=[("sd_E", d, g), ("ssd_C", g)], w=[("sd_CE", d, g)])
                        S.op("dve", lambda e, d=d, g=g, b=b: e.tensor_tensor(out=Bw[d][:, 4 * g:4 * g + 4, :], in0=B_tok[:, b, g, :].unsqueeze(1).to_broadcast([128, 4, 64]),
                                                                            in1=wcol[d][:, 4 * g:4 * g + 4].unsqueeze(2).to_broadcast([128, 4, 64]), op=ALU.mult),
                             r=[("sd_Btok", g), ("sd_w", d)], w=[("sd_Bw", d, g)])
                    if sd_stage <= 5:
                        S.barrier()
                        return
                    fns = []
                    for h in range(8):
                        po = self.ps[p2][64 * (h % 2):64 * (h % 2) + 64, (h // 2) * 128:(h // 2 + 1) * 128]
                        fns.append(lambda e, h=h, po=po, d=d, b=b: e.matmul(po, lhsT=x_tok[:, b, 64 * h:64 * h + 64], rhs=PT[d][:, h, :], start=True, stop=False))
                        fns.append(lambda e, h=h, po=po, d=d: e.matmul(po, lhsT=Sbf[d][:, h, :], rhs=CE[d][:, h, :], start=False, stop=True))
                    S.group("pe", fns, r=xtok_keys + [("sd_PT", d, 0), ("sd_PT", d, 1), ("sd_CE", d, 0), ("sd_CE", d, 1), ("sd_Sbf", d)], w=[("ps", p2)])
                    yv = yacc[:, :, bs]
                    pv3 = self.ps[p2][:, :].rearrange("p (a b) -> p a b", b=128)
                    if b not in ywritten:
                        ywritten.add(b)
                        S.op("act", lambda e, yv=yv, pv3=pv3: e.activation(out=yv, in_=pv3, func=AF.Copy), r=[("ps", p2)], w=[("sd_yacc", b)])
                    else:
                        S.op("dve", lambda e, yv=yv, pv3=pv3: e.tensor_tensor(out=yv, in0=yv, in1=pv3, op=ALU.add), r=[("ps", p2), ("sd_yacc", b)], w=[("sd_yacc", b)])
                    if sd_stage <= 6:
                        S.barrier()
                        return
                    S.group("pe", [
                        (lambda e, h=h, d=d, b=b, p3=p3: e.matmul(self.ps[p3][0:64, 64 * h:64 * h + 64], lhsT=Bw[d][:, h, :], rhs=x_tok[:, b, 64 * h:64 * h + 64],
                                                                  start=True, stop=True))
                        for h in range(8)], r=xtok_keys + [("sd_Bw", d, 0), ("sd_Bw", d, 1)], w=[("ps", p3)])
                    cur = step % 2
                    nxt = 1 - cur
                    S.op("pool", lambda e, d=d, tl=tl, cur=cur, nxt=nxt: e.tensor_tensor(out=Sst[d][nxt][:], in0=Sst[d][cur][:], in1=E[d][:, :, tl].unsqueeze(2).to_broadcast([64, 8, 64]), op=ALU.mult),
                         r=[("sd_S", d, cur), ("sd_E", d, 0), ("sd_E", d, 1)], w=[("sd_S", d, nxt)])
                    S.op("dve", lambda e, d=d, p3=p3, nxt=nxt: e.tensor_tensor(out=Sst[d][nxt][:], in0=Sst[d][nxt][:], in1=self.ps[p3][0:64, :].rearrange("p (a b) -> p a b", b=64), op=ALU.add),
                         r=[("sd_S", d, nxt), ("ps", p3)], w=[("sd_S", d, nxt)])
                    S.op("act", lambda e, d=d, nxt=nxt: e.activation(out=Sbf[d][:], in_=Sst[d][nxt][:], func=AF.Copy), r=[("sd_S", d, nxt)], w=[("sd_Sbf", d)])
            xflat = x_tok[:].rearrange("p a b -> p (a b)")
            sqb = [xflat[:, c * T:(c + 1) * T] for c in range(4)]
            yk = [("sd_yacc", b) for b in range(NB)]
            for c in range(4):
                gi = self.GIDX["ssd_z"][c]
                S.dma("sp", uin[:], self.Us[gi, :, :], r=[("Us", gi)], w=["sd_uin"])
                S.op("act", lambda e: e.activation(out=t1[:], in_=uin[:], func=AF.Silu), r=["sd_uin"], w=["sd_t1"])
                S.op("dve", lambda e, c=c: e.scalar_tensor_tensor(out=yacc[:, c, :], in0=xs[:, c, :], scalar=self.pvc("ssd_d", c), in1=yacc[:, c, :],
                                                                  op0=ALU.mult, op1=ALU.add), r=yk + [("sd_xs", c), "pvt"], w=[("sd_y", c)])
                S.op("dve", lambda e, c=c: e.tensor_tensor(out=yacc[:, c, :], in0=yacc[:, c, :], in1=t1[:], op=ALU.mult), r=[("sd_y", c), "sd_t1"], w=[("sd_y", c)])
                S.op("act", lambda e, c=c: e.activation(out=sqb[c][:], in_=yacc[:, c, :], func=AF.Square), r=[("sd_y", c)], w=[("sd_sq", c)] + xtok_keys)
            self.rms_rows(sqb, 4, 512.0, t1, [("sd_sq", c) for c in range(4)], "sd_t1", pbase=6)
            for c in range(4):
                S.op("dve", lambda e, c=c: e.scalar_tensor_tensor(out=sqb[c][:], in0=yacc[:, c, :], scalar=self.pvc("ssd_nw", c), in1=t1[:],
                                                                  op0=ALU.mult, op1=ALU.mult), r=[("sd_y", c), "sd_t1", "pvt", ("sd_sq", c)], w=[("sd_yo", c)])
                S.dma("sp", self.Ys[c, :, :], sqb[c][:], r=[("sd_yo", c)], w=[("Ys", c)])
            S.barrier()

    def outproj(self, l, tiles):
        nc, S = self.nc, self.S
        TT = 768
        with ExitStack() as ls:
            sbl = lambda name, shape, dt=F32: ls.enter_context(nc.sbuf_tensor(self.un(name), list(shape), dt))
            wo = sbl("op_wo", [128, KC, D], BF16)
            yTs = [sbl(f"op_yT{k}", [128, KC, TT], BF16) for k in range(2)]
            oTs = [sbl(f"op_oT{k}", [128, KC, TT], BF16) for k in range(2)]
            xg = [sbl(f"op_xg{k}", [128, 2, TT], F32) for k in range(2)]
            sq = [sbl(f"op_sq{k}", [128, 2, TT], BF16) for k in range(2)]
            rstds = [sbl(f"op_rstd{k}", [128, TT], F32) for k in range(2)]
            tmp = [sbl(f"op_tmp{k}", [128, TT], F32) for k in range(2)]
            wv = self.w_out[l].rearrange("(kc p) n -> p kc n", p=128)
            for q4 in range(4):
                S.dma("pool", wo[:, :, q4 * 512:(q4 + 1) * 512], wv[:, :, q4 * 512:(q4 + 1) * 512], w=[("op_wo", q4)])
            okeys = ["oTa", "oTb"]
            pending = None
            for ti_, (t0, n) in enumerate(tiles):
                yT = yTs[ti_ % 2]; oT = oTs[ti_ % 2]; ok = okeys[ti_ % 2]; rstd = rstds[ti_ % 2]
                hv = halves(n)
                nh = len(hv)
                S.dma("sp", yT[:, :, 0:n], self.Ys[:, :, t0:t0 + n].rearrange("k p t -> p k t"), r=[("Ys", c) for c in range(KC)], w=[("op_yT", ti_ % 2)])
                for mo in range(KC):
                    pbank = [(mo % 2) * 2 + 2, (mo % 2) * 2 + 3]
                    S.group("pe", [
                        (lambda e, k=k, hi=hi, o=o, sz=sz, mo=mo, pbank=pbank: e.matmul(
                            self.ps[pbank[hi]][:, 0:sz], lhsT=wo[:, k, mo * 128:(mo + 1) * 128], rhs=yT[:, k, o:o + sz], start=(k == 0), stop=(k == KC - 1)))
                        for k in range(KC) for hi, (o, sz) in enumerate(hv)],
                        r=[("op_wo", mo // 4), ("op_yT", ti_ % 2)], w=[("ps", p) for p in pbank[:nh]])
                    for hi, (o, sz) in enumerate(hv):
                        S.op("act", lambda e, hi=hi, o=o, sz=sz, mo=mo, pbank=pbank: e.activation(out=oT[:, mo, o:o + sz], in_=self.ps[pbank[hi]][:, 0:sz], func=AF.Copy),
                             r=[("ps", pbank[hi])], w=[(ok, mo)])
                    if pending is not None and 1 <= mo < 9:
                        pending(mo - 1)
                        if mo == 8:
                            pending = None
                pss = [self.ps[0], self.ps[1]]
                for mo in range(KC):
                    j = mo % 2
                    S.op("act", lambda e, mo=mo, j=j: e.activation(out=sq[0][:, j, 0:n], in_=oT[:, mo, 0:n], func=AF.Square), r=[(ok, mo)], w=[("sq", 0, j)])
                    S.group("pe", [
                        (lambda e, hi=hi, o=o, sz=sz, j=j, mo=mo: e.matmul(pss[hi][:, 0:sz], lhsT=self.ones_bf[:], rhs=sq[0][:, j, o:o + sz],
                                                                       start=(mo == 0), stop=(mo == KC - 1)))
                        for hi, (o, sz) in enumerate(hv)], r=[("sq", 0, j), "ones_bf"], w=[("ps", 0), ("ps", 1)])
                rk = [("rstd", id(rstd), 0), ("rstd", id(rstd), 1)]
                for hi, (o, sz) in enumerate(hv):
                    S.op("act", lambda e, hi=hi, o=o, sz=sz, rstd=rstd: e.activation(out=rstd[:, o:o + sz], in_=pss[hi][:, 0:sz], func=AF.Sqrt,
                                                                                    scale=1.0 / D, bias=self.eps_col[:, 0:1]),
                         r=[("ps", hi), "eps_col"], w=[rk[hi]])
                    S.op("dve", lambda e, o=o, sz=sz, rstd=rstd: e.reciprocal(rstd[:, o:o + sz], rstd[:, o:o + sz]), r=[rk[hi]], w=[rk[hi]])

                def post(g=None, t0=t0, n=n, oT=oT, ok=ok, rstd=rstd):
                    self.postnorm_apply(1, t0, n, oT, (xg, sq, rstd, tmp), ykey=ok, groups=(None if g is None else [g]))
                if ti_ + 1 < len(tiles):
                    pending = post
                else:
                    post()
            S.barrier()


CT_N = 8


def host_pack(inputs):
    f = lambda a: np.ascontiguousarray(np.asarray(a, dtype=np.float32))
    pv = np.zeros((DEPTH, 128, NPV), np.float32)

    def put(l, name, arr):
        o, n = PV[name]
        assert arr.shape == (128, n), (name, arr.shape)
        pv[l, :, o:o + n] = arr
    cm = lambda v: f(v).reshape(-1, 128).T
    for l in range(DEPTH):
        put(l, "b_mod", cm(inputs["b_mod"][l]))
        put(l, "npre", cm(inputs["norm_pre"][l].reshape(-1)))
        put(l, "npost", cm(inputs["norm_post"][l].reshape(-1)))
        def rg(v):
            v = f(v)
            o = np.zeros((128, 8), np.float32)
            for c in range(4):
                o[:, c] = v[128 * c:128 * c + 128]
            for g in range(4):
                o[0:64, 4 + g] = v[512 + 64 * g:512 + 64 * g + 64]
            return o
        put(l, "ssd_cw", np.concatenate([rg(inputs["ssd_conv_w"][l, j]) for j in range(4)], axis=1))
        put(l, "ssd_cb", rg(inputs["ssd_conv_b"][l]))
        put(l, "ssd_d", cm(np.repeat(f(inputs["ssd_d"][l]), 64)))
        put(l, "ssd_nw", cm(inputs["ssd_norm_w"][l]))
        col = np.zeros((128, 1), np.float32)
        col[0:16, 0] = f(inputs["ssd_dt_bias"][l]).reshape(-1)
        put(l, "ssd_dtb", col)
        col = np.zeros((128, 1), np.float32)
        col[0:16, 0] = f(inputs["ssd_a_log"][l]).reshape(-1)
        put(l, "ssd_alog", col)
        put(l, "lru_cw", np.concatenate([cm(inputs["lru_conv_w"][l, j]) for j in range(4)], axis=1))
        put(l, "lru_cb", cm(inputs["lru_conv_b"][l]))
        put(l, "lru_ba", np.concatenate([cm(inputs["lru_ba"][l, d].reshape(-1)) for d in range(2)], axis=1))
        put(l, "lru_bx", np.concatenate([cm(inputs["lru_bx"][l, d].reshape(-1)) for d in range(2)], axis=1))
        put(l, "lru_lam", np.concatenate([cm(inputs["lru_lambda"][l, d]) for d in range(2)], axis=1))
        put(l, "hg_lb", np.concatenate([cm(inputs["hgrn_lb_logits"][d, ll]) for d in range(2) for ll in range(2)], axis=1))
        put(l, "hg_nw", cm(inputs["hgrn_norm_w"][l]))
    lru_w = np.stack([f(inputs["lru_wa"]), f(inputs["lru_wx"])], axis=1)
    ctab = np.zeros((128, CT_N), np.float32)
    tpos = np.arange(NLAT)
    rows_, cols_ = tpos // 64, tpos % 64
    inv = 10000.0 ** (-np.arange(16, dtype=np.float64) / 16)
    rope = np.zeros((2, 64, NLAT), np.float32)
    ropeR = np.zeros((64, 64), np.float32)
    for j in range(64):
        pos = rows_ if j < 32 else cols_
        i = j % 32
        ang = (pos.astype(np.float32)[:, None] * inv.astype(np.float32)[None, :])[:, i % 16].astype(np.float32)
        rope[0, j] = np.cos(ang)
        rope[1, j] = np.sin(ang)
        if i < 16:
            ropeR[j + 16, j] = -1.0
        else:
            ropeR[j - 16, j] = 1.0
    rtab = np.zeros((4, 128, 5, 128), np.float32)
    ar = np.arange(128, dtype=np.float64)
    for h in range(4):
        lg = np.log1p(-(2.0 ** (-5.0 - h)))
        dist = np.abs(ar[:, None] - ar[None, :])
        rtab[h, :, 0, :] = np.exp(lg * dist) * (1.0 + np.eye(128))
        rtab[h, :, 1, :] = np.exp(lg * (ar + 1))[None, :]
        rtab[h, :, 2, :] = np.exp(lg * (128 - ar))[None, :]
        rtab[h, :, 3, :] = np.exp(lg * (127 - ar))[None, :]
        rtab[h, :, 4, :] = np.exp(lg * ar)[None, :]
    shared = {
        "w_mod": f(inputs["w_mod"]), "ffn_w1": f(inputs["ffn_w1"]), "ffn_w3": f(inputs["ffn_w3"]), "ffn_w2": f(inputs["ffn_w2"]),
        "w_in": f(inputs["w_in"]), "w_out": f(inputs["w_out"]), "pv": pv, "lru_w": np.ascontiguousarray(lru_w), "ctab": ctab,
        "rope": rope, "ropeR": ropeR, "rtab": rtab,
    }
    per_core = []
    x = f(inputs["x"]); ctx = f(inputs["ctx"]); c = f(inputs["c"]); c_ctx = f(inputs["c_ctx"])
    for b in range(x.shape[0]):
        xin = np.ascontiguousarray(np.concatenate([ctx[b], x[b]], axis=0))
        cT = np.ascontiguousarray(np.stack([c[b].reshape(KC, 128).T, c_ctx.reshape(KC, 128).T], axis=2))
        m = dict(shared)
        m["xin"] = xin
        m["cT"] = cT
        per_core.append(m)
    return per_core


_CACHE = {}


def kernel(**inputs):
    maps = host_pack(inputs)
    if "nc" not in _CACHE:
        _CACHE["nc"] = Builder().build()
    nc = _CACHE["nc"]
    res = run_bass_kernel_spmd(nc, maps, core_ids=list(range(len(maps))))
    out = np.stack([np.asarray(r["out"]) for r in res.results], axis=0)
    return out.astype(np.float32)
```

```python
import numpy as np
import ml_dtypes
import concourse.bass as bass
import concourse.mybir as mybir
from concourse.bass_utils import run_bass_kernel_spmd
from contextlib import ExitStack

F32 = mybir.dt.float32
BF16 = mybir.dt.bfloat16
AF = mybir.ActivationFunctionType
ALU = mybir.AluOpType

T = 2304
NCTX = 256
NLAT = 2048
D = 2048
KC = 16
DFF = 5632
MFF = 44
NB = 18
EPS = 1e-6
IN_COLS = 6416
DEPTH = 2

PV = {}
_off = 0
for _name, _n in [("b_mod", 144), ("npre", 48), ("npost", 48), ("ssd_cw", 32), ("ssd_cb", 8), ("ssd_d", 4),
                  ("ssd_nw", 4), ("ssd_dtb", 1), ("ssd_alog", 1), ("lru_cw", 16), ("lru_cb", 4), ("lru_ba", 8),
                  ("lru_bx", 8), ("lru_lam", 8), ("hg_lb", 16), ("hg_nw", 1)]:
    PV[_name] = (_off, _n)
    _off += _n
NPV = _off


class Sched:
    NDS = 12

    def __init__(self, nc, es):
        self.nc = nc
        self.eng = {"pe": nc.tensor, "dve": nc.vector, "act": nc.scalar, "pool": nc.gpsimd, "sp": nc.sync}
        self.sems = {}
        self.pcnt = {}
        for e in ["pe", "dve", "act", "pool"]:
            self.sems[("p", e)] = es.enter_context(nc.semaphore("prog_" + e))
            self.pcnt[e] = 0
        self.dslots = {}
        self.dqi = {}
        for q in ["sp", "pool", "act"]:
            self.dslots[q] = []
            for i in range(self.NDS):
                key = ("d", q, i)
                self.sems[key] = es.enter_context(nc.semaphore(f"dma_{q}_{i}"))
                self.dslots[q].append([key, 0])
            self.dqi[q] = 0
        self.seen = {e: {} for e in self.eng}
        self.res = {}
        self.n_ops = 0
        self.n_waits = 0

    def _deps(self, r, w):
        deps = {}

        def add(tok):
            if tok is None:
                return
            k, v = tok
            if deps.get(k, 0) < v:
                deps[k] = v
        for k in r:
            st = self.res.get(k)
            if st:
                add(st["w"])
        for k in w:
            st = self.res.get(k)
            if st:
                add(st["w"])
                for kk, vv in st["r"].items():
                    add((kk, vv))
        return deps

    def _wait(self, e, deps):
        eng = self.eng[e]
        for k, v in deps.items():
            if e == "pe" and k == ("p", "pe"):
                continue
            if self.seen[e].get(k, 0) >= v:
                continue
            eng.wait_ge(self.sems[k], v)
            self.seen[e][k] = v
            self.n_waits += 1

    def _commit(self, tok, r, w):
        k, v = tok
        for x in r:
            st = self.res.setdefault(x, {"w": None, "r": {}})
            if st["r"].get(k, 0) < v:
                st["r"][k] = v
        for x in w:
            self.res[x] = {"w": tok, "r": {}}

    def op(self, e, fn, r=(), w=()):
        return self.group(e, [fn], r, w)

    def group(self, e, fns, r=(), w=()):
        psr = [k for k in r if isinstance(k, tuple) and k[0] == "ps"]
        if psr:
            r = [k for k in r if k not in psr]
            w = list(w) + psr
        self._wait(e, self._deps(r, w))
        eng = self.eng[e]
        ins = None
        for fn in fns:
            ins = fn(eng)
        self.pcnt[e] += 1
        ins.then_inc(self.sems[("p", e)], 1)
        tok = (("p", e), self.pcnt[e])
        self._commit(tok, r, w)
        self.n_ops += len(fns)
        return tok

    def dma(self, q, out, in_, r=(), w=(), **kw):
        deps = self._deps(r, w)
        slot = self.dslots[q][self.dqi[q] % self.NDS]
        self.dqi[q] += 1
        if slot[1] > 0 and deps.get(slot[0], 0) < slot[1]:
            deps[slot[0]] = slot[1]
        self._wait(q, deps)
        ins = self.eng[q].dma_start(out=out, in_=in_, **kw)
        ins.then_inc(self.sems[slot[0]], 16)
        slot[1] += 16
        tok = (slot[0], slot[1])
        self._commit(tok, r, w)
        self.n_ops += 1
        return tok

    def barrier(self):
        allt = {}
        for e in ["pe", "dve", "act", "pool"]:
            if self.pcnt[e] > 0:
                allt[("p", e)] = self.pcnt[e]
        for q in self.dslots:
            for key, tot in self.dslots[q]:
                if tot > 0:
                    allt[key] = tot
        for e in self.eng:
            d = {k: v for k, v in allt.items() if k != ("p", e)}
            if e != "pe":
                if ("p", e) in allt:
                    d[("p", e)] = allt[("p", e)]
            self._wait(e, d)
        self.res = {}


def tiles_of(start, stop, step=768):
    out = []
    t = start
    while t < stop:
        n = min(step, stop - t)
        out.append((t, n))
        t += n
    return out


def halves(n, maxn=512):
    k = (n + maxn - 1) // maxn
    base = (n // 128) // k
    rem = (n // 128) % k
    out = []
    o = 0
    for i in range(k):
        sz = (base + (1 if i < rem else 0)) * 128
        out.append((o, sz))
        o += sz
    return out


def seg_ranges(t0, n):
    out = []
    if t0 < NCTX:
        c = min(NCTX, t0 + n) - t0
        out.append((0, c, 1))
        if c < n:
            out.append((c, n - c, 0))
    else:
        out.append((0, n, 0))
    return out


class Builder:
    def __init__(self, debug=None):
        self.debug = debug or {}
        self.nc = bass.Bass("TRN2", target_bir_lowering=False)
        self.es = ExitStack()
        nc = self.nc
        di = lambda name, shape, dt=F32: nc.dram_tensor(name, list(shape), dt, kind="ExternalInput").ap()
        self.xin = di("xin", [T, D])
        self.cT = di("cT", [128, KC, 2])
        self.w_mod = di("w_mod", [DEPTH, D, 9 * D])
        self.w1 = di("ffn_w1", [DEPTH, 2, D, DFF])
        self.w3 = di("ffn_w3", [DEPTH, 2, D, DFF])
        self.w2 = di("ffn_w2", [DEPTH, 2, DFF, D])
        self.w_in = di("w_in", [DEPTH, D, IN_COLS])
        self.w_out = di("w_out", [DEPTH, D, D])
        self.pv = di("pv", [DEPTH, 128, NPV])
        self.lru_w = di("lru_w", [DEPTH, 2, 2, 8, 64, 64])
        self.ctab = di("ctab", [128, CT_N])
        self.rope = di("rope", [2, 64, NLAT])
        self.ropeR = di("ropeR", [64, 64])
        self.rtab = di("rtab", [4, 128, 5, 128])
        self.out = nc.dram_tensor("out", [NLAT, D], F32, kind="ExternalOutput").ap()
        self.Xs = nc.dram_tensor("Xs", [KC, 128, T], F32, kind="Internal").ap()
        self.Ys = nc.dram_tensor("Ys", [KC, 128, T], BF16, kind="Internal").ap()
        self.Us = nc.dram_tensor("Us", [Builder.NG, 128, T], F32, kind="Internal").ap()
        self.dbg_out = None
        if self.debug.get("dump"):
            self.dbg_out = nc.dram_tensor("dbg", [KC, 128, T], F32, kind="ExternalOutput").ap()

    def build(self):
        nc, es = self.nc, self.es
        with es:
            self.S = S = Sched(nc, es)
            sb = lambda name, shape, dt=F32: es.enter_context(nc.sbuf_tensor(name, list(shape), dt))
            self.ps = [es.enter_context(nc.psum_tensor(f"ps{i}", [128, 512], F32)) for i in range(8)]
            self.ones_bf = sb("ones_bf", [128, 128], BF16)
            self.identf = sb("identf", [128, 128], F32)
            self.ident_bf = sb("ident_bf", [128, 128], BF16)
            self.pvt_l = [sb(f"pvt{l}", [128, NPV], F32) for l in range(DEPTH)]
            self.modT_l = [sb(f"modT{l}", [128, 144, 2], F32) for l in range(DEPTH)]
            self.Apre_l = [sb(f"Apre{l}", [128, 3, KC, 2], F32) for l in range(DEPTH)]
            self.Gpost_l = [sb(f"Gpost{l}", [128, 3, KC, 2], F32) for l in range(DEPTH)]
            self.sc_l = sb("mod_sc", [128, KC, 2], BF16)
            self.mod_q = []
            self.pump_wm = None
            self.pump_cnt = 0
            self.ctab_sb = sb("ctab_sb", [128, CT_N], F32)
            self.eps_col = sb("eps_col", [128, 1], F32)
            S.op("pool", lambda e: e.memset(self.eps_col[:], EPS), w=["eps_col"])
            self.one_col = sb("one_col", [128, 1], F32)
            S.op("pool", lambda e: e.memset(self.one_col[:], 1.0), w=["one_col"])
            S.op("pool", lambda e: e.memset(self.ones_bf[:], 1.0), w=["ones_bf"])
            S.op("pool", lambda e: e.memset(self.identf[:], 1.0), w=["identf"])
            S.op("pool", lambda e: e.affine_select(out=self.identf[:], in_=self.identf[:], pattern=[[-1, 128]],
                                                   compare_op=ALU.is_equal, fill=0.0, base=0, channel_multiplier=1),
                 r=["identf"], w=["identf"])
            S.op("dve", lambda e: e.tensor_copy(out=self.ident_bf[:], in_=self.identf[:]), r=["identf"], w=["ident_bf"])
            S.dma("sp", self.ctab_sb[:], self.ctab[:, :], w=["ctab_sb"])

            self.load_x()
            stop = self.debug.get("stop")
            self.mods_setup()
            for l in range(DEPTH):
                last = l == DEPTH - 1
                self.pvt, self.modT, self.Apre, self.Gpost = self.pvt_l[l], self.modT_l[l], self.Apre_l[l], self.Gpost_l[l]
                self.cur_l = l
                if l == 0:
                    self.mods_initial()
                if not self.debug.get("noffn"):
                    self.ffn(l, 0, 0, tiles_of(0, T))
                if stop == (l, "ffn1"):
                    break
                if not self.debug.get("nomix"):
                    self.mixer(l)
                if stop == (l, "mix"):
                    break
                t_lo = NCTX if last else 0
                self.outproj(l, tiles_of(t_lo, T))
                if stop == (l, "outproj"):
                    break
                self.ffn(l, 1, 2, tiles_of(t_lo, T))
                if stop == (l, "ffn2"):
                    break
            if self.dbg_out is not None:
                src = self.Ys if self.debug.get("dump") == "Ys" else self.Xs
                self.dump(src)
            self.store_out()
            S.barrier()
        return nc

    def un(self, name):
        self._uid = getattr(self, "_uid", 0) + 1
        return f"{name}_u{self._uid}"

    def load_x(self):
        nc, S, es = self.nc, self.S, self.es
        with ExitStack() as ls:
            xt = [ls.enter_context(nc.sbuf_tensor(self.un(f"lx_in{i}"), [128, D], F32)) for i in range(2)]
            xo = [ls.enter_context(nc.sbuf_tensor(self.un(f"lx_out{i}"), [128, KC, 128], F32)) for i in range(2)]
            for b in range(NB):
                bi = b % 2
                S.dma("sp", xt[bi][:], self.xin[b * 128:(b + 1) * 128, :], w=[("lx_in", bi)])
                for g in range(4):
                    pst = self.ps[(b * 4 + g) % 8]
                    S.group("pe", [
                        (lambda e, kc=kc, pst=pst, bi=bi: e.transpose(pst[:, (kc % 4) * 128:(kc % 4 + 1) * 128],
                                                                     xt[bi][:, kc * 128:(kc + 1) * 128], self.identf[:]))
                        for kc in range(g * 4, g * 4 + 4)], r=[("lx_in", bi), "identf"], w=[("ps", (b * 4 + g) % 8)])
                    eng = "dve" if g % 2 == 0 else "act"
                    if eng == "dve":
                        S.op("dve", lambda e, g=g, pst=pst, bi=bi: e.tensor_copy(
                            out=xo[bi][:, g * 4:(g + 1) * 4, :], in_=pst[:, :].rearrange("p (a b) -> p a b", a=4)),
                            r=[("ps", (b * 4 + g) % 8)], w=[("lx_out", bi, g)])
                    else:
                        S.op("act", lambda e, g=g, pst=pst, bi=bi: e.activation(
                            out=xo[bi][:, g * 4:(g + 1) * 4, :], in_=pst[:, :].rearrange("p (a b) -> p a b", a=4), func=AF.Copy),
                            r=[("ps", (b * 4 + g) % 8)], w=[("lx_out", bi, g)])
                S.dma("sp", self.Xs[:, :, b * 128:(b + 1) * 128].rearrange("k p t -> p k t"), xo[bi][:],
                      r=[("lx_out", bi, g) for g in range(4)], w=[("Xs", b)])
            S.barrier()

    def store_out(self):
        nc, S = self.nc, self.S
        with ExitStack() as ls:
            xi = [ls.enter_context(nc.sbuf_tensor(self.un(f"so_in{i}"), [128, KC, 128], F32)) for i in range(2)]
            xo = [ls.enter_context(nc.sbuf_tensor(self.un(f"so_out{i}"), [128, D], F32)) for i in range(2)]
            for b in range(2, NB):
                bi = b % 2
                S.dma("sp", xi[bi][:], self.Xs[:, :, b * 128:(b + 1) * 128].rearrange("k p t -> p k t"),
                      r=[("Xs", b), "Xs"], w=[("so_in", bi)])
                for g in range(4):
                    pi = (b * 4 + g) % 8
                    pst = self.ps[pi]
                    S.group("pe", [
                        (lambda e, kc=kc, pst=pst, bi=bi: e.transpose(pst[:, (kc % 4) * 128:(kc % 4 + 1) * 128],
                                                                     xi[bi][:, kc, :], self.identf[:]))
                        for kc in range(g * 4, g * 4 + 4)], r=[("so_in", bi), "identf"], w=[("ps", pi)])
                    if g % 2 == 0:
                        S.op("dve", lambda e, g=g, pst=pst, bi=bi: e.tensor_copy(out=xo[bi][:, g * 512:(g + 1) * 512], in_=pst[:, :]),
                             r=[("ps", pi)], w=[("so_out", bi, g)])
                    else:
                        S.op("act", lambda e, g=g, pst=pst, bi=bi: e.activation(out=xo[bi][:, g * 512:(g + 1) * 512], in_=pst[:, :], func=AF.Copy),
                             r=[("ps", pi)], w=[("so_out", bi, g)])
                S.dma("sp", self.out[(b - 2) * 128:(b - 1) * 128, :], xo[bi][:],
                      r=[("so_out", bi, g) for g in range(4)], w=[("out", b)])

    def dump(self, src):
        nc, S = self.nc, self.S
        S.barrier()
        with ExitStack() as ls:
            if src is self.Ys:
                tb = ls.enter_context(nc.sbuf_tensor(self.un("dump_b"), [128, T], BF16))
            tf = ls.enter_context(nc.sbuf_tensor(self.un("dump_f"), [128, T], F32))
            for kc in range(KC):
                if src is self.Ys:
                    S.dma("sp", tb[:], src[kc, :, :], w=["dump_b"])
                    S.op("dve", lambda e: e.tensor_copy(out=tf[:], in_=tb[:]), r=["dump_b"], w=["dump_f"])
                else:
                    S.dma("sp", tf[:], src[kc, :, :], w=["dump_f"])
                S.dma("sp", self.dbg_out[kc, :, :], tf[:], r=["dump_f"], w=[("dbg", kc)])
            S.barrier()

    def pvc(self, name, i=0, rows=128):
        o, n = PV[name]
        return self.pvt[0:rows, o + i:o + i + 1]

    def mods_setup(self):
        nc, S = self.nc, self.S
        for l in range(DEPTH):
            S.dma("sp", self.pvt_l[l][:], self.pv[l, :, :], w=["pvt"])
        with ExitStack() as ls:
            ct = ls.enter_context(nc.sbuf_tensor(self.un("lp_ct"), [128, KC, 2], F32))
            S.dma("sp", ct[:], self.cT[:, :, :], w=["lp_ct"])
            S.op("act", lambda e: e.activation(out=self.sc_l[:], in_=ct[:], func=AF.Silu), r=["lp_ct"], w=["mod_sc"])
            S.barrier()
        self.mod_q = [(l, s) for l in range(DEPTH) for s in range(72)]

    def mod_slab(self, l, s, wm, key):
        nc, S = self.nc, self.S
        wv = self.w_mod[l].rearrange("(kc p) n -> p kc n", p=128)
        S.dma("pool", wm[:], wv[:, :, s * 256:(s + 1) * 256], w=[key])
        pm = self.ps[7]
        S.group("pe", [
            (lambda e, kc=kc, m=m: e.matmul(pm[:, 2 * m:2 * m + 2], lhsT=wm[:, kc, m * 128:(m + 1) * 128], rhs=self.sc_l[:, kc, :],
                                            start=(kc == 0), stop=(kc == KC - 1)))
            for m in range(2) for kc in range(KC)], r=[key, "mod_sc"], w=[("ps", 7)])
        o, _ = PV["b_mod"]
        modT, pvt = self.modT_l[l], self.pvt_l[l]
        S.op("dve", lambda e: e.tensor_tensor(out=modT[:, 2 * s:2 * s + 2, :], in0=pm[:, 0:4].rearrange("p (a b) -> p a b", b=2),
                                              in1=pvt[:, o + 2 * s:o + 2 * s + 2].unsqueeze(2).to_broadcast([128, 2, 2]), op=ALU.add),
             r=[("ps", 7), "pvt"], w=["modT"])
        if s % 8 == 7:
            j = s // 8
            i = j // 3
            opre, _ = PV["npre"]
            opost, _ = PV["npost"]
            if j % 3 == 1:
                S.op("dve", lambda e: e.scalar_tensor_tensor(
                    out=self.Apre_l[l][:, i, :, :], in0=modT[:, j * 16:(j + 1) * 16, :], scalar=1.0,
                    in1=pvt[:, opre + 16 * i:opre + 16 * i + 16].unsqueeze(2).to_broadcast([128, 16, 2]),
                    op0=ALU.add, op1=ALU.mult), r=["modT", "pvt"], w=[("Apre", i)])
            elif j % 3 == 2:
                S.op("dve", lambda e: e.scalar_tensor_tensor(
                    out=self.Gpost_l[l][:, i, :, :], in0=modT[:, j * 16:(j + 1) * 16, :], scalar=(1.0 if i == 1 else 0.5),
                    in1=pvt[:, opost + 16 * i:opost + 16 * i + 16].unsqueeze(2).to_broadcast([128, 16, 2]),
                    op0=ALU.mult, op1=ALU.mult), r=["modT", "pvt"], w=[("Gpost", i)])

    def mods_initial(self):
        nc, S = self.nc, self.S
        with ExitStack() as ls:
            wm = [ls.enter_context(nc.sbuf_tensor(self.un(f"lp_wm{i}"), [128, KC, 256], BF16)) for i in range(4)]
            for k in range(40):
                l, s = self.mod_q.pop(0)
                self.mod_slab(l, s, wm[k % 4], ("lp_wm", k % 4))
            S.barrier()

    def pump_mods(self, n):
        for _ in range(n):
            if not self.mod_q or self.pump_wm is None:
                return
            l, s = self.mod_q.pop(0)
            k = self.pump_cnt % len(self.pump_wm)
            self.pump_cnt += 1
            self.mod_slab(l, s, self.pump_wm[k], ("pump_wm", k))

    def prenorm_h(self, i, t0, n, hT, ls_tiles):
        self.prenorm_stats(t0, n, ls_tiles)
        self.prenorm_apply(i, t0, n, hT, ls_tiles)

    def prenorm_stats(self, t0, n, ls_tiles):
        nc, S = self.nc, self.S
        xg, sq, rstd, tmp = ls_tiles
        hv = halves(n)
        pss = [self.ps[0], self.ps[1]]
        for g in range(8):
            bi = g % 2
            S.dma("sp", xg[bi][:, :, 0:n], self.Xs[2 * g:2 * g + 2, :, t0:t0 + n].rearrange("k p t -> p k t"),
                  r=["Xs"], w=[("xg", bi)])
            S.op("act", lambda e, bi=bi: e.activation(out=sq[bi][:, :, 0:n], in_=xg[bi][:, :, 0:n], func=AF.Square),
                 r=[("xg", bi)], w=[("sq", bi, 0), ("sq", bi, 1)])
            for j in range(2):
                kc = 2 * g + j
                S.group("pe", [
                    (lambda e, hi=hi, o=o, sz=sz, j=j, bi=bi, kc=kc: e.matmul(pss[hi][:, 0:sz], lhsT=self.ones_bf[:], rhs=sq[bi][:, j, o:o + sz],
                                                                           start=(kc == 0), stop=(kc == KC - 1)))
                    for hi, (o, sz) in enumerate(hv)], r=[("sq", bi, j), "ones_bf"], w=[("ps", 0), ("ps", 1)])
        for hi, (o, sz) in enumerate(hv):
            S.op("act", lambda e, hi=hi, o=o, sz=sz: e.activation(out=rstd[:, o:o + sz], in_=pss[hi][:, 0:sz], func=AF.Sqrt,
                                                                 scale=1.0 / D, bias=self.eps_col[:, 0:1]),
                 r=[("ps", hi), "eps_col"], w=[("rstd", id(rstd), hi)])
            S.op("dve", lambda e, o=o, sz=sz: e.reciprocal(rstd[:, o:o + sz], rstd[:, o:o + sz]), r=[("rstd", id(rstd), hi)], w=[("rstd", id(rstd), hi)])

    def prenorm_apply(self, i, t0, n, hT, ls_tiles, hkey="hT"):
        nc, S = self.nc, self.S
        xg, sq, rstd, tmp = ls_tiles
        for g in range(8):
            bi = g % 2
            S.dma("sp", xg[bi][:, :, 0:n], self.Xs[2 * g:2 * g + 2, :, t0:t0 + n].rearrange("k p t -> p k t"),
                  r=["Xs"], w=[("xg", bi)])
            for j in range(2):
                kc = 2 * g + j
                S.op("dve", lambda e, bi=bi, j=j: e.tensor_tensor(out=tmp[j][:, 0:n], in0=xg[bi][:, j, 0:n], in1=rstd[:, 0:n], op=ALU.mult),
                     r=[("xg", bi), ("rstd", id(rstd), 0), ("rstd", id(rstd), 1)], w=[("tmp", j)])
                for (o, sz, mi) in seg_ranges(t0, n):
                    S.op("act", lambda e, o=o, sz=sz, mi=mi, kc=kc, j=j: e.activation(
                        out=hT[:, kc, o:o + sz], in_=tmp[j][:, o:o + sz], func=AF.Identity,
                        scale=self.Apre[:, i, kc, mi:mi + 1], bias=self.modT[:, (3 * i) * 16 + kc, mi:mi + 1]),
                        r=[("tmp", j), ("Apre", i), "modT"], w=[(hkey, kc)])

    def postnorm_update(self, i, t0, n, yT, ls_tiles, ss_ps_ready, ykey="hT"):
        nc, S = self.nc, self.S
        xg, sq, rstd, tmp = ls_tiles
        hv = halves(n)
        pss = [self.ps[0], self.ps[1]]
        rk = [("rstd", id(rstd), 0), ("rstd", id(rstd), 1)]
        for hi, (o, sz) in enumerate(hv):
            S.op("act", lambda e, hi=hi, o=o, sz=sz: e.activation(out=rstd[:, o:o + sz], in_=pss[hi][:, 0:sz], func=AF.Sqrt,
                                                                 scale=1.0 / D, bias=self.eps_col[:, 0:1]),
                 r=[("ps", hi), "eps_col"], w=[rk[hi]])
            S.op("dve", lambda e, o=o, sz=sz: e.reciprocal(rstd[:, o:o + sz], rstd[:, o:o + sz]), r=[rk[hi]], w=[rk[hi]])
        for g in range(8):
            bi = g % 2
            S.dma("sp", xg[bi][:, :, 0:n], self.Xs[2 * g:2 * g + 2, :, t0:t0 + n].rearrange("k p t -> p k t"),
                  r=["Xs"], w=[("xg", bi)])
            for j in range(2):
                kc = 2 * g + j
                for (o, sz, mi) in seg_ranges(t0, n):
                    S.op("dve", lambda e, o=o, sz=sz, mi=mi, kc=kc, j=j: e.scalar_tensor_tensor(
                        out=tmp[j][:, o:o + sz], in0=yT[:, kc, o:o + sz], scalar=self.Gpost[:, i, kc, mi:mi + 1],
                        in1=rstd[:, o:o + sz], op0=ALU.mult, op1=ALU.mult),
                        r=[(ykey, kc), ("Gpost", i)] + rk, w=[("tmp", j)])
                S.op("dve", lambda e, bi=bi, j=j: e.tensor_tensor(out=xg[bi][:, j, 0:n], in0=xg[bi][:, j, 0:n], in1=tmp[j][:, 0:n], op=ALU.add),
                     r=[("tmp", j), ("xg", bi)], w=[("xg", bi)])
            S.dma("sp", self.Xs[2 * g:2 * g + 2, :, t0:t0 + n].rearrange("k p t -> p k t"), xg[bi][:, :, 0:n],
                  r=[("xg", bi)], w=["Xs"])

    def postnorm_apply(self, i, t0, n, yT, ls_tiles, ykey="hT", groups=None):
        nc, S = self.nc, self.S
        xg, sq, rstd, tmp = ls_tiles
        rk = [("rstd", id(rstd), 0), ("rstd", id(rstd), 1)]
        for g in (range(8) if groups is None else groups):
            bi = g % 2
            S.dma("sp", xg[bi][:, :, 0:n], self.Xs[2 * g:2 * g + 2, :, t0:t0 + n].rearrange("k p t -> p k t"),
                  r=["Xs"], w=[("xg", bi)])
            for j in range(2):
                kc = 2 * g + j
                for (o, sz, mi) in seg_ranges(t0, n):
                    S.op("dve", lambda e, o=o, sz=sz, mi=mi, kc=kc, j=j: e.scalar_tensor_tensor(
                        out=tmp[j][:, o:o + sz], in0=yT[:, kc, o:o + sz], scalar=self.Gpost[:, i, kc, mi:mi + 1],
                        in1=rstd[:, o:o + sz], op0=ALU.mult, op1=ALU.mult),
                        r=[(ykey, kc), ("Gpost", i)] + rk, w=[("tmp", j)])
                S.op("dve", lambda e, bi=bi, j=j: e.tensor_tensor(out=xg[bi][:, j, 0:n], in0=xg[bi][:, j, 0:n], in1=tmp[j][:, 0:n], op=ALU.add),
                     r=[("tmp", j), ("xg", bi)], w=[("xg", bi)])
            S.dma("sp", self.Xs[2 * g:2 * g + 2, :, t0:t0 + n].rearrange("k p t -> p k t"), xg[bi][:, :, 0:n],
                  r=[("xg", bi)], w=["Xs"])

    def ffn(self, l, fi, i, tiles):
        nc, S = self.nc, self.S
        TT = 768
        with ExitStack() as ls:
            sbl = lambda name, shape, dt=F32: ls.enter_context(nc.sbuf_tensor(self.un(name), list(shape), dt))
            hTs = [sbl(f"f_hT{k}", [128, KC, TT], BF16) for k in range(2)]
            gT = sbl("f_gT", [128, MFF, TT], BF16)
            w1s = [sbl(f"f_w1s{k}", [128, KC, 256], BF16) for k in range(2)]
            w3s = [sbl(f"f_w3s{k}", [128, KC, 256], BF16) for k in range(2)]
            w2s = [sbl(f"f_w2s{k}", [128, 11, 256], BF16) for k in range(4)]
            xg = [sbl(f"f_xg{k}", [128, 2, TT], F32) for k in range(2)]
            sq = [sbl(f"f_sq{k}", [128, 2, TT], BF16) for k in range(2)]
            rstd_a = sbl("f_rstda", [128, TT], F32)
            rstd_b = sbl("f_rstdb", [128, TT], F32)
            tmp = [sbl(f"f_tmp{k}", [128, TT], F32) for k in range(2)]
            sa = [sbl("f_sa0", [128, TT], BF16)]
            lt_pre = (xg, sq, rstd_a, tmp)
            lt_post = (xg, sq, rstd_b, tmp)
            w1v = self.w1[l, fi].rearrange("(kc p) n -> p kc n", p=128)
            w3v = self.w3[l, fi].rearrange("(kc p) n -> p kc n", p=128)
            w2v = self.w2[l, fi].rearrange("(kc p) n -> p kc n", p=128)
            hkeys = ["hTa", "hTb"]
            t0, n = tiles[0]
            self.prenorm_stats(t0, n, lt_pre)
            self.prenorm_apply(i, t0, n, hTs[0], lt_pre, hkey=hkeys[0])
            pending_post = None
            for ti_, (t0, n) in enumerate(tiles):
                hT = hTs[ti_ % 2]
                hk = hkeys[ti_ % 2]
                hv = halves(n)
                nh = len(hv)
                for s_ in range(22):
                    bi = s_ % 2
                    S.dma("pool", w1s[bi][:], w1v[:, :, s_ * 256:(s_ + 1) * 256], w=[("w1s", bi)])
                    S.dma("pool", w3s[bi][:], w3v[:, :, s_ * 256:(s_ + 1) * 256], w=[("w3s", bi)])
                    for m in range(2):
                        mc = 2 * s_ + m
                        pa = [(mc % 2) * 4 + 0, (mc % 2) * 4 + 1]
                        pb = [(mc % 2) * 4 + 2, (mc % 2) * 4 + 3]
                        S.group("pe", [
                            (lambda e, kc=kc, hi=hi, o=o, sz=sz, m=m, bi=bi, pa=pa: e.matmul(
                                self.ps[pa[hi]][:, 0:sz], lhsT=w1s[bi][:, kc, m * 128:(m + 1) * 128], rhs=hT[:, kc, o:o + sz],
                                start=(kc == 0), stop=(kc == KC - 1)))
                            for kc in range(KC) for hi, (o, sz) in enumerate(hv)],
                            r=[("w1s", bi)] + [(hk, kc) for kc in range(KC)], w=[("ps", p) for p in pa[:nh]])
                        S.group("pe", [
                            (lambda e, kc=kc, hi=hi, o=o, sz=sz, m=m, bi=bi, pb=pb: e.matmul(
                                self.ps[pb[hi]][:, 0:sz], lhsT=w3s[bi][:, kc, m * 128:(m + 1) * 128], rhs=hT[:, kc, o:o + sz],
                                start=(kc == 0), stop=(kc == KC - 1)))
                            for kc in range(KC) for hi, (o, sz) in enumerate(hv)],
                            r=[("w3s", bi)] + [(hk, kc) for kc in range(KC)], w=[("ps", p) for p in pb[:nh]])
                        sai = 0
                        for hi, (o, sz) in enumerate(hv):
                            S.op("act", lambda e, hi=hi, o=o, sz=sz, pa=pa, sai=sai: e.activation(
                                out=sa[sai][:, o:o + sz], in_=self.ps[pa[hi]][:, 0:sz], func=AF.Silu),
                                r=[("ps", pa[hi])], w=[("sa", sai, hi)])
                            S.op("dve", lambda e, hi=hi, o=o, sz=sz, pb=pb, sai=sai, mc=mc: e.tensor_tensor(
                                out=gT[:, mc, o:o + sz], in0=self.ps[pb[hi]][:, 0:sz], in1=sa[sai][:, o:o + sz], op=ALU.mult),
                                r=[("ps", pb[hi]), ("sa", sai, hi)], w=[("gT", mc)])
                    if pending_post is not None and 2 <= s_ < 10:
                        pending_post(s_ - 2)
                        if s_ == 9:
                            pending_post = None
                yT = hT
                has_next = ti_ + 1 < len(tiles)
                if has_next:
                    t0n, nn = tiles[ti_ + 1]
                    hvn = halves(nn)
                    hTn = hTs[(ti_ + 1) % 2]
                    hkn = hkeys[(ti_ + 1) % 2]
                PB = [(2, 3), (4, 5), (6, 7)]
                pss = [self.ps[0], self.ps[1]]
                rka = [("rstd", id(rstd_a), 0), ("rstd", id(rstd_a), 1)]

                def pre_stats_group(g):
                    bi = g % 2
                    S.dma("sp", xg[bi][:, :, 0:nn], self.Xs[2 * g:2 * g + 2, :, t0n:t0n + nn].rearrange("k p t -> p k t"),
                          r=["Xs"], w=[("xg", bi)])
                    S.op("act", lambda e, bi=bi: e.activation(out=sq[bi][:, :, 0:nn], in_=xg[bi][:, :, 0:nn], func=AF.Square),
                         r=[("xg", bi)], w=[("sq", bi, 0), ("sq", bi, 1)])
                    for j in range(2):
                        kc = 2 * g + j
                        S.group("pe", [
                            (lambda e, hi=hi, o=o, sz=sz, j=j, bi=bi, kc=kc: e.matmul(pss[hi][:, 0:sz], lhsT=self.ones_bf[:], rhs=sq[bi][:, j, o:o + sz],
                                                                                   start=(kc == 0), stop=(kc == KC - 1)))
                            for hi, (o, sz) in enumerate(hvn)], r=[("sq", bi, j), "ones_bf"], w=[("ps", 0), ("ps", 1)])

                def pre_rstd():
                    for hi, (o, sz) in enumerate(hvn):
                        S.op("act", lambda e, hi=hi, o=o, sz=sz: e.activation(out=rstd_a[:, o:o + sz], in_=pss[hi][:, 0:sz], func=AF.Sqrt,
                                                                             scale=1.0 / D, bias=self.eps_col[:, 0:1]),
                             r=[("ps", hi), "eps_col"], w=[rka[hi]])
                        S.op("dve", lambda e, o=o, sz=sz: e.reciprocal(rstd_a[:, o:o + sz], rstd_a[:, o:o + sz]), r=[rka[hi]], w=[rka[hi]])

                def apply_group(g):
                    bi = g % 2
                    S.dma("sp", xg[bi][:, :, 0:nn], self.Xs[2 * g:2 * g + 2, :, t0n:t0n + nn].rearrange("k p t -> p k t"),
                          r=["Xs"], w=[("xg", bi)])
                    for j in range(2):
                        kc = 2 * g + j
                        S.op("dve", lambda e, bi=bi, j=j: e.tensor_tensor(out=tmp[j][:, 0:nn], in0=xg[bi][:, j, 0:nn], in1=rstd_a[:, 0:nn], op=ALU.mult),
                             r=[("xg", bi)] + rka, w=[("tmp", j)])
                        for (o, sz, mi) in seg_ranges(t0n, nn):
                            S.op("act", lambda e, o=o, sz=sz, mi=mi, kc=kc, j=j: e.activation(
                                out=hTn[:, kc, o:o + sz], in_=tmp[j][:, o:o + sz], func=AF.Identity,
                                scale=self.Apre[:, i, kc, mi:mi + 1], bias=self.modT[:, (3 * i) * 16 + kc, mi:mi + 1]),
                                r=[("tmp", j), ("Apre", i), "modT"], w=[(hkn, kc)])

                def post_stats(mo):
                    j = mo % 2
                    S.op("act", lambda e, mo=mo, j=j: e.activation(out=sq[0][:, j, 0:n], in_=yT[:, mo, 0:n], func=AF.Square),
                         r=[(hk, mo)], w=[("sq", 0, j)])
                    S.group("pe", [
                        (lambda e, hi=hi, o=o, sz=sz, j=j, mo=mo: e.matmul(pss[hi][:, 0:sz], lhsT=self.ones_bf[:], rhs=sq[0][:, j, o:o + sz],
                                                                       start=(mo == 0), stop=(mo == KC - 1)))
                        for hi, (o, sz) in enumerate(hv)], r=[("sq", 0, j), "ones_bf"], w=[("ps", 0), ("ps", 1)])

                for pr in range(8):
                    pbank = [list(PB[(2 * pr) % 3]), list(PB[(2 * pr + 1) % 3])]
                    for hf in range(4):
                        S.dma("pool", w2s[hf][:], w2v[:, hf * 11:(hf + 1) * 11, pr * 256:(pr + 1) * 256], w=[("w2s", hf)])
                        for m in range(2):
                            S.group("pe", [
                                (lambda e, k=k, hi=hi, o=o, sz=sz, m=m, hf=hf, pbank=pbank: e.matmul(
                                    self.ps[pbank[m][hi]][:, 0:sz], lhsT=w2s[hf][:, k, m * 128:(m + 1) * 128], rhs=gT[:, hf * 11 + k, o:o + sz],
                                    start=(hf == 0 and k == 0), stop=(hf == 3 and k == 10)))
                                for k in range(11) for hi, (o, sz) in enumerate(hv)],
                                r=[("w2s", hf)] + [("gT", hf * 11 + k) for k in range(11)], w=[("ps", p) for p in pbank[m][:nh]])
                    for m in range(2):
                        mo = 2 * pr + m
                        for hi, (o, sz) in enumerate(hv):
                            S.op("act", lambda e, hi=hi, o=o, sz=sz, mo=mo, m=m, pbank=pbank: e.activation(
                                out=yT[:, mo, o:o + sz], in_=self.ps[pbank[m][hi]][:, 0:sz], func=AF.Copy),
                                r=[("ps", pbank[m][hi])], w=[(hk, mo)])
                    if has_next:
                        if pr < 4:
                            pre_stats_group(2 * pr)
                            pre_stats_group(2 * pr + 1)
                            if pr == 3:
                                pre_rstd()
                        else:
                            apply_group(2 * (pr - 4))
                            apply_group(2 * (pr - 4) + 1)
                    if pr >= 4:
                        for mo in range(4 * (pr - 4), 4 * (pr - 4) + 4):
                            post_stats(mo)
                hv_ = hv
                rk = [("rstd", id(rstd_b), 0), ("rstd", id(rstd_b), 1)]

                def post(t0=t0, n=n, yT=yT, hk=hk):
                    self.postnorm_update(i, t0, n, yT, lt_post, None, ykey=hk)
                if ti_ + 1 < len(tiles):
                    for hi, (o, sz) in enumerate(hv):
                        S.op("act", lambda e, hi=hi, o=o, sz=sz: e.activation(out=rstd_b[:, o:o + sz], in_=pss[hi][:, 0:sz], func=AF.Sqrt,
                                                                             scale=1.0 / D, bias=self.eps_col[:, 0:1]),
                             r=[("ps", hi), "eps_col"], w=[rk[hi]])
                        S.op("dve", lambda e, o=o, sz=sz: e.reciprocal(rstd_b[:, o:o + sz], rstd_b[:, o:o + sz]), r=[rk[hi]], w=[rk[hi]])

                    def post(g, t0=t0, n=n, yT=yT, hk=hk):
                        self.postnorm_apply(i, t0, n, yT, lt_post, ykey=hk, groups=[g])
                    pending_post = post
                else:
                    post()
            S.barrier()

    GROUPS = ([("ssd_z", 128 * c, 128) for c in range(4)] + [("ssd_x", 512 + 128 * c, 128) for c in range(4)]
              + [("ssd_B", 1024 + 64 * g, 64) for g in range(2)] + [("ssd_C", 1152 + 64 * g, 64) for g in range(2)] + [("ssd_dt", 1280, 16)]
              + [("lru_x", 1296 + 128 * c, 128) for c in range(4)] + [("lru_g", 1808 + 128 * c, 128) for c in range(4)]
              + [("hg_q", 2320 + 128 * c, 128) for c in range(4)] + [("hg_ff", 2832 + 128 * c, 128) for c in range(4)]
              + [("hg_fb", 3344 + 128 * c, 128) for c in range(4)] + [("hg_i", 3856 + 128 * c, 128) for c in range(4)]
              + [("hg_g", 4368 + 128 * c, 128) for c in range(4)]
              + [("ret_q", 4880 + 64 * h, 64) for h in range(4)] + [("ret_k", 5136 + 64 * h, 64) for h in range(4)]
              + [("ret_v", 5392 + 128 * c, 128) for c in range(4)] + [("ret_g", 5904 + 128 * c, 128) for c in range(4)])
    NG = len(GROUPS)
    GIDX = {}
    for _i, (_n, _c0, _nc) in enumerate(GROUPS):
        GIDX.setdefault(_n, []).append(_i)
    TG = [(0, 512), (512, 512), (1024, 512), (1536, 512), (2048, 256)]

    def mixer(self, l):
        nc, S = self.nc, self.S
        self.mixer_inproj(l)
        which = self.debug.get("mixers", ["ssd", "lru", "hg", "ret"])
        if "lru" in which:
            self.mix_lru(l)
        if "hg" in which:
            self.mix_hgrn(l)
        if "ret" in which:
            self.mix_ret(l)
        if "ssd" in which:
            self.mix_ssd(l)

    def mixer_inproj(self, l):
        nc, S = self.nc, self.S
        with ExitStack() as ls:
            sbl = lambda name, shape, dt=F32: ls.enter_context(nc.sbuf_tensor(self.un(name), list(shape), dt))
            hT = sbl("mi_hT", [128, KC, T], BF16)
            xg = [sbl(f"mi_xg{k}", [128, 2, 768], F32) for k in range(2)]
            sq = [sbl(f"mi_sq{k}", [128, 2, 768], BF16) for k in range(2)]
            rstd = sbl("mi_rstd", [128, 768], F32)
            tmp = [sbl(f"mi_tmp{k}", [128, 768], F32) for k in range(2)]
            ws = [sbl(f"mi_ws{k}", [128, KC, 512], BF16) for k in range(2)]
            stage = [sbl(f"mi_st{k}", [128, T], F32) for k in range(2)]
            self.pump_wm = [sbl(f"mi_pump{k}", [128, KC, 256], BF16) for k in range(4)]
            self.pump_cnt = 0
            n_need = len([x for x in self.mod_q if x[0] in (l, l + 1)])
            per_group = (n_need + len(self.GROUPS) - 1) // len(self.GROUPS)
            for (t0, n) in tiles_of(0, T):
                self.prenorm_h(1, t0, n, hT[:, :, t0:t0 + n], (xg, sq, rstd, tmp))
            wv = self.w_in[l].rearrange("(kc p) n -> p kc n", p=128)
            slabs = []
            cur = []
            for gi, (name, c0, ncol) in enumerate(self.GROUPS):
                if cur and (cur[0][1] != name or sum(x[3] for x in cur) + ncol > 512):
                    slabs.append(cur)
                    cur = []
                cur.append((gi, name, c0, ncol))
            slabs.append(cur)
            cnt = 0
            for si, slab in enumerate(slabs):
                bi = si % 2
                c0 = slab[0][2]
                tot = sum(x[3] for x in slab)
                S.dma("pool", ws[bi][:, :, 0:tot], wv[:, :, c0:c0 + tot], w=[("mi_ws", bi)])
                for (gi, name, gc0, ncol) in slab:
                    lo = gc0 - c0
                    sti = cnt % 2
                    for ti, (to, tn) in enumerate(self.TG):
                        pi = cnt * 5 + ti
                        pb = pi % 4
                        S.group("pe", [
                            (lambda e, kc=kc, pb=pb, lo=lo, ncol=ncol, to=to, tn=tn, bi=bi: e.matmul(
                                self.ps[pb][0:ncol, 0:tn], lhsT=ws[bi][:, kc, lo:lo + ncol], rhs=hT[:, kc, to:to + tn],
                                start=(kc == 0), stop=(kc == KC - 1)))
                            for kc in range(KC)], r=[("mi_ws", bi)] + [("hT", kc) for kc in range(KC)], w=[("ps", pb)])
                        if pi % 2 == 0:
                            S.op("act", lambda e, pb=pb, ncol=ncol, to=to, tn=tn, sti=sti: e.activation(
                                out=stage[sti][0:ncol, to:to + tn], in_=self.ps[pb][0:ncol, 0:tn], func=AF.Copy),
                                r=[("ps", pb)], w=[("mi_st", sti, ti)])
                        else:
                            S.op("dve", lambda e, pb=pb, ncol=ncol, to=to, tn=tn, sti=sti: e.tensor_copy(
                                out=stage[sti][0:ncol, to:to + tn], in_=self.ps[pb][0:ncol, 0:tn]),
                                r=[("ps", pb)], w=[("mi_st", sti, ti)])
                    S.dma("sp", self.Us[gi, 0:ncol, :], stage[sti][0:ncol, :], r=[("mi_st", sti, ti) for ti in range(5)], w=[("Us", gi)])
                    cnt += 1
                    self.pump_mods(per_group)
            self.pump_mods(len([x for x in self.mod_q if x[0] in (l, l + 1)]))
            S.barrier()
            self.pump_wm = None

    def conv_row(self, eng, out, x, wcols, bcol, key_out, key_x, extra_r=()):
        S = self.S
        r = [key_x] + list(extra_r)
        S.op(eng, lambda e: e.tensor_scalar(out=out, in0=x, scalar1=wcols[1], scalar2=bcol, op0=ALU.mult, op1=ALU.add), r=r, w=[key_out])
        for (s0, s1) in [(0, NCTX), (NCTX, T)]:
            S.op(eng, lambda e, s0=s0, s1=s1: e.scalar_tensor_tensor(out=out[:, s0 + 1:s1], in0=x[:, s0:s1 - 1], scalar=wcols[0],
                                                                     in1=out[:, s0 + 1:s1], op0=ALU.mult, op1=ALU.add), r=r + [key_out], w=[key_out])
            S.op(eng, lambda e, s0=s0, s1=s1: e.scalar_tensor_tensor(out=out[:, s0:s1 - 1], in0=x[:, s0 + 1:s1], scalar=wcols[2],
                                                                     in1=out[:, s0:s1 - 1], op0=ALU.mult, op1=ALU.add), r=r + [key_out], w=[key_out])
            S.op(eng, lambda e, s0=s0, s1=s1: e.scalar_tensor_tensor(out=out[:, s0:s1 - 2], in0=x[:, s0 + 2:s1], scalar=wcols[3],
                                                                     in1=out[:, s0:s1 - 2], op0=ALU.mult, op1=ALU.add), r=r + [key_out], w=[key_out])

    def mix_lru(self, l):
        nc, S = self.nc, self.S
        with ExitStack() as ls:
            sbl = lambda name, shape, dt=F32: ls.enter_context(nc.sbuf_tensor(self.un(name), list(shape), dt))
            ux = sbl("lr_ux", [128, T]); ug = sbl("lr_ug", [128, T]); xb = sbl("lr_xb", [128, T])
            xbb = sbl("lr_xbb", [128, T], BF16)
            rr_ = [sbl(f"lr_r{d}", [128, T]) for d in range(2)]; ii_ = [sbl(f"lr_i{d}", [128, T]) for d in range(2)]
            aa_ = [sbl(f"lr_a{d}", [128, T]) for d in range(2)]; bb_ = [sbl(f"lr_b{d}", [128, T]) for d in range(2)]
            hf = sbl("lr_hf", [128, T]); hb = sbl("lr_hb", [128, T]); t1 = sbl("lr_t1", [128, T])
            yb = sbl("lr_yb", [128, T], BF16)
            wbd = [[sbl(f"lr_w{g}{d}", [128, 128], BF16) for d in range(2)] for g in range(2)]
            c8 = sbl("lr_c8", [128, 8])
            olam, _ = PV["lru_lam"]
            S.op("act", lambda e: e.activation(out=c8[:], in_=self.pvt[:, olam:olam + 8], func=AF.Exp, scale=-1.0), r=["pvt"], w=["lr_c8"])
            S.op("act", lambda e: e.activation(out=c8[:], in_=c8[:], func=AF.Ln, bias=self.one_col[:, 0:1]), r=["lr_c8", "one_col"], w=["lr_c8"])
            S.op("dve", lambda e: e.tensor_scalar(out=c8[:], in0=c8[:], scalar1=-8.0, scalar2=None, op0=ALU.mult), r=["lr_c8"], w=["lr_c8"])
            ocw, _ = PV["lru_cw"]
            for c in range(4):
                gx = self.GIDX["lru_x"][c]; gg = self.GIDX["lru_g"][c]
                S.dma("sp", ux[:], self.Us[gx, :, :], r=[("Us", gx)], w=["lr_ux"])
                S.dma("sp", ug[:], self.Us[gg, :, :], r=[("Us", gg)], w=["lr_ug"])
                for g in range(2):
                    for d in range(2):
                        S.op("pool", lambda e, g=g, d=d: e.memset(wbd[g][d][:], 0.0), w=[("lr_w", g, d)])
                        for hh in range(2):
                            S.dma("pool", wbd[g][d][64 * hh:64 * hh + 64, 64 * hh:64 * hh + 64], self.lru_w[l, g, d, 2 * c + hh, :, :],
                                  r=[("lr_w", g, d)], w=[("lr_w", g, d, hh)])
                wc = [self.pvt[:, ocw + 4 * j + c:ocw + 4 * j + c + 1] for j in range(4)]
                self.conv_row("dve", xb[:], ux[:], wc, self.pvc("lru_cb", c), "lr_xb", "lr_ux", ["pvt"])
                S.op("act", lambda e: e.activation(out=xbb[:], in_=xb[:], func=AF.Copy), r=["lr_xb"], w=["lr_xbb"])
                S.op("act", lambda e: e.activation(out=t1[:], in_=ug[:], func=AF.Square), r=["lr_ug"], w=["lr_t1"])
                S.op("pool", lambda e: e.tensor_scalar(out=t1[:], in0=t1[:], scalar1=0.044715, scalar2=1.0, op0=ALU.mult, op1=ALU.add), r=["lr_t1"], w=["lr_t1"])
                S.op("pool", lambda e: e.tensor_tensor(out=t1[:], in0=t1[:], in1=ug[:], op=ALU.mult), r=["lr_t1", "lr_ug"], w=["lr_t1"])
                S.op("act", lambda e: e.activation(out=t1[:], in_=t1[:], func=AF.Sigmoid, scale=1.5957691216057308), r=["lr_t1"], w=["lr_t1"])
                S.op("pool", lambda e: e.tensor_tensor(out=ug[:], in0=t1[:], in1=ug[:], op=ALU.mult), r=["lr_t1", "lr_ug"], w=["lr_ug"])
                for d in range(2):
                    rr, ii, aa, bb = rr_[d], ii_[d], aa_[d], bb_[d]
                    kr, ki, ka, kb_ = ("lr_r", d), ("lr_i", d), ("lr_a", d), ("lr_b", d)
                    for g, (dst, bname, key) in enumerate([(rr, "lru_ba", kr), (ii, "lru_bx", ki)]):
                        for ti, (to, tn) in enumerate(self.TG):
                            pb = (g * 5 + ti) % 4
                            S.op("pe", lambda e, g=g, d=d, pb=pb, to=to, tn=tn: e.matmul(self.ps[pb][:, 0:tn], lhsT=wbd[g][d][:], rhs=xbb[:, to:to + tn],
                                                                                        start=True, stop=True),
                                 r=[("lr_w", g, d), ("lr_w", g, d, 0), ("lr_w", g, d, 1), "lr_xbb"], w=[("ps", pb)])
                            S.op("act", lambda e, dst=dst, bname=bname, d=d, pb=pb, to=to, tn=tn: e.activation(
                                out=dst[:, to:to + tn], in_=self.ps[pb][:, 0:tn], func=AF.Sigmoid, bias=self.pvc(bname, 4 * d + c)),
                                r=[("ps", pb), "pvt"], w=[key])
                    S.op("act", lambda e, d=d, rr=rr, ii=ii, aa=aa, bb=bb: e.activation(out=aa[:], in_=rr[:], func=AF.Exp, scale=c8[:, 4 * d + c:4 * d + c + 1]),
                         r=[kr, "lr_c8"], w=[ka])
                    S.op("dve", lambda e, rr=rr, ii=ii, aa=aa, bb=bb: e.tensor_tensor(out=bb[:], in0=aa[:], in1=aa[:], op=ALU.mult), r=[ka], w=[kb_])
                    S.op("dve", lambda e, rr=rr, ii=ii, aa=aa, bb=bb: e.tensor_scalar(out=bb[:], in0=bb[:], scalar1=-1.0, scalar2=1.0, op0=ALU.mult, op1=ALU.add), r=[kb_], w=[kb_])
                    S.op("dve", lambda e, rr=rr, ii=ii, aa=aa, bb=bb: e.tensor_scalar(out=bb[:], in0=bb[:], scalar1=1e-12, scalar2=None, op0=ALU.max), r=[kb_], w=[kb_])
                    S.op("act", lambda e, rr=rr, ii=ii, aa=aa, bb=bb: e.activation(out=bb[:], in_=bb[:], func=AF.Sqrt), r=[kb_], w=[kb_])
                    S.op("pool", lambda e, rr=rr, ii=ii, aa=aa, bb=bb: e.tensor_tensor(out=ii[:], in0=ii[:], in1=xb[:], op=ALU.mult), r=[ki, "lr_xb"], w=[ki])
                    S.op("dve", lambda e, rr=rr, ii=ii, aa=aa, bb=bb: e.tensor_tensor(out=bb[:], in0=bb[:], in1=ii[:], op=ALU.mult), r=[kb_, ki], w=[kb_])
                    if d == 0:
                        S.op("dve", lambda e, rr=rr, ii=ii, aa=aa, bb=bb: e.tensor_tensor_scan(out=hf[:], data0=aa[:], data1=bb[:], initial=0.0, op0=ALU.mult, op1=ALU.add),
                             r=[ka, kb_], w=["lr_hf"])
                    else:
                        S.op("dve", lambda e, rr=rr, ii=ii, aa=aa, bb=bb: e.tensor_tensor_scan(out=hb[:, 0:NCTX][:, ::-1], data0=aa[:, 0:NCTX][:, ::-1], data1=bb[:, 0:NCTX][:, ::-1],
                                                                   initial=0.0, op0=ALU.mult, op1=ALU.add), r=[ka, kb_], w=["lr_hb"])
                        S.op("dve", lambda e, rr=rr, ii=ii, aa=aa, bb=bb: e.tensor_tensor_scan(out=hb[:, NCTX:T][:, ::-1], data0=aa[:, NCTX:T][:, ::-1], data1=bb[:, NCTX:T][:, ::-1],
                                                                   initial=hb[:, 0:1], op0=ALU.mult, op1=ALU.add), r=[ka, kb_, "lr_hb"], w=["lr_hb"])
                S.op("dve", lambda e: e.tensor_tensor(out=hf[:], in0=hf[:], in1=hb[:], op=ALU.add), r=["lr_hf", "lr_hb"], w=["lr_hf"])
                S.op("dve", lambda e: e.tensor_tensor(out=yb[:], in0=hf[:], in1=ug[:], op=ALU.mult), r=["lr_hf", "lr_ug"], w=["lr_yb"])
                S.dma("sp", self.Ys[4 + c, :, :], yb[:], r=["lr_yb"], w=[("Ys", 4 + c)])
                self.pump_mods(9)
            S.barrier()

    def rms_rows(self, rows_sq, n_chunks, width, rstd, key_sq, key_rstd, pbase=0):
        S = self.S
        for ti, (to, tn) in enumerate(self.TG):
            pb = pbase + ti % 2
            S.group("pe", [
                (lambda e, ci=ci, pb=pb, to=to, tn=tn: e.matmul(self.ps[pb][:, 0:tn], lhsT=self.ones_bf[:], rhs=rows_sq[ci][:, to:to + tn],
                                                             start=(ci == 0), stop=(ci == n_chunks - 1)))
                for ci in range(n_chunks)], r=list(key_sq) + ["ones_bf"], w=[("ps", pb)])
            S.op("act", lambda e, pb=pb, to=to, tn=tn: e.activation(out=rstd[:, to:to + tn], in_=self.ps[pb][:, 0:tn], func=AF.Sqrt,
                                                                  scale=1.0 / width, bias=self.eps_col[:, 0:1]),
                 r=[("ps", pb), "eps_col"], w=[key_rstd])
        S.op("dve", lambda e: e.reciprocal(rstd[:, :], rstd[:, :]), r=[key_rstd], w=[key_rstd])

    def mix_hgrn(self, l):
        nc, S = self.nc, self.S
        SUB = 32
        NSC = T // SUB
        orders = {0: list(range(NB)), 1: [1, 0] + list(range(NB - 1, 1, -1))}
        with ExitStack() as ls:
            sbl = lambda name, shape, dt=F32: ls.enter_context(nc.sbuf_tensor(self.un(name), list(shape), dt))
            uin = sbl("hg_uin", [128, T])
            q = sbl("hg_q", [128, T]); sgate = sbl("hg_sg", [128, T]); vbf = sbl("hg_vbf", [128, T], BF16)
            v_tok = sbl("hg_vtok", [128, NB, 128], BF16)
            maskS = [sbl(f"hg_mask{d}", [128, T], BF16) for d in range(2)]
            bdm = [sbl(f"hg_bdm{d}", [128, 128], F32) for d in range(2)]
            cum = [sbl(f"hg_cum{d}", [128, T]) for d in range(2)]
            kk = [sbl(f"hg_kk{d}", [128, T]) for d in range(2)]
            tmpr = [sbl(f"hg_tmp{d}", [128, T]) for d in range(2)]
            qd = [sbl(f"hg_qd{d}", [128, T], BF16) for d in range(2)]
            kd = [sbl(f"hg_kd{d}", [128, T], BF16) for d in range(2)]
            kl = [sbl(f"hg_kl{d}", [128, T], BF16) for d in range(2)]
            kl_tok = [sbl(f"hg_kltok{d}", [128, NB, 128], BF16) for d in range(2)]
            klm = [sbl(f"hg_klm{d}", [128, 4, 128], BF16) for d in range(2)]
            submask = sbl("hg_submask", [128, 4], BF16)
            S.op("pool", lambda e: e.memset(submask[:], 0.0), w=["hg_submask"])
            for c in range(4):
                S.op("pool", lambda e, c=c: e.memset(submask[32 * c:32 * c + 32, c:c + 1], 1.0), r=["hg_submask"], w=["hg_submask"])
            gdec = [sbl(f"hg_gdec{d}", [128, NSC]) for d in range(2)]
            orow = [sbl(f"hg_o{d}", [128, T]) for d in range(2)]
            Sst = [[sbl(f"hg_S{d}{k}", [128, 128]) for k in range(2)] for d in range(2)]
            Sbf = [[sbl(f"hg_Sbf{d}{k}", [128, 128], BF16) for k in range(2)] for d in range(2)]
            PT = [sbl(f"hg_PT{d}", [128, 128], BF16) for d in range(2)]
            lbc = sbl("hg_lbc", [128, 8]); omlb = sbl("hg_omlb", [128, 8]); nomlb = sbl("hg_nomlb", [128, 8])
            for d in range(2):
                S.op("pool", lambda e, d=d: e.memset(maskS[d][:], 1.0), w=[("hg_mask", d)])
                off = 0 if d == 0 else SUB - 1
                S.op("pool", lambda e, d=d, off=off: e.memset(maskS[d][:, off::SUB], 0.0), r=[("hg_mask", d)], w=[("hg_mask", d)])
                S.op("pool", lambda e, d=d: e.memset(bdm[d][:], 1.0), w=[("hg_bdm", d)])
                sgn = 1 if d == 0 else -1
                S.op("pool", lambda e, d=d, sgn=sgn: e.affine_select(out=bdm[d][:], in_=bdm[d][:], pattern=[[sgn, 128]], compare_op=ALU.is_ge,
                                                                    fill=0.0, base=0, channel_multiplier=-sgn), r=[("hg_bdm", d)], w=[("hg_bdm", d)])
                for c in range(4):
                    if d == 0:
                        S.op("pool", lambda e, d=d, c=c: e.affine_select(out=bdm[d][:, SUB * c:SUB * c + SUB], in_=bdm[d][:, SUB * c:SUB * c + SUB],
                                                                        pattern=[[0, SUB]], compare_op=ALU.is_ge, fill=0.0, base=-SUB * c, channel_multiplier=1),
                             r=[("hg_bdm", d)], w=[("hg_bdm", d)])
                    else:
                        S.op("pool", lambda e, d=d, c=c: e.affine_select(out=bdm[d][:, SUB * c:SUB * c + SUB], in_=bdm[d][:, SUB * c:SUB * c + SUB],
                                                                        pattern=[[0, SUB]], compare_op=ALU.is_ge, fill=0.0, base=SUB * c + SUB - 1, channel_multiplier=-1),
                             r=[("hg_bdm", d)], w=[("hg_bdm", d)])
            olb, _ = PV["hg_lb"]
            if l == 0:
                S.op("dve", lambda e: e.memset(lbc[:], 0.0), w=["hg_lbc"])
            else:
                for d in range(2):
                    S.op("dve", lambda e, d=d: e.tensor_tensor(out=lbc[:, 4 * d:4 * d + 4], in0=self.pvt[:, olb + (2 * d + 1) * 4:olb + (2 * d + 1) * 4 + 4],
                                                               in1=self.pvt[:, olb + (2 * d) * 4:olb + (2 * d) * 4 + 4], op=ALU.subtract), r=["pvt"], w=["hg_lbc"])
                S.op("act", lambda e: e.activation(out=lbc[:], in_=lbc[:], func=AF.Sigmoid), r=["hg_lbc"], w=["hg_lbc"])
            S.op("dve", lambda e: e.tensor_scalar(out=omlb[:], in0=lbc[:], scalar1=-1.0, scalar2=1.0, op0=ALU.mult, op1=ALU.add), r=["hg_lbc"], w=["hg_omlb"])
            S.op("dve", lambda e: e.tensor_scalar(out=nomlb[:], in0=lbc[:], scalar1=-1.0, scalar2=None, op0=ALU.add), r=["hg_lbc"], w=["hg_nomlb"])
            hg_stage = self.debug.get("hg_stage", 99)
            if hg_stage <= 1:
                S.barrier()
                return
            for hd in range(4):
                g_q = self.GIDX["hg_q"][hd]; g_i = self.GIDX["hg_i"][hd]; g_g = self.GIDX["hg_g"][hd]
                S.dma("sp", uin[:], self.Us[g_q, :, :], r=[("Us", g_q)], w=["hg_uin"])
                S.op("act", lambda e: e.activation(out=q[:], in_=uin[:], func=AF.Silu), r=["hg_uin"], w=["hg_q"])
                S.op("pool", lambda e: e.tensor_scalar(out=q[:], in0=q[:], scalar1=128.0 ** -0.5, scalar2=None, op0=ALU.mult), r=["hg_q"], w=["hg_q"])
                S.dma("sp", uin[:], self.Us[g_i, :, :], r=[("Us", g_i)], w=["hg_uin"])
                S.op("act", lambda e: e.activation(out=vbf[:], in_=uin[:], func=AF.Copy), r=["hg_uin"], w=["hg_vbf"])
                S.dma("sp", uin[:], self.Us[g_g, :, :], r=[("Us", g_g)], w=["hg_uin"])
                S.op("act", lambda e: e.activation(out=sgate[:], in_=uin[:], func=AF.Silu), r=["hg_uin"], w=["hg_sg"])
                for b4 in range(0, NB, 4):
                    nb_ = min(4, NB - b4)
                    pb = 6 + (b4 // 4) % 2
                    pv_ = self.ps[pb][:, :].bitcast(BF16)
                    S.group("pe", [
                        (lambda e, b=b, pv_=pv_, b4=b4: e.transpose(pv_[:, (b - b4) * 128:(b - b4 + 1) * 128], vbf[:, b * 128:(b + 1) * 128], self.ident_bf[:]))
                        for b in range(b4, b4 + nb_)], r=["hg_vbf", "ident_bf"], w=[("ps", pb)])
                    S.op("dve", lambda e, pv_=pv_, b4=b4, nb_=nb_: e.tensor_copy(out=v_tok[:, b4:b4 + nb_, :],
                                                                                in_=pv_[:, 0:nb_ * 128].rearrange("p (a b) -> p a b", b=128)),
                         r=[("ps", pb)], w=["hg_vtok"])
                if hg_stage <= 2:
                    S.barrier()
                    return
                for d in range(2):
                    g_f = self.GIDX["hg_ff" if d == 0 else "hg_fb"][hd]
                    ci = 4 * d + hd
                    S.dma("sp", uin[:], self.Us[g_f, :, :], r=[("Us", g_f)], w=["hg_uin"])
                    S.op("act", lambda e, d=d: e.activation(out=tmpr[d][:], in_=uin[:], func=AF.Sigmoid), r=["hg_uin"], w=[("hg_tmp", d)])
                    S.op("dve", lambda e, d=d, ci=ci: e.tensor_scalar(out=kk[d][:], in0=tmpr[d][:], scalar1=nomlb[:, ci:ci + 1], scalar2=omlb[:, ci:ci + 1],
                                                                     op0=ALU.mult, op1=ALU.add), r=[("hg_tmp", d), "hg_omlb", "hg_nomlb"], w=[("hg_kk", d)])
                    S.op("act", lambda e, d=d, ci=ci: e.activation(out=tmpr[d][:], in_=tmpr[d][:], func=AF.Ln, scale=omlb[:, ci:ci + 1], bias=lbc[:, ci:ci + 1]),
                         r=[("hg_tmp", d), "hg_omlb", "hg_lbc"], w=[("hg_tmp", d)])
                    if d == 0:
                        S.op("dve", lambda e, d=d: e.tensor_tensor_scan(out=cum[d][:], data0=maskS[d][:], data1=tmpr[d][:], initial=0.0, op0=ALU.mult, op1=ALU.add),
                             r=[("hg_mask", d), ("hg_tmp", d)], w=[("hg_cum", d)])
                    else:
                        S.op("dve", lambda e, d=d: e.tensor_tensor_scan(out=cum[d][:, ::-1], data0=maskS[d][:, ::-1], data1=tmpr[d][:, ::-1], initial=0.0,
                                                                       op0=ALU.mult, op1=ALU.add), r=[("hg_mask", d), ("hg_tmp", d)], w=[("hg_cum", d)])
                    cl_off = SUB - 1 if d == 0 else 0
                    cumL = cum[d][:, cl_off::SUB]
                    S.op("act", lambda e, d=d, cumL=cumL: e.activation(out=gdec[d][:], in_=cumL, func=AF.Exp), r=[("hg_cum", d)], w=[("hg_gdec", d)])
                    S.op("pool", lambda e, d=d, cumL=cumL: e.tensor_tensor(out=tmpr[d][:].rearrange("p (a b) -> p a b", b=SUB),
                                                                          in0=cumL.unsqueeze(2).to_broadcast([128, NSC, SUB]),
                                                                          in1=cum[d][:].rearrange("p (a b) -> p a b", b=SUB), op=ALU.subtract),
                         r=[("hg_cum", d)], w=[("hg_tmp", d)])
                    S.op("act", lambda e, d=d: e.activation(out=tmpr[d][:], in_=tmpr[d][:], func=AF.Exp), r=[("hg_tmp", d)], w=[("hg_tmp", d)])
                    S.op("dve", lambda e, d=d: e.tensor_tensor(out=kl[d][:], in0=tmpr[d][:], in1=kk[d][:], op=ALU.mult), r=[("hg_tmp", d), ("hg_kk", d)], w=[("hg_kl", d)])
                    S.op("act", lambda e, d=d: e.activation(out=tmpr[d][:], in_=cum[d][:], func=AF.Exp, scale=-1.0), r=[("hg_cum", d)], w=[("hg_tmp", d)])
                    S.op("dve", lambda e, d=d: e.tensor_tensor(out=kd[d][:], in0=tmpr[d][:], in1=kk[d][:], op=ALU.mult), r=[("hg_tmp", d), ("hg_kk", d)], w=[("hg_kd", d)])
                    S.op("act", lambda e, d=d: e.activation(out=tmpr[d][:], in_=cum[d][:], func=AF.Exp), r=[("hg_cum", d)], w=[("hg_tmp", d)])
                    S.op("pool", lambda e, d=d: e.tensor_tensor(out=qd[d][:], in0=tmpr[d][:], in1=q[:], op=ALU.mult), r=[("hg_tmp", d), "hg_q"], w=[("hg_qd", d)])
                    for b4 in range(0, NB, 4):
                        nb_ = min(4, NB - b4)
                        pb = 6 + (b4 // 4) % 2
                        pv_ = self.ps[pb][:, :].bitcast(BF16)
                        S.group("pe", [
                            (lambda e, b=b, pv_=pv_, b4=b4, d=d: e.transpose(pv_[:, (b - b4) * 128:(b - b4 + 1) * 128], kl[d][:, b * 128:(b + 1) * 128], self.ident_bf[:]))
                            for b in range(b4, b4 + nb_)], r=[("hg_kl", d), "ident_bf"], w=[("ps", pb)])
                        S.op("act", lambda e, pv_=pv_, b4=b4, nb_=nb_, d=d: e.activation(out=kl_tok[d][:, b4:b4 + nb_, :],
                                                                                         in_=pv_[:, 0:nb_ * 128].rearrange("p (a b) -> p a b", b=128), func=AF.Copy),
                             r=[("ps", pb)], w=[("hg_kltok", d)])
                    S.op("pool", lambda e, d=d: e.memset(Sst[d][0][:], 0.0), w=[("hg_S", d, 0)])
                    S.op("pool", lambda e, d=d: e.memset(Sbf[d][0][:], 0.0), w=[("hg_Sbf", d, 0)])
                if hg_stage <= 3:
                    S.barrier()
                    return
                sbi = [0, 0]
                for step in range(NB):
                    for d in range(2):
                        b = orders[d][step]
                        p_sc, p_o, p_u = 3 * d, 3 * d + 1, 3 * d + 2
                        bs = slice(b * 128, (b + 1) * 128)
                        S.op("pe", lambda e, d=d, bs=bs, p_sc=p_sc: e.matmul(self.ps[p_sc][:, 0:128], lhsT=kd[d][:, bs], rhs=qd[d][:, bs], start=True, stop=True),
                             r=[("hg_kd", d), ("hg_qd", d)], w=[("ps", p_sc)])
                        S.op("dve", lambda e, d=d, p_sc=p_sc: e.tensor_tensor(out=PT[d][:], in0=self.ps[p_sc][:, 0:128], in1=bdm[d][:], op=ALU.mult),
                             r=[("ps", p_sc), ("hg_bdm", d)], w=[("hg_PT", d)])
                        S.op("pool", lambda e, d=d, b=b: e.tensor_tensor(out=klm[d][:], in0=kl_tok[d][:, b, :].unsqueeze(1).to_broadcast([128, 4, 128]),
                                                                        in1=submask[:, :].unsqueeze(2).to_broadcast([128, 4, 128]), op=ALU.mult),
                             r=[("hg_kltok", d), "hg_submask"], w=[("hg_klm", d)])
                        S.group("pe", [
                            (lambda e, c=c, d=d, b=b, p_u=p_u: e.matmul(self.ps[p_u][:, c * 128:(c + 1) * 128], lhsT=klm[d][:, c, :],
                                                                        rhs=v_tok[:, b, :], start=True, stop=True))
                            for c in range(4)], r=[("hg_klm", d), "hg_vtok"], w=[("ps", p_u)])
                        S.op("pe", lambda e, d=d, b=b, p_o=p_o: e.matmul(self.ps[p_o][:, 0:128], lhsT=v_tok[:, b, :], rhs=PT[d][:], start=True, stop=False),
                             r=["hg_vtok", ("hg_PT", d)], w=[("ps", p_o)])
                        corder = range(4) if d == 0 else range(3, -1, -1)
                        for ci_, c in enumerate(corder):
                            cur = sbi[d]
                            S.op("pe", lambda e, d=d, b=b, c=c, p_o=p_o, cur=cur, ci_=ci_: e.matmul(
                                self.ps[p_o][:, SUB * c:SUB * c + SUB], lhsT=Sbf[d][cur][:], rhs=qd[d][:, b * 128 + SUB * c:b * 128 + SUB * c + SUB],
                                start=False, stop=(ci_ == 3)), r=[("hg_Sbf", d, cur), ("hg_qd", d)], w=[("ps", p_o)])
                            sc_idx = b * 4 + c
                            nxt = 1 - cur
                            S.op("dve", lambda e, d=d, c=c, p_u=p_u, sc_idx=sc_idx, cur=cur, nxt=nxt: e.scalar_tensor_tensor(
                                out=Sst[d][nxt][:], in0=Sst[d][cur][:], scalar=gdec[d][:, sc_idx:sc_idx + 1], in1=self.ps[p_u][:, c * 128:(c + 1) * 128],
                                op0=ALU.mult, op1=ALU.add), r=[("hg_S", d, cur), ("hg_gdec", d), ("ps", p_u)], w=[("hg_S", d, nxt)])
                            S.op("act", lambda e, d=d, nxt=nxt: e.activation(out=Sbf[d][nxt][:], in_=Sst[d][nxt][:], func=AF.Copy),
                                 r=[("hg_S", d, nxt)], w=[("hg_Sbf", d, nxt)])
                            sbi[d] = nxt
                        S.op("act", lambda e, d=d, bs=bs, p_o=p_o: e.activation(out=orow[d][:, bs], in_=self.ps[p_o][:, 0:128], func=AF.Copy),
                             r=[("ps", p_o)], w=[("hg_o", d)])
                if hg_stage <= 4:
                    S.barrier()
                    return
                S.op("dve", lambda e: e.tensor_tensor(out=orow[0][:], in0=orow[0][:], in1=orow[1][:], op=ALU.add), r=[("hg_o", 0), ("hg_o", 1)], w=[("hg_o", 0)])
                S.op("act", lambda e: e.activation(out=qd[0][:], in_=orow[0][:], func=AF.Square), r=[("hg_o", 0)], w=[("hg_qd", 0)])
                self.rms_rows([qd[0]], 1, 128.0, tmpr[0], [("hg_qd", 0)], ("hg_tmp", 0), pbase=6)
                S.op("dve", lambda e: e.scalar_tensor_tensor(out=orow[0][:], in0=orow[0][:], scalar=self.pvc("hg_nw"), in1=tmpr[0][:], op0=ALU.mult, op1=ALU.mult),
                     r=[("hg_o", 0), ("hg_tmp", 0), "pvt"], w=[("hg_o", 0)])
                S.op("dve", lambda e: e.tensor_tensor(out=kd[0][:], in0=orow[0][:], in1=sgate[:], op=ALU.mult), r=[("hg_o", 0), "hg_sg"], w=[("hg_kd", 0)])
                S.dma("sp", self.Ys[8 + hd, :, :], kd[0][:], r=[("hg_kd", 0)], w=[("Ys", 8 + hd)])
                self.pump_mods(9)
            S.barrier()

    def mix_ret(self, l):
        nc, S = self.nc, self.S
        orders = {0: list(range(NB)), 1: [1, 0] + list(range(NB - 1, 1, -1))}
        with ExitStack() as ls:
            sbl = lambda name, shape, dt=F32: ls.enter_context(nc.sbuf_tensor(self.un(name), list(shape), dt))
            uin = sbl("rt_uin", [128, T]); t1 = sbl("rt_t1", [128, T]); t2 = sbl("rt_t2", [128, T])
            xbf = sbl("rt_xbf", [64, T], BF16)
            cosT = sbl("rt_cos", [64, NLAT]); sinT = sbl("rt_sin", [64, NLAT])
            Rbf = sbl("rt_R", [64, 64], BF16)
            tab = sbl("rt_tab", [128, 5, 128])
            qb = sbl("rt_qb", [64, T], BF16); kb = sbl("rt_kb", [64, T], BF16)
            qE = [sbl(f"rt_qE{d}", [64, T], BF16) for d in range(2)]
            kw = [sbl(f"rt_kw{d}", [64, T], BF16) for d in range(2)]
            kw_tok = [sbl(f"rt_kwtok{d}", [128, NB, 64], BF16) for d in range(2)]
            vbf = sbl("rt_vbf", [128, T], BF16); v_tok = sbl("rt_vtok", [128, NB, 128], BF16)
            Sall = [sbl(f"rt_Sall{d}", [64, NB, 128], BF16) for d in range(2)]
            Sst = [[sbl(f"rt_S{d}{k}", [64, 128]) for k in range(2)] for d in range(2)]
            PT = [sbl(f"rt_PT{k}", [128, 128], BF16) for k in range(2)]
            orow = sbl("rt_o", [128, T]); sgate = sbl("rt_sg", [128, T]); ybf = sbl("rt_ybf", [128, T], BF16)
            S.dma("sp", cosT[:], self.rope[0, :, :], w=["rt_cos"])
            S.dma("sp", sinT[:], self.rope[1, :, :], w=["rt_sin"])
            S.dma("pool", Rbf[:], self.ropeR[:, :], w=["rt_R"])
            for h in range(4):
                g128 = float(np.exp(128.0 * np.log1p(-(2.0 ** (-5.0 - h)))))
                S.dma("sp", tab[:], self.rtab[h, :, :, :], w=["rt_tab"])
                for which, dst, scl in (("ret_q", qb, 1.0), ("ret_k", kb, 0.125)):
                    gi = self.GIDX[which][h]
                    S.dma("sp", uin[0:64, :], self.Us[gi, 0:64, :], r=[("Us", gi)], w=["rt_uin"])
                    S.op("act", lambda e: e.activation(out=xbf[:], in_=uin[0:64, :], func=AF.Copy), r=["rt_uin"], w=["rt_xbf"])
                    for ti in range(4):
                        pb = ti % 2
                        S.op("pe", lambda e, ti=ti, pb=pb: e.matmul(self.ps[pb][0:64, 0:512], lhsT=Rbf[:], rhs=xbf[:, NCTX + ti * 512:NCTX + (ti + 1) * 512],
                                                                  start=True, stop=True), r=["rt_R", "rt_xbf"], w=[("ps", pb)])
                        S.op("dve", lambda e, ti=ti, pb=pb: e.tensor_tensor(out=t1[0:64, NCTX + ti * 512:NCTX + (ti + 1) * 512], in0=self.ps[pb][0:64, 0:512],
                                                                           in1=sinT[:, ti * 512:(ti + 1) * 512], op=ALU.mult),
                             r=[("ps", pb), "rt_sin"], w=["rt_t1"])
                    S.op("pool", lambda e: e.tensor_tensor(out=t2[0:64, NCTX:T], in0=uin[0:64, NCTX:T], in1=cosT[:], op=ALU.mult), r=["rt_uin", "rt_cos"], w=["rt_t2"])
                    S.op("dve", lambda e, dst=dst, scl=scl: e.scalar_tensor_tensor(out=dst[:, NCTX:T], in0=t1[0:64, NCTX:T], scalar=scl, in1=t2[0:64, NCTX:T],
                                                                                  op0=ALU.mult, op1=ALU.add) if scl == 1.0 else
                         e.tensor_tensor(out=t1[0:64, NCTX:T], in0=t1[0:64, NCTX:T], in1=t2[0:64, NCTX:T], op=ALU.add),
                         r=["rt_t1", "rt_t2"], w=["rt_t1", ("rt_qk", which)])
                    if scl != 1.0:
                        S.op("dve", lambda e, dst=dst, scl=scl: e.tensor_scalar(out=dst[:, NCTX:T], in0=t1[0:64, NCTX:T], scalar1=scl, scalar2=None, op0=ALU.mult),
                             r=["rt_t1"], w=[("rt_qk", which)])
                    S.op("act", lambda e, dst=dst, scl=scl: e.activation(out=dst[:, 0:NCTX], in_=uin[0:64, 0:NCTX], func=AF.Copy, scale=scl),
                         r=["rt_uin"], w=[("rt_qk", which, "c")])
                rq = [("rt_qk", "ret_q"), ("rt_qk", "ret_q", "c")]
                rk = [("rt_qk", "ret_k"), ("rt_qk", "ret_k", "c")]
                for d in range(2):
                    S.op("dve", lambda e, d=d: e.tensor_tensor(out=qE[d][:].rearrange("p (a b) -> p a b", b=128), in0=qb[:].rearrange("p (a b) -> p a b", b=128),
                                                               in1=tab[0:64, 1 + d, :].unsqueeze(1).to_broadcast([64, NB, 128]), op=ALU.mult),
                         r=rq + ["rt_tab"], w=[("rt_qE", d)])
                    S.op("pool", lambda e, d=d: e.tensor_tensor(out=kw[d][:].rearrange("p (a b) -> p a b", b=128), in0=kb[:].rearrange("p (a b) -> p a b", b=128),
                                                                in1=tab[0:64, 3 + d, :].unsqueeze(1).to_broadcast([64, NB, 128]), op=ALU.mult),
                         r=rk + ["rt_tab"], w=[("rt_kw", d)])
                    for b8 in range(0, NB, 8):
                        nb_ = min(8, NB - b8)
                        pb = 6 + (b8 // 8) % 2
                        pv_ = self.ps[pb][:, :].bitcast(BF16)
                        S.group("pe", [
                            (lambda e, b=b, pv_=pv_, b8=b8, d=d: e.transpose(pv_[:, (b - b8) * 64:(b - b8 + 1) * 64], kw[d][:, b * 128:(b + 1) * 128], self.ident_bf[0:64, 0:64]))
                            for b in range(b8, b8 + nb_)], r=[("rt_kw", d), "ident_bf"], w=[("ps", pb)])
                        S.op("act", lambda e, pv_=pv_, b8=b8, nb_=nb_, d=d: e.activation(out=kw_tok[d][:, b8:b8 + nb_, :],
                                                                                         in_=pv_[:, 0:nb_ * 64].rearrange("p (a b) -> p a b", b=64), func=AF.Copy),
                             r=[("ps", pb)], w=[("rt_kwtok", d)])
                gv = self.GIDX["ret_v"][h]; gg = self.GIDX["ret_g"][h]
                S.dma("sp", uin[:], self.Us[gv, :, :], r=[("Us", gv)], w=["rt_uin"])
                S.op("act", lambda e: e.activation(out=vbf[:], in_=uin[:], func=AF.Copy), r=["rt_uin"], w=["rt_vbf"])
                for b4 in range(0, NB, 4):
                    nb_ = min(4, NB - b4)
                    pb = 6 + (b4 // 4) % 2
                    pv_ = self.ps[pb][:, :].bitcast(BF16)
                    S.group("pe", [
                        (lambda e, b=b, pv_=pv_, b4=b4: e.transpose(pv_[:, (b - b4) * 128:(b - b4 + 1) * 128], vbf[:, b * 128:(b + 1) * 128], self.ident_bf[:]))
                        for b in range(b4, b4 + nb_)], r=["rt_vbf", "ident_bf"], w=[("ps", pb)])
                    S.op("dve", lambda e, pv_=pv_, b4=b4, nb_=nb_: e.tensor_copy(out=v_tok[:, b4:b4 + nb_, :],
                                                                                in_=pv_[:, 0:nb_ * 128].rearrange("p (a b) -> p a b", b=128)),
                         r=[("ps", pb)], w=["rt_vtok"])
                S.dma("sp", uin[:], self.Us[gg, :, :], r=[("Us", gg)], w=["rt_uin"])
                S.op("act", lambda e: e.activation(out=sgate[:], in_=uin[:], func=AF.Silu), r=["rt_uin"], w=["rt_sg"])
                for d in range(2):
                    S.op("pool", lambda e, d=d: e.memset(Sst[d][0][:], 0.0), w=[("rt_S", d, 0)])
                    S.op("pool", lambda e, d=d: e.memset(Sall[d][:, orders[d][0], :], 0.0), w=[("rt_Sall", d)])
                for step in range(NB - 1):
                    for d in range(2):
                        b = orders[d][step]
                        bn = orders[d][step + 1]
                        pu = 4 + d
                        S.op("pe", lambda e, d=d, b=b, pu=pu: e.matmul(self.ps[pu][0:64, 0:128], lhsT=kw_tok[d][:, b, :], rhs=v_tok[:, b, :], start=True, stop=True),
                             r=[("rt_kwtok", d), "rt_vtok"], w=[("ps", pu)])
                        cur = step % 2
                        nxt = 1 - cur
                        S.op("dve", lambda e, d=d, pu=pu, cur=cur, nxt=nxt: e.scalar_tensor_tensor(out=Sst[d][nxt][:], in0=Sst[d][cur][:], scalar=g128, in1=self.ps[pu][0:64, 0:128],
                                                                                 op0=ALU.mult, op1=ALU.add), r=[("rt_S", d, cur), ("ps", pu)], w=[("rt_S", d, nxt)])
                        S.op("act", lambda e, d=d, bn=bn, nxt=nxt: e.activation(out=Sall[d][:, bn, :], in_=Sst[d][nxt][:], func=AF.Copy), r=[("rt_S", d, nxt)], w=[("rt_Sall", d, bn)])
                for b in range(NB):
                    bs = slice(b * 128, (b + 1) * 128)
                    pa = b % 2
                    po = 2 + b % 2
                    S.op("pe", lambda e, bs=bs, pa=pa: e.matmul(self.ps[pa][:, 0:128], lhsT=kb[:, bs], rhs=qb[:, bs], start=True, stop=True),
                         r=rq + rk, w=[("ps", pa)])
                    S.op("dve", lambda e, pa=pa: e.tensor_tensor(out=PT[pa][:], in0=self.ps[pa][:, 0:128], in1=tab[:, 0, :], op=ALU.mult),
                         r=[("ps", pa), "rt_tab"], w=[("rt_PT", pa)])
                    S.group("pe", [
                        lambda e, b=b, po=po, pa=pa: e.matmul(self.ps[po][:, 0:128], lhsT=v_tok[:, b, :], rhs=PT[pa][:], start=True, stop=False),
                        lambda e, b=b, po=po, bs=bs: e.matmul(self.ps[po][:, 0:128], lhsT=Sall[0][:, b, :], rhs=qE[0][:, bs], start=False, stop=False),
                        lambda e, b=b, po=po, bs=bs: e.matmul(self.ps[po][:, 0:128], lhsT=Sall[1][:, b, :], rhs=qE[1][:, bs], start=False, stop=True)],
                        r=["rt_vtok", ("rt_PT", pa), ("rt_Sall", 0), ("rt_Sall", 1), ("rt_Sall", 0, b), ("rt_Sall", 1, b), ("rt_qE", 0), ("rt_qE", 1)], w=[("ps", po)])
                    S.op("act", lambda e, bs=bs, po=po: e.activation(out=orow[:, bs], in_=self.ps[po][:, 0:128], func=AF.Copy), r=[("ps", po)], w=["rt_o"])
                S.op("act", lambda e: e.activation(out=ybf[:], in_=orow[:], func=AF.Square), r=["rt_o"], w=["rt_ybf"])
                self.rms_rows([ybf], 1, 128.0, t1, ["rt_ybf"], "rt_t1", pbase=6)
                S.op("dve", lambda e: e.tensor_tensor(out=orow[:], in0=orow[:], in1=t1[:], op=ALU.mult), r=["rt_o", "rt_t1"], w=["rt_o"])
                S.op("dve", lambda e: e.tensor_tensor(out=ybf[:], in0=orow[:], in1=sgate[:], op=ALU.mult), r=["rt_o", "rt_sg"], w=["rt_ybf"])
                S.dma("sp", self.Ys[12 + h, :, :], ybf[:], r=["rt_ybf"], w=[("Ys", 12 + h)])
                self.pump_mods(9)
            S.barrier()

    def mix_ssd(self, l):
        nc, S = self.nc, self.S
        orders = {0: list(range(NB)), 1: [1, 0] + list(range(NB - 1, 1, -1))}
        with ExitStack() as ls:
            sbl = lambda name, shape, dt=F32: ls.enter_context(nc.sbuf_tensor(self.un(name), list(shape), dt))
            uin = sbl("sd_uin", [128, T]); t1 = sbl("sd_t1", [128, T])
            xs = sbl("sd_xs", [128, 4, T], BF16)
            BT = [sbl(f"sd_BT{g}", [64, T], BF16) for g in range(2)]
            CT = [sbl(f"sd_CT{g}", [64, T], BF16) for g in range(2)]
            x_tok = sbl("sd_xtok", [128, NB, 512], BF16)
            B_tok = sbl("sd_Btok", [128, NB, 2, 64], BF16)
            dl = sbl("sd_dl", [16, T]); la = sbl("sd_la", [16, T])
            dl_tok = sbl("sd_dltok", [128, NB, 16]); la_tok = sbl("sd_latok", [128, NB, 16])
            negA = sbl("sd_negA", [16, 1])
            yacc = sbl("sd_yacc", [128, 4, T])
            utri = [sbl(f"sd_utri{d}", [128, 128]) for d in range(2)]
            cumt = [sbl(f"sd_cumt{d}", [128, 8]) for d in range(2)]
            la_bc = [sbl(f"sd_labc{d}", [128, 8, 128]) for d in range(2)]
            E = [sbl(f"sd_E{d}", [64, 8, 128]) for d in range(2)]
            LT = [sbl(f"sd_LT{d}", [128, 8, 128]) for d in range(2)]
            GT = [sbl(f"sd_GT{d}", [128, 2, 128]) for d in range(2)]
            PT = [sbl(f"sd_PT{d}", [128, 8, 128], BF16) for d in range(2)]
            CE = [sbl(f"sd_CE{d}", [64, 8, 128], BF16) for d in range(2)]
            wcol = [sbl(f"sd_w{d}", [128, 8]) for d in range(2)]
            Bw = [sbl(f"sd_Bw{d}", [128, 8, 64], BF16) for d in range(2)]
            Sst = [[sbl(f"sd_S{d}{k}", [64, 8, 64]) for k in range(2)] for d in range(2)]
            Sbf = [sbl(f"sd_Sbf{d}", [64, 8, 64], BF16) for d in range(2)]
            ocw, _ = PV["ssd_cw"]
            onesf = sbl("sd_onesf", [128, 8, 128])
            S.op("pool", lambda e: e.memset(onesf[:], 1.0), w=["sd_onesf"])
            for d in range(2):
                sgn = 1 if d == 0 else -1
                S.op("pool", lambda e, d=d: e.memset(utri[d][:], 1.0), w=[("sd_utri", d)])
                S.op("pool", lambda e, d=d, sgn=sgn: e.affine_select(out=utri[d][:], in_=utri[d][:], pattern=[[sgn, 128]], compare_op=ALU.is_ge,
                                                                    fill=0.0, base=0, channel_multiplier=-sgn), r=[("sd_utri", d)], w=[("sd_utri", d)])
            for c in range(4):
                gi = self.GIDX["ssd_x"][c]
                S.dma("sp", uin[:], self.Us[gi, :, :], r=[("Us", gi)], w=["sd_uin"])
                wc = [self.pvt[:, ocw + 8 * j + c:ocw + 8 * j + c + 1] for j in range(4)]
                self.conv_row("dve", t1[:], uin[:], wc, self.pvc("ssd_cb", c), "sd_t1", "sd_uin", ["pvt"])
                S.op("act", lambda e, c=c: e.activation(out=xs[:, c, :], in_=t1[:], func=AF.Silu), r=["sd_t1"], w=[("sd_xs", c)])
            for which, dsts, base in (("ssd_B", BT, 4), ("ssd_C", CT, 6)):
                for g in range(2):
                    gi = self.GIDX[which][g]
                    S.dma("sp", uin[0:64, :], self.Us[gi, 0:64, :], r=[("Us", gi)], w=["sd_uin"])
                    col = base + g
                    wc = [self.pvt[0:64, ocw + 8 * j + col:ocw + 8 * j + col + 1] for j in range(4)]
                    self.conv_row("dve", t1[0:64, :], uin[0:64, :], wc, self.pvc("ssd_cb", col, rows=64), "sd_t1", "sd_uin", ["pvt"])
                    S.op("act", lambda e, dsts=dsts, g=g: e.activation(out=dsts[g][:], in_=t1[0:64, :], func=AF.Silu), r=["sd_t1"], w=[(which, g)])
            gi = self.GIDX["ssd_dt"][0]
            S.dma("sp", dl[:], self.Us[gi, 0:16, :], r=[("Us", gi)], w=["sd_dl"])
            S.op("act", lambda e: e.activation(out=dl[:], in_=dl[:], func=AF.Exp, bias=self.pvc("ssd_dtb", 0, rows=16)), r=["sd_dl", "pvt"], w=["sd_dl"])
            S.op("act", lambda e: e.activation(out=dl[:], in_=dl[:], func=AF.Ln, bias=self.one_col[0:16, 0:1]), r=["sd_dl", "one_col"], w=["sd_dl"])
            S.op("act", lambda e: e.activation(out=negA[:], in_=self.pvc("ssd_alog", 0, rows=16), func=AF.Exp), r=["pvt"], w=["sd_negA"])
            S.op("dve", lambda e: e.tensor_scalar(out=negA[:], in0=negA[:], scalar1=-1.0, scalar2=None, op0=ALU.mult), r=["sd_negA"], w=["sd_negA"])
            S.op("dve", lambda e: e.tensor_scalar(out=la[:], in0=dl[:], scalar1=negA[:, 0:1], scalar2=None, op0=ALU.mult), r=["sd_dl", "sd_negA"], w=["sd_la"])
            sd_stage = self.debug.get("sd_stage", 99)
            if sd_stage <= 1:
                S.barrier()
                return
            for src, dst, key in ((dl, dl_tok, "sd_dltok"), (la, la_tok, "sd_latok")):
                S.group("pe", [
                    (lambda e, b=b, src=src: e.transpose(self.ps[6][:, b * 16:(b + 1) * 16], src[:, b * 128:(b + 1) * 128], self.identf[0:16, 0:16]))
                    for b in range(NB)], r=["sd_dl", "sd_la", "identf"], w=[("ps", 6)])
                S.op("dve", lambda e, dst=dst: e.tensor_copy(out=dst[:], in_=self.ps[6][:, 0:NB * 16].rearrange("p (a b) -> p a b", b=16)),
                     r=[("ps", 6)], w=[key])
            if sd_stage <= 2:
                S.barrier()
                return
            cnt = 0
            for c in range(4):
                for b4 in range(0, NB, 4):
                    nb_ = min(4, NB - b4)
                    pb = 6 + cnt % 2
                    cnt += 1
                    pv_ = self.ps[pb][:, :].bitcast(BF16)
                    S.group("pe", [
                        (lambda e, b=b, pv_=pv_, b4=b4, c=c: e.transpose(pv_[:, (b - b4) * 128:(b - b4 + 1) * 128], xs[:, c, b * 128:(b + 1) * 128], self.ident_bf[:]))
                        for b in range(b4, b4 + nb_)], r=[("sd_xs", c), "ident_bf"], w=[("ps", pb)])
                    S.op("act" if cnt % 2 else "dve", (lambda e, pv_=pv_, b4=b4, nb_=nb_, c=c: e.activation(
                        out=x_tok[:, b4:b4 + nb_, c * 128:(c + 1) * 128], in_=pv_[:, 0:nb_ * 128].rearrange("p (a b) -> p a b", b=128), func=AF.Copy)) if cnt % 2 else
                        (lambda e, pv_=pv_, b4=b4, nb_=nb_, c=c: e.tensor_copy(
                            out=x_tok[:, b4:b4 + nb_, c * 128:(c + 1) * 128], in_=pv_[:, 0:nb_ * 128].rearrange("p (a b) -> p a b", b=128))),
                        r=[("ps", pb)], w=[("sd_xtok", c)])
            for g in range(2):
                for b8 in range(0, NB, 8):
                    nb_ = min(8, NB - b8)
                    pb = 6 + cnt % 2
                    cnt += 1
                    pv_ = self.ps[pb][:, :].bitcast(BF16)
                    S.group("pe", [
                        (lambda e, b=b, pv_=pv_, b8=b8, g=g: e.transpose(pv_[:, (b - b8) * 64:(b - b8 + 1) * 64], BT[g][:, b * 128:(b + 1) * 128], self.ident_bf[0:64, 0:64]))
                        for b in range(b8, b8 + nb_)], r=[("ssd_B", g), "ident_bf"], w=[("ps", pb)])
                    S.op("dve", lambda e, pv_=pv_, b8=b8, nb_=nb_, g=g: e.tensor_copy(out=B_tok[:, b8:b8 + nb_, g, :],
                                                                                     in_=pv_[:, 0:nb_ * 64].rearrange("p (a b) -> p a b", b=64)),
                         r=[("ps", pb)], w=[("sd_Btok", g)])
            for d in range(2):
                S.op("pool", lambda e, d=d: e.memset(Sst[d][0][:], 0.0), w=[("sd_S", d, 0)])
                S.op("pool", lambda e, d=d: e.memset(Sbf[d][:], 0.0), w=[("sd_Sbf", d)])
            xtok_keys = [("sd_xtok", c) for c in range(4)]
            if sd_stage <= 3:
                S.barrier()
                return
            ywritten = set()
            for step in range(NB):
                for d in range(2):
                    b = orders[d][step]
                    bs = slice(b * 128, (b + 1) * 128)
                    p0, p1, p2, p3 = 4 * d, 4 * d + 1, 4 * d + 2, 4 * d + 3
                    tl = 127 if d == 0 else 0
                    sgn = 1 if d == 0 else -1
                    la_b = la_tok[:, b, 8 * d:8 * d + 8]
                    dl_b = dl_tok[:, b, 8 * d:8 * d + 8]
                    S.op("pe", lambda e, d=d, p0=p0, la_b=la_b: e.matmul(self.ps[p0][:, 256:264], lhsT=utri[d][:], rhs=la_b, start=True, stop=True),
                         r=[("sd_utri", d), "sd_latok"], w=[("ps", p0)])
                    S.op("act", lambda e, d=d, p0=p0: e.activation(out=cumt[d][:], in_=self.ps[p0][:, 256:264], func=AF.Copy), r=[("ps", p0)], w=[("sd_cumt", d)])
                    S.op("dve", lambda e, d=d, la_b=la_b: e.tensor_tensor(out=la_bc[d][:], in0=onesf[:], in1=la_b.unsqueeze(2).to_broadcast([128, 8, 128]), op=ALU.mult),
                         r=["sd_latok", "sd_onesf"], w=[("sd_labc", d)])
                    if sd_stage == 41:
                        S.barrier()
                        return
                    S.group("pe", [
                        (lambda e, g=g, p0=p0, bs=bs: e.matmul(self.ps[p0][:, g * 128:(g + 1) * 128], lhsT=BT[g][:, bs], rhs=CT[g][:, bs], start=True, stop=True))
                        for g in range(2)], r=[("ssd_B", 0), ("ssd_B", 1), ("ssd_C", 0), ("ssd_C", 1)], w=[("ps", p0)])
                    S.op("act", lambda e, d=d, p0=p0: e.activation(out=GT[d][:], in_=self.ps[p0][:, 0:256].rearrange("p (a b) -> p a b", b=128), func=AF.Copy),
                         r=[("ps", p0)], w=[("sd_GT", d)])
                    if sd_stage == 42:
                        S.barrier()
                        return
                    for hf in range(2):
                        hs = slice(4 * hf, 4 * hf + 4)
                        S.group("pe", [
                            (lambda e, j=j, d=d, p1=p1, hf=hf: e.matmul(self.ps[p1][:, j * 128:(j + 1) * 128], lhsT=la_bc[d][:, 4 * hf + j, :], rhs=utri[d][:],
                                                                        start=True, stop=True))
                            for j in range(4)], r=[("sd_labc", d), ("sd_utri", d)], w=[("ps", p1)])
                        S.op("act", lambda e, d=d, p1=p1, hs=hs: e.activation(out=E[d][:, hs, :], in_=self.ps[p1][0:64, :].rearrange("p (a b) -> p a b", b=128), func=AF.Exp),
                             r=[("ps", p1)], w=[("sd_E", d, hf)])
                        if sd_stage == 43:
                            S.barrier()
                            return
                        S.op("dve", lambda e, d=d, p1=p1, hs=hs: e.tensor_tensor(out=LT[d][:, hs, :], in0=self.ps[p1][:, :].rearrange("p (a b) -> p a b", b=128),
                                                                                in1=cumt[d][:, hs].unsqueeze(2).to_broadcast([128, 4, 128]), op=ALU.subtract),
                             r=[("ps", p1), ("sd_cumt", d)], w=[("sd_LT", d, hf)])
                        if sd_stage == 44:
                            S.barrier()
                            return
                        S.op("act", lambda e, d=d, hs=hs: e.activation(out=LT[d][:, hs, :], in_=LT[d][:, hs, :], func=AF.Exp), r=[("sd_LT", d, hf)], w=[("sd_LT", d, hf)])
                        if sd_stage == 45:
                            S.barrier()
                            return
                        S.op("pool", lambda e, d=d, hs=hs, sgn=sgn: e.affine_select(out=LT[d][:, hs, :], in_=LT[d][:, hs, :], pattern=[[0, 4], [sgn, 128]],
                                                                                  compare_op=ALU.is_ge, fill=0.0, base=0, channel_multiplier=-sgn),
                             r=[("sd_LT", d, hf)], w=[("sd_LT", d, hf)])
                    ltk = [("sd_LT", d, 0), ("sd_LT", d, 1)]
                    if sd_stage <= 4:
                        S.barrier()
                        return
                    S.op("dve", lambda e, d=d, dl_b=dl_b, tl=tl: e.tensor_tensor(out=wcol[d][:], in0=dl_b, in1=LT[d][:, :, tl], op=ALU.mult),
                         r=ltk + ["sd_dltok"], w=[("sd_w", d)])
                    S.op("pool", lambda e, d=d, dl_b=dl_b: e.tensor_tensor(out=LT[d][:], in0=LT[d][:], in1=dl_b.unsqueeze(2).to_broadcast([128, 8, 128]), op=ALU.mult),
                         r=ltk + ["sd_dltok"], w=ltk)
                    for g in range(2):
                        S.op("dve", lambda e, d=d, g=g: e.tensor_tensor(out=PT[d][:, 4 * g:4 * g + 4, :], in0=LT[d][:, 4 * g:4 * g + 4, :],
                                                                       in1=GT[d][:, g, :].unsqueeze(1).to_broadcast([128, 4, 128]), op=ALU.mult),
                             r=ltk + [("sd_GT", d)], w=[("sd_PT", d, g)])
                        S.op("pool", lambda e, d=d, g=g, bs=bs: e.tensor_tensor(out=CE[d][:, 4 * g:4 * g + 4, :], in0=E[d][:, 4 * g:4 * g + 4, :],
                                                                              in1=CT[g][:, bs].unsqueeze(1).to_broadcast([64, 4, 128]), op=ALU.mult),
                             r=[("sd_E", d, g), ("ssd_C", g)], w=[("sd_CE", d, g)])
                        S.op("dve", lambda e, d=d, g=g, b=b: e.tensor_tensor(out=Bw[d][:, 4 * g:4 * g + 4, :], in0=B_tok[:, b, g, :].unsqueeze(1).to_broadcast([128, 4, 64]),
                                                                            in1=wcol[d][:, 4 * g:4 * g + 4].unsqueeze(2).to_broadcast([128, 4, 64]), op=ALU.mult),
                             r=[("sd_Btok", g), ("sd_w", d)], w=[("sd_Bw", d, g)])
                    if sd_stage <= 5:
                        S.barrier()
                        return
                    fns = []
                    for h in range(8):
                        po = self.ps[p2][64 * (h % 2):64 * (h % 2) + 64, (h // 2) * 128:(h // 2 + 1) * 128]
                        fns.append(lambda e, h=h, po=po, d=d, b=b: e.matmul(po, lhsT=x_tok[:, b, 64 * h:64 * h + 64], rhs=PT[d][:, h, :], start=True, stop=False))
                        fns.append(lambda e, h=h, po=po, d=d: e.matmul(po, lhsT=Sbf[d][:, h, :], rhs=CE[d][:, h, :], start=False, stop=True))
                    S.group("pe", fns, r=xtok_keys + [("sd_PT", d, 0), ("sd_PT", d, 1), ("sd_CE", d, 0), ("sd_CE", d, 1), ("sd_Sbf", d)], w=[("ps", p2)])
                    yv = yacc[:, :, bs]
                    pv3 = self.ps[p2][:, :].rearrange("p (a b) -> p a b", b=128)
                    if b not in ywritten:
                        ywritten.add(b)
                        S.op("act", lambda e, yv=yv, pv3=pv3: e.activation(out=yv, in_=pv3, func=AF.Copy), r=[("ps", p2)], w=[("sd_yacc", b)])
                    else:
                        S.op("dve", lambda e, yv=yv, pv3=pv3: e.tensor_tensor(out=yv, in0=yv, in1=pv3, op=ALU.add), r=[("ps", p2), ("sd_yacc", b)], w=[("sd_yacc", b)])
                    if sd_stage <= 6:
                        S.barrier()
                        return
                    S.group("pe", [
                        (lambda e, h=h, d=d, b=b, p3=p3: e.matmul(self.ps[p3][0:64, 64 * h:64 * h + 64], lhsT=Bw[d][:, h, :], rhs=x_tok[:, b, 64 * h:64 * h + 64],
                                                                  start=True, stop=True))
                        for h in range(8)], r=xtok_keys + [("sd_Bw", d, 0), ("sd_Bw", d, 1)], w=[("ps", p3)])
                    cur = step % 2
                    nxt = 1 - cur
                    S.op("pool", lambda e, d=d, tl=tl, cur=cur, nxt=nxt: e.tensor_tensor(out=Sst[d][nxt][:], in0=Sst[d][cur][:], in1=E[d][:, :, tl].unsqueeze(2).to_broadcast([64, 8, 64]), op=ALU.mult),
                         r=[("sd_S", d, cur), ("sd_E", d, 0), ("sd_E", d, 1)], w=[("sd_S", d, nxt)])
                    S.op("dve", lambda e, d=d, p3=p3, nxt=nxt: e.tensor_tensor(out=Sst[d][nxt][:], in0=Sst[d][nxt][:], in1=self.ps[p3][0:64, :].rearrange("p (a b) -> p a b", b=64), op=ALU.add),
                         r=[("sd_S", d, nxt), ("ps", p3)], w=[("sd_S", d, nxt)])
                    S.op("act", lambda e, d=d, nxt=nxt: e.activation(out=Sbf[d][:], in_=Sst[d][nxt][:], func=AF.Copy), r=[("sd_S", d, nxt)], w=[("sd_Sbf", d)])
            xflat = x_tok[:].rearrange("p a b -> p (a b)")
            sqb = [xflat[:, c * T:(c + 1) * T] for c in range(4)]
            yk = [("sd_yacc", b) for b in range(NB)]
            for c in range(4):
                gi = self.GIDX["ssd_z"][c]
                S.dma("sp", uin[:], self.Us[gi, :, :], r=[("Us", gi)], w=["sd_uin"])
                S.op("act", lambda e: e.activation(out=t1[:], in_=uin[:], func=AF.Silu), r=["sd_uin"], w=["sd_t1"])
                S.op("dve", lambda e, c=c: e.scalar_tensor_tensor(out=yacc[:, c, :], in0=xs[:, c, :], scalar=self.pvc("ssd_d", c), in1=yacc[:, c, :],
                                                                  op0=ALU.mult, op1=ALU.add), r=yk + [("sd_xs", c), "pvt"], w=[("sd_y", c)])
                S.op("dve", lambda e, c=c: e.tensor_tensor(out=yacc[:, c, :], in0=yacc[:, c, :], in1=t1[:], op=ALU.mult), r=[("sd_y", c), "sd_t1"], w=[("sd_y", c)])
                S.op("act", lambda e, c=c: e.activation(out=sqb[c][:], in_=yacc[:, c, :], func=AF.Square), r=[("sd_y", c)], w=[("sd_sq", c)] + xtok_keys)
            self.rms_rows(sqb, 4, 512.0, t1, [("sd_sq", c) for c in range(4)], "sd_t1", pbase=6)
            for c in range(4):
                S.op("dve", lambda e, c=c: e.scalar_tensor_tensor(out=sqb[c][:], in0=yacc[:, c, :], scalar=self.pvc("ssd_nw", c), in1=t1[:],
                                                                  op0=ALU.mult, op1=ALU.mult), r=[("sd_y", c), "sd_t1", "pvt", ("sd_sq", c)], w=[("sd_yo", c)])
                S.dma("sp", self.Ys[c, :, :], sqb[c][:], r=[("sd_yo", c)], w=[("Ys", c)])
            S.barrier()

    def outproj(self, l, tiles):
        nc, S = self.nc, self.S
        TT = 768
        with ExitStack() as ls:
            sbl = lambda name, shape, dt=F32: ls.enter_context(nc.sbuf_tensor(self.un(name), list(shape), dt))
            wo = sbl("op_wo", [128, KC, D], BF16)
            yTs = [sbl(f"op_yT{k}", [128, KC, TT], BF16) for k in range(2)]
            oTs = [sbl(f"op_oT{k}", [128, KC, TT], BF16) for k in range(2)]
            xg = [sbl(f"op_xg{k}", [128, 2, TT], F32) for k in range(2)]
            sq = [sbl(f"op_sq{k}", [128, 2, TT], BF16) for k in range(2)]
            rstds = [sbl(f"op_rstd{k}", [128, TT], F32) for k in range(2)]
            tmp = [sbl(f"op_tmp{k}", [128, TT], F32) for k in range(2)]
            wv = self.w_out[l].rearrange("(kc p) n -> p kc n", p=128)
            for q4 in range(4):
                S.dma("pool", wo[:, :, q4 * 512:(q4 + 1) * 512], wv[:, :, q4 * 512:(q4 + 1) * 512], w=[("op_wo", q4)])
            okeys = ["oTa", "oTb"]
            pending = None
            for ti_, (t0, n) in enumerate(tiles):
                yT = yTs[ti_ % 2]; oT = oTs[ti_ % 2]; ok = okeys[ti_ % 2]; rstd = rstds[ti_ % 2]
                hv = halves(n)
                nh = len(hv)
                S.dma("sp", yT[:, :, 0:n], self.Ys[:, :, t0:t0 + n].rearrange("k p t -> p k t"), r=[("Ys", c) for c in range(KC)], w=[("op_yT", ti_ % 2)])
                for mo in range(KC):
                    pbank = [(mo % 2) * 2 + 2, (mo % 2) * 2 + 3]
                    S.group("pe", [
                        (lambda e, k=k, hi=hi, o=o, sz=sz, mo=mo, pbank=pbank: e.matmul(
                            self.ps[pbank[hi]][:, 0:sz], lhsT=wo[:, k, mo * 128:(mo + 1) * 128], rhs=yT[:, k, o:o + sz], start=(k == 0), stop=(k == KC - 1)))
                        for k in range(KC) for hi, (o, sz) in enumerate(hv)],
                        r=[("op_wo", mo // 4), ("op_yT", ti_ % 2)], w=[("ps", p) for p in pbank[:nh]])
                    for hi, (o, sz) in enumerate(hv):
                        S.op("act", lambda e, hi=hi, o=o, sz=sz, mo=mo, pbank=pbank: e.activation(out=oT[:, mo, o:o + sz], in_=self.ps[pbank[hi]][:, 0:sz], func=AF.Copy),
                             r=[("ps", pbank[hi])], w=[(ok, mo)])
                    if pending is not None and 1 <= mo < 9:
                        pending(mo - 1)
                        if mo == 8:
                            pending = None
                pss = [self.ps[0], self.ps[1]]
                for mo in range(KC):
                    j = mo % 2
                    S.op("act", lambda e, mo=mo, j=j: e.activation(out=sq[0][:, j, 0:n], in_=oT[:, mo, 0:n], func=AF.Square), r=[(ok, mo)], w=[("sq", 0, j)])
                    S.group("pe", [
                        (lambda e, hi=hi, o=o, sz=sz, j=j, mo=mo: e.matmul(pss[hi][:, 0:sz], lhsT=self.ones_bf[:], rhs=sq[0][:, j, o:o + sz],
                                                                       start=(mo == 0), stop=(mo == KC - 1)))
                        for hi, (o, sz) in enumerate(hv)], r=[("sq", 0, j), "ones_bf"], w=[("ps", 0), ("ps", 1)])
                rk = [("rstd", id(rstd), 0), ("rstd", id(rstd), 1)]
                for hi, (o, sz) in enumerate(hv):
                    S.op("act", lambda e, hi=hi, o=o, sz=sz, rstd=rstd: e.activation(out=rstd[:, o:o + sz], in_=pss[hi][:, 0:sz], func=AF.Sqrt,
                                                                                    scale=1.0 / D, bias=self.eps_col[:, 0:1]),
                         r=[("ps", hi), "eps_col"], w=[rk[hi]])
                    S.op("dve", lambda e, o=o, sz=sz, rstd=rstd: e.reciprocal(rstd[:, o:o + sz], rstd[:, o:o + sz]), r=[rk[hi]], w=[rk[hi]])

                def post(g=None, t0=t0, n=n, oT=oT, ok=ok, rstd=rstd):
                    self.postnorm_apply(1, t0, n, oT, (xg, sq, rstd, tmp), ykey=ok, groups=(None if g is None else [g]))
                if ti_ + 1 < len(tiles):
                    pending = post
                else:
                    post()
            S.barrier()


CT_N = 8


def host_pack(inputs):
    f = lambda a: np.ascontiguousarray(np.asarray(a, dtype=np.float32))
    pv = np.zeros((DEPTH, 128, NPV), np.float32)

    def put(l, name, arr):
        o, n = PV[name]
        assert arr.shape == (128, n), (name, arr.shape)
        pv[l, :, o:o + n] = arr
    cm = lambda v: f(v).reshape(-1, 128).T
    for l in range(DEPTH):
        put(l, "b_mod", cm(inputs["b_mod"][l]))
        put(l, "npre", cm(inputs["norm_pre"][l].reshape(-1)))
        put(l, "npost", cm(inputs["norm_post"][l].reshape(-1)))
        def rg(v):
            v = f(v)
            o = np.zeros((128, 8), np.float32)
            for c in range(4):
                o[:, c] = v[128 * c:128 * c + 128]
            for g in range(4):
                o[0:64, 4 + g] = v[512 + 64 * g:512 + 64 * g + 64]
            return o
        put(l, "ssd_cw", np.concatenate([rg(inputs["ssd_conv_w"][l, j]) for j in range(4)], axis=1))
        put(l, "ssd_cb", rg(inputs["ssd_conv_b"][l]))
        put(l, "ssd_d", cm(np.repeat(f(inputs["ssd_d"][l]), 64)))
        put(l, "ssd_nw", cm(inputs["ssd_norm_w"][l]))
        col = np.zeros((128, 1), np.float32)
        col[0:16, 0] = f(inputs["ssd_dt_bias"][l]).reshape(-1)
        put(l, "ssd_dtb", col)
        col = np.zeros((128, 1), np.float32)
        col[0:16, 0] = f(inputs["ssd_a_log"][l]).reshape(-1)
        put(l, "ssd_alog", col)
        put(l, "lru_cw", np.concatenate([cm(inputs["lru_conv_w"][l, j]) for j in range(4)], axis=1))
        put(l, "lru_cb", cm(inputs["lru_conv_b"][l]))
        put(l, "lru_ba", np.concatenate([cm(inputs["lru_ba"][l, d].reshape(-1)) for d in range(2)], axis=1))
        put(l, "lru_bx", np.concatenate([cm(inputs["lru_bx"][l, d].reshape(-1)) for d in range(2)], axis=1))
        put(l, "lru_lam", np.concatenate([cm(inputs["lru_lambda"][l, d]) for d in range(2)], axis=1))
        put(l, "hg_lb", np.concatenate([cm(inputs["hgrn_lb_logits"][d, ll]) for d in range(2) for ll in range(2)], axis=1))
        put(l, "hg_nw", cm(inputs["hgrn_norm_w"][l]))
    lru_w = np.stack([f(inputs["lru_wa"]), f(inputs["lru_wx"])], axis=1)
    ctab = np.zeros((128, CT_N), np.float32)
    tpos = np.arange(NLAT)
    rows_, cols_ = tpos // 64, tpos % 64
    inv = 10000.0 ** (-np.arange(16, dtype=np.float64) / 16)
    rope = np.zeros((2, 64, NLAT), np.float32)
    ropeR = np.zeros((64, 64), np.float32)
    for j in range(64):
        pos = rows_ if j < 32 else cols_
        i = j % 32
        ang = (pos.astype(np.float32)[:, None] * inv.astype(np.float32)[None, :])[:, i % 16].astype(np.float32)
        rope[0, j] = np.cos(ang)
        rope[1, j] = np.sin(ang)
        if i < 16:
            ropeR[j + 16, j] = -1.0
        else:
            ropeR[j - 16, j] = 1.0
    rtab = np.zeros((4, 128, 5, 128), np.float32)
    ar = np.arange(128, dtype=np.float64)
    for h in range(4):
        lg = np.log1p(-(2.0 ** (-5.0 - h)))
        dist = np.abs(ar[:, None] - ar[None, :])
        rtab[h, :, 0, :] = np.exp(lg * dist) * (1.0 + np.eye(128))
        rtab[h, :, 1, :] = np.exp(lg * (ar + 1))[None, :]
        rtab[h, :, 2, :] = np.exp(lg * (128 - ar))[None, :]
        rtab[h, :, 3, :] = np.exp(lg * (127 - ar))[None, :]
        rtab[h, :, 4, :] = np.exp(lg * ar)[None, :]
    shared = {
        "w_mod": f(inputs["w_mod"]), "ffn_w1": f(inputs["ffn_w1"]), "ffn_w3": f(inputs["ffn_w3"]), "ffn_w2": f(inputs["ffn_w2"]),
        "w_in": f(inputs["w_in"]), "w_out": f(inputs["w_out"]), "pv": pv, "lru_w": np.ascontiguousarray(lru_w), "ctab": ctab,
        "rope": rope, "ropeR": ropeR, "rtab": rtab,
    }
    per_core = []
    x = f(inputs["x"]); ctx = f(inputs["ctx"]); c = f(inputs["c"]); c_ctx = f(inputs["c_ctx"])
    for b in range(x.shape[0]):
        xin = np.ascontiguousarray(np.concatenate([ctx[b], x[b]], axis=0))
        cT = np.ascontiguousarray(np.stack([c[b].reshape(KC, 128).T, c_ctx.reshape(KC, 128).T], axis=2))
        m = dict(shared)
        m["xin"] = xin
        m["cT"] = cT
        per_core.append(m)
    return per_core


_CACHE = {}


def kernel(**inputs):
    maps = host_pack(inputs)
    if "nc" not in _CACHE:
        _CACHE["nc"] = Builder().build()
    nc = _CACHE["nc"]
    res = run_bass_kernel_spmd(nc, maps, core_ids=list(range(len(maps))))
    out = np.stack([np.asarray(r["out"]) for r in res.results], axis=0)
    return out.astype(np.float32)
```
